# Optimizing a Trainium2 kernel written in Bass

```python
import jax, jax.numpy as jnp
from jax import lax
import numpy as np

D_MODEL = 2048
BATCH = 4
SEQ = 4096
DEPTH = 1

PLE_DIM = 256
D_MIX = D_MODEL
D_GLA = D_MIX // 2
D_SB = D_MIX - D_GLA
GLA_HEADS = 4
GLA_DK = (D_GLA // 2) // GLA_HEADS
GLA_DV = D_GLA // GLA_HEADS
GLA_GATE_RANK = 16
GLA_TAU = 16.0
GLA_CHUNK = 64
SB_HEADS = 8
SB_DH = D_SB // SB_HEADS
SB_BLOCK = 128
EPS = 1e-6

GLA_QK = GLA_HEADS * GLA_DK
SPLITS = [GLA_QK, GLA_QK, D_GLA, D_GLA, GLA_GATE_RANK, D_SB, D_SB, D_SB, D_SB]
D_IN = sum(SPLITS)

kernel_name = "hybrid_gla_stickbreaking_parallel_heads"


def rmsnorm(x, g):
    xf = x.astype(jnp.float32)
    return xf * lax.rsqrt(jnp.mean(xf * xf, axis=-1, keepdims=True) + EPS) * g.astype(jnp.float32)


def gla_chunked(q, k, v, log_a):
    B, S, H, dk = q.shape
    dv = v.shape[-1]
    C = GLA_CHUNK
    n = S // C

    def to_chunks(t):
        return t.astype(jnp.float32).reshape(B, n, C, H, t.shape[-1]).transpose(1, 0, 3, 2, 4)

    qc = to_chunks(q) * (dk ** -0.5)
    kc = to_chunks(k)
    vc = to_chunks(v)
    bc = jnp.cumsum(to_chunks(log_a), axis=3)
    causal = jnp.tril(jnp.ones((C, C), dtype=bool))[None, None, :, :, None]

    def step(state, inp):
        qi, ki, vi, bi = inp
        inter = jnp.einsum('bhcd,bhde->bhce', qi * jnp.exp(bi), state)
        diff = bi[:, :, :, None, :] - bi[:, :, None, :, :]
        decay = jnp.exp(jnp.where(causal, diff, -jnp.inf))
        scores = jnp.einsum('bhid,bhjd,bhijd->bhij', qi, ki, decay)
        intra = jnp.einsum('bhij,bhje->bhie', scores, vi)
        b_last = bi[:, :, -1:, :]
        new_state = state * jnp.exp(b_last)[:, :, 0, :, None] + jnp.einsum(
            'bhcd,bhce->bhde', ki * jnp.exp(b_last - bi), vi)
        return new_state, inter + intra

    s0 = jnp.zeros((B, H, dk, dv), jnp.float32)
    _, out = lax.scan(step, s0, (qc, kc, vc, bc))
    return out.transpose(1, 0, 3, 2, 4).reshape(B, S, H, dv)


def stick_breaking(q, k, v):
    S = q.shape[2]
    scale = q.shape[-1] ** -0.5
    qf, kf, vf = (t.astype(jnp.float32) for t in (q, k, v))
    outs = []
    for blk in range(S // SB_BLOCK):
        t0 = blk * SB_BLOCK
        t1 = t0 + SB_BLOCK
        z = jnp.einsum('bhtd,bhsd->bhts', qf[:, :, t0:t1], kf[:, :, :t1]) * scale
        t_idx = jnp.arange(t0, t1)[:, None]
        s_idx = jnp.arange(t1)[None, :]
        before = s_idx < t_idx
        log_keep = jnp.where(before, jax.nn.log_sigmoid(-z), 0.0)
        suffix = lax.cumsum(log_keep, axis=3, reverse=True) - log_keep
        w = jnp.where(before, jnp.exp(jax.nn.log_sigmoid(z) + suffix), 0.0)
        outs.append(jnp.einsum('bhts,bhsd->bhtd', w, vf[:, :, :t1]))
    return jnp.concatenate(outs, axis=2)


def setup_inputs(seed: int = 0) -> dict:
    key = jax.random.key(seed)
    ks = jax.random.split(key, 12)
    f32 = jnp.float32
    x = jax.random.normal(ks[0], (BATCH, SEQ, D_MODEL), f32)
    p = jax.random.normal(ks[1], (DEPTH, BATCH, SEQ, PLE_DIM), f32)
    g_pre = 1.0 + 0.02 * jax.random.normal(ks[2], (DEPTH, D_MODEL), f32)
    w_in = jax.random.normal(ks[3], (DEPTH, D_MODEL, D_IN), f32) * D_MODEL ** -0.5
    w_a2 = jax.random.normal(ks[4], (DEPTH, GLA_GATE_RANK, GLA_QK), f32) * GLA_GATE_RANK ** -0.5
    b_a = 0.1 * jax.random.normal(ks[5], (DEPTH, GLA_QK), f32)
    g_gla_head = 1.0 + 0.02 * jax.random.normal(ks[6], (DEPTH, GLA_DV), f32)
    w_out = jax.random.normal(ks[7], (DEPTH, D_MIX, D_MODEL), f32) * D_MIX ** -0.5
    g_post = 1.0 + 0.02 * jax.random.normal(ks[8], (DEPTH, D_MODEL), f32)
    w_ple_gate = jax.random.normal(ks[9], (DEPTH, D_MODEL, D_MODEL), f32) * D_MODEL ** -0.5
    b_ple_gate = 0.02 * jax.random.normal(ks[10], (DEPTH, D_MODEL), f32)
    w_ple_proj = jax.random.normal(ks[11], (DEPTH, PLE_DIM, D_MODEL), f32) * PLE_DIM ** -0.5
    return {"x": x, "p": p, "g_pre": g_pre, "w_in": w_in, "w_a2": w_a2, "b_a": b_a,
            "g_gla_head": g_gla_head, "w_out": w_out, "g_post": g_post,
            "w_ple_gate": w_ple_gate, "b_ple_gate": b_ple_gate, "w_ple_proj": w_ple_proj}


def reference(x, p, g_pre, w_in, w_a2, b_a, g_gla_head, w_out, g_post,
              w_ple_gate, b_ple_gate, w_ple_proj):
    B, S, _ = x.shape
    out_dtype = x.dtype
    h_res = x.astype(jnp.float32)
    offsets = np.cumsum(SPLITS)[:-1].tolist()
    for i in range(DEPTH):
        h = rmsnorm(h_res, g_pre[i])
        u = h @ w_in[i].astype(jnp.float32)
        (gq, gk, gv, g_gate, g_lr, sq, sk, sv, s_gate) = jnp.split(u, offsets, axis=-1)

        log_a = jax.nn.log_sigmoid(g_lr @ w_a2[i].astype(jnp.float32) + b_a[i]) / GLA_TAU
        o_gla = gla_chunked(gq.reshape(B, S, GLA_HEADS, GLA_DK),
                            gk.reshape(B, S, GLA_HEADS, GLA_DK),
                            gv.reshape(B, S, GLA_HEADS, GLA_DV),
                            log_a.reshape(B, S, GLA_HEADS, GLA_DK))
        o_gla = rmsnorm(o_gla, g_gla_head[i]).reshape(B, S, D_GLA) * jax.nn.silu(g_gate)

        def heads(t):
            return t.reshape(B, S, SB_HEADS, SB_DH).transpose(0, 2, 1, 3)
        o_sb = stick_breaking(heads(sq), heads(sk), heads(sv))
        o_sb = o_sb.transpose(0, 2, 1, 3).reshape(B, S, D_SB) * jax.nn.silu(s_gate)

        mix = jnp.concatenate([o_gla, o_sb], axis=-1) @ w_out[i].astype(jnp.float32)
        h_res = h_res + rmsnorm(mix, g_post[i])

        gate = jax.nn.sigmoid(h_res @ w_ple_gate[i].astype(jnp.float32) + b_ple_gate[i])
        h_res = h_res + gate * (p[i].astype(jnp.float32) @ w_ple_proj[i].astype(jnp.float32))
    return h_res.astype(out_dtype)
```

```python
from contextlib import ExitStack
import numpy as np
import concourse.bass as bass
import concourse.mybir as mybir
from concourse.bass_utils import run_bass_kernel_spmd

F32 = mybir.dt.float32
BF16 = mybir.dt.bfloat16
AF = mybir.ActivationFunctionType
ALU = mybir.AluOpType

D = 2048
NCH = 16
EPS = 1e-6
TQ = 1024
NBK = TQ // 512


class T:
    __slots__ = ("w", "r", "x")

    def __init__(self, x=False):
        self.w = None
        self.r = []
        self.x = x


class Sched:
    ENG = ("pe", "act", "dve", "pool", "sp")

    def __init__(self, nc, n_dma_sems=28):
        self.nc = nc
        self.e = dict(pe=nc.tensor, act=nc.scalar, dve=nc.vector, pool=nc.gpsimd, sp=nc.sync)
        self.sems = {}
        self.cnt = {}
        for k in self.ENG:
            self.sems[k] = nc.alloc_semaphore("s_" + k)
            self.cnt[k] = 0
        self.dma_keys = []
        for i in range(n_dma_sems):
            k = "d%d" % i
            self.sems[k] = nc.alloc_semaphore("s_" + k)
            self.cnt[k] = 0
            self.dma_keys.append(k)
        self.sems["cc"] = nc.alloc_semaphore("s_cc")
        self.cnt["cc"] = 0
        self.dma_rr = 0
        self.seen = {k: {} for k in self.ENG}

    def _deps(self, eng, reads, writes):
        need = {}

        def add(d, same_ok):
            if d is None:
                return
            k, v = d
            if k == eng and same_ok:
                return
            if need.get(k, 0) < v:
                need[k] = v
        for t in reads:
            add(t.w, False)
            if t.x:
                for d in t.r:
                    add(d, True)
        for t in writes:
            add(t.w, True)
            for d in t.r:
                add(d, False)
        return need

    def _wait(self, eng, need):
        seen = self.seen[eng]
        for k, v in need.items():
            if seen.get(k, 0) < v:
                self.e[eng].wait_ge(self.sems[k], v)
                seen[k] = v

    def _record(self, d, reads, writes):
        for t in reads:
            t.r.append(d)
        for t in writes:
            t.w = d
            t.r = []

    def op(self, eng, fn, reads=(), writes=(), inc=True):
        self._wait(eng, self._deps(eng, reads, writes))
        ins = fn()
        if inc:
            self.cnt[eng] += 1
            ins.then_inc(self.sems[eng], 1)
            seq = self.cnt[eng]
        else:
            seq = self.cnt[eng] + 1
        self._record((eng, seq), reads, writes)
        return ins

    def dma(self, eng, out, in_, reads=(), writes=()):
        self._wait(eng, self._deps(eng, reads, writes))
        k = self.dma_keys[self.dma_rr]
        self.dma_rr = (self.dma_rr + 1) % len(self.dma_keys)
        ins = self.e[eng].dma_start(out=out, in_=in_)
        self.cnt[k] += 16
        ins.then_inc(self.sems[k], 16)
        self._record((k, self.cnt[k]), reads, writes)
        return ins

    def barrier(self, skip=()):
        for eng in self.ENG:
            need = {k: v for k, v in self.cnt.items() if v > 0 and k not in skip}
            self._wait(eng, need)


def build_program(S_LEN, mode="AB"):
    STOP = 9
    P3 = 9
    HS = S_LEN // 2
    nc = bass.Bass("TRN2", target_bir_lowering=False)
    S = Sched(nc)
    pe, act, dve, pool = nc.tensor, nc.scalar, nc.vector, nc.gpsimd

    def mm(out, **kw):
        return pe.matmul(out, skip_group_check=True, **kw)

    def din(name, shape, dt=F32):
        return nc.dram_tensor(name, shape, dt, kind="ExternalInput").ap()

    xT = din("xT", [D, S_LEN])
    xTo = din("xTo", [D, HS])
    pTo = din("pTo", [256, HS])
    wsb = din("wsb", [4, D, 512])
    wgla = din("wgla", [2, D, 768])
    wlr = din("wlr", [D, 16])
    wa2 = din("wa2", [16, 256])
    ba = din("ba", [1, 256])
    cols = din("cols", [128, 64])
    wo = din("wo", [D, D])
    wgt = din("wgt", [D, D])
    wp = din("wp", [256, D])
    cst = din("cst", [128, 5, 128])
    outT = nc.dram_tensor("outT", [D, HS], F32, kind="ExternalOutput").ap()
    if mode == "A":
        mix_own = nc.dram_tensor("mix_own", [4, 1024, S_LEN // 4], BF16, kind="ExternalOutput").ap()
    else:
        mix_own = nc.dram_tensor("mix_own", [4, 1024, S_LEN // 4], BF16).ap()
    mix_all = nc.dram_tensor("mix_all", [4, 2048, S_LEN // 4], BF16).ap()
    mg_own = nc.dram_tensor("mg_own", [4, 512, S_LEN // 4], BF16).ap()
    mg_all = nc.dram_tensor("mg_all", [4, 1024, S_LEN // 4], BF16).ap()
    ms_own = nc.dram_tensor("ms_own", [4, 2, 128, HS], BF16).ap()
    ms_all = nc.dram_tensor("ms_all", [4, 512, HS], BF16).ap()
    GROUPS = [[0, 1], [2, 3], [4, 5], [6, 7]]
    t_mix = T()

    def issue_cc(src, dst):
        pool.collective_compute("AllGather", ALU.bypass, replica_groups=GROUPS, ins=[src], outs=[dst]
                                ).then_inc(S.sems["cc"], 1)
        S.cnt["cc"] += 1
        t_mix.w = ("cc", S.cnt["cc"])
    EB = S_LEN // 4
    if mode == "B":
        mixin = din("mixin", [2048, HS], BF16)
    SKIP = (mode == "B")

    cst_f = nc.alloc_sbuf_tensor("cst_f", [128, 5, 128], F32)
    cst_b = nc.alloc_sbuf_tensor("cst_b", [128, 5, 128], BF16)
    cols_f = nc.alloc_sbuf_tensor("cols_f", [128, 64], F32)
    wa2_f = nc.alloc_sbuf_tensor("wa2_f", [16, 256], F32)
    ba_f = nc.alloc_sbuf_tensor("ba_f", [1, 256], F32)
    t_cst = T()
    S.dma("sp", cst_f[:], cst[:, :, :], writes=[t_cst])
    S.dma("sp", cols_f[:], cols[:, :], writes=[t_cst])
    S.dma("sp", wa2_f[:], wa2[:, :], writes=[t_cst])
    S.dma("sp", ba_f[:], ba[:, :], writes=[t_cst])
    S.op("dve", lambda: dve.tensor_copy(cst_b[:], cst_f[:]), reads=[t_cst], writes=[t_cst])
    ident_b = cst_b[:, 0, :]
    Uincl_f = cst_f[:, 1, :]
    Ustr_f = cst_f[:, 2, :]
    Ustr_b = cst_b[:, 2, :]
    Lincl_b = cst_b[:, 3, :]
    ones_b = cst_b[:, 4, :]
    ones_f = cst_f[:, 4, :]

    PS = nc.alloc_psum_tensor("ps", [128, 4096], F32)
    t_ps = [T(True) for _ in range(8)]

    def bank(i):
        return PS[:, i * 512:(i + 1) * 512]

    with nc.sbuf_tensor("hT", [128, NCH, S_LEN], BF16) as hT:
        NB0 = S_LEN // 256
        NLOOP0 = 0 if SKIP else NB0
        t_hT = [T() for _ in range(NB0)]
        t_hTp = [T() for _ in range(NB0)]

        def h_reads(t0, t1):
            rng = range(t0 // 256, (t1 + 255) // 256)
            return [t_hT[i] for i in rng] + [t_hTp[i] for i in rng]

        with ExitStack() as _es0:
            xb0 = _es0.enter_context(nc.sbuf_tensor("xb0", [128, NCH, 256], F32))
            xb1 = _es0.enter_context(nc.sbuf_tensor("xb1", [128, NCH, 256], F32))
            sq0 = _es0.enter_context(nc.sbuf_tensor("sq0", [128, NCH, 256], BF16))
            r0a = _es0.enter_context(nc.sbuf_tensor("r0", [128, 256], F32))
            r0b = _es0.enter_context(nc.sbuf_tensor("r1", [128, 256], F32))
            p0tmp = (_es0.enter_context(nc.sbuf_tensor("p0ta", [128, 256], F32)), _es0.enter_context(nc.sbuf_tensor("p0tb", [128, 256], F32)))
            t_p0tmp = (T(), T())
            xbs = (xb0, xb1)
            t_xb = (T(), T())
            t_sq = T()
            rs = (r0a, r0b)
            t_r = (T(), T())
            xT_v = xT.rearrange("(c p) t -> p c t", p=128)
            for nb in range(NLOOP0):
                xb = xbs[nb % 2]
                txb = t_xb[nb % 2]
                r = rs[nb % 2]
                tr = t_r[nb % 2]
                tsl = slice(nb * 256, (nb + 1) * 256)
                S.dma("sp", xb[:], xT_v[:, :, tsl], writes=[txb])
                S.op("act", lambda xb=xb: act.activation(out=sq0[:], in_=xb[:], func=AF.Square),
                     reads=[txb], writes=[t_sq])
                bk = nb % 2
                for c in range(NCH):
                    S.op("pe", lambda c=c, bk=bk: mm(bank(bk)[:, 0:256], lhsT=ones_b, rhs=sq0[:, c, :],
                                                           start=(c == 0), stop=(c == NCH - 1)),
                         reads=[t_sq, t_cst], writes=[t_ps[bk]], inc=(c == NCH - 1))
                S.op("act", lambda r=r, bk=bk: act.activation(out=r[:], in_=bank(bk)[:, 0:256], func=AF.Ln,
                                                               scale=1.0 / D, bias=EPS),
                     reads=[t_ps[bk]], writes=[tr])
                S.op("act", lambda r=r: act.activation(out=r[:], in_=r[:], func=AF.Exp, scale=-0.5),
                     reads=[tr], writes=[tr])
                for c in range(NCH):
                    if c % 3 == 2:
                        k = (c // 3) % 2
                        S.op("act", lambda c=c, xb=xb, k=k: act.activation(
                            out=p0tmp[k][:], in_=xb[:, c, :], func=AF.Identity, scale=cols_f[:, c:c + 1]),
                            reads=[txb, t_cst], writes=[t_p0tmp[k]])
                        S.op("pool", lambda c=c, r=r, k=k: pool.tensor_tensor(
                            out=hT[:, c, tsl], in0=p0tmp[k][:], in1=r[:], op=ALU.mult),
                            reads=[t_p0tmp[k], tr], writes=[t_hTp[nb]])
                        continue
                    S.op("dve", lambda c=c, xb=xb, r=r: dve.scalar_tensor_tensor(
                        out=hT[:, c, tsl], in0=xb[:, c, :], scalar=cols_f[:, c:c + 1], in1=r[:],
                        op0=ALU.mult, op1=ALU.mult),
                        reads=[txb, tr, t_cst], writes=[t_hT[nb]])
        S.barrier()
        if STOP <= 0:
            return nc

        def silu_evac(src_ap, dst_ap, tmp_a, tmp_b, t_tmp, reads, writes):
            S.op("act", lambda: act.activation(out=tmp_a, in_=src_ap, func=AF.Exp, scale=-1.0),
                 reads=reads, writes=[t_tmp])
            S.op("dve", lambda: dve.tensor_scalar_add(out=tmp_a, in0=tmp_a, scalar1=1.0),
                 reads=[t_tmp], writes=[t_tmp])
            S.op("dve", lambda: dve.reciprocal(out=tmp_b, in_=tmp_a), reads=[t_tmp], writes=[t_tmp])
            S.op("dve", lambda: dve.tensor_tensor(out=dst_ap, in0=src_ap, in1=tmp_b, op=ALU.mult),
                 reads=list(reads) + [t_tmp], writes=writes)

        with ExitStack() as _es1:
            def A1(name, shape, dt):
                return _es1.enter_context(nc.sbuf_tensor(name, shape, dt))
            wg = A1("wg", [128, NCH, 768], BF16)
            wlr_b = A1("wlr_b", [128, NCH, 16], BF16)
            g_qTs = (A1("g_qTa", [128, 512], BF16), A1("g_qTb", [128, 512], BF16))
            g_kTs = (A1("g_kTa", [128, 512], BF16), A1("g_kTb", [128, 512], BF16))
            g_silus = (A1("g_silua", [128, 2, 512], BF16), A1("g_silub", [128, 2, 512], BF16))
            g_vs = (A1("g_va", [128, 4, 256], BF16), A1("g_vb", [128, 4, 256], BF16))
            g_lr1 = A1("g_lra", [16, 512], F32)
            g_lrs = (g_lr1, g_lr1)
            g_tmpa = A1("g_tmpa", [128, 512], F32)
            g_tmpb = A1("g_tmpb", [128, 512], F32)
            g_mask4 = A1("g_mask4", [128, 4, 128], F32)
            g_e = A1("g_e", [128, 512], F32)
            g_tmpc = g_e
            g_sp = A1("g_sp", [128, 4, 128], F32)
            g_Eq = A1("g_Eq", [128, 512], F32)
            g_Ek = A1("g_Ek", [128, 512], F32)
            g_qe = A1("g_qe", [128, 512], BF16)
            g_ke = A1("g_ke", [128, 512], BF16)
            g_klT = A1("g_klT", [128, 512], BF16)
            g_kl = A1("g_kl", [128, 4, 128], BF16)
            g_scm = A1("g_scm", [128, 512], BF16)
            g_Sf = A1("g_Sf", [128, 256], F32)
            g_Sb = A1("g_Sb", [128, 4, 256], BF16)
            g_sq = A1("g_sq", [128, 2, 512], BF16)
            g_r = A1("g_r", [128, 512], F32)
            g_y = A1("g_y", [128, 2, 512], BF16)
            t_wg, t_wlr, t_mask4 = T(), T(), T()
            t_qT, t_kT, t_silu, t_v = ((T(), T()) for _ in range(4))
            t_lr = (T(),) * 2
            t_tmp = T()
            t_e, t_sp, t_Eq, t_Ek, t_qe, t_ke, t_klT, t_kl, t_scm = (T() for _ in range(9))
            t_Sf, t_sq2, t_r2, t_y = T(), T(), T(), T()
            t_tmpc = t_e
            t_Sb = [T() for _ in range(4)]
            S.dma("pool", wlr_b[:], wlr.rearrange("(c p) n -> p c n", p=128), writes=[t_wlr])
            for cc in range(4):
                S.op("pool", lambda cc=cc: pool.tensor_copy(g_mask4[:, cc, :], Uincl_f), reads=[t_cst], writes=[t_mask4])
            gblocks = [] if SKIP else [(g, nb) for g in range(2) for nb in range(S_LEN // 512)]
            rot = [0]

            def nextbank():
                rot[0] ^= 1
                return rot[0]

            def gla_inproj_gen(bi):
                g, nb = gblocks[bi]
                par = bi % 2
                t0 = nb * 512
                tok = slice(t0, t0 + 512)
                hr = h_reads(t0, t0 + 512)
                if nb == 0:
                    S.dma("pool", wg[:], wgla[g].rearrange("(c p) n -> p c n", p=128), writes=[t_wg])
                pending = [None]

                def flush():
                    if pending[0] is not None:
                        pending[0]()
                        pending[0] = None

                def group(lhs_fn, rhs_fn, out_fn, evac, extra_reads):
                    bk = nextbank()
                    for c in range(NCH):
                        S.op("pe", lambda c=c, bk=bk: mm(out_fn(bk), lhsT=lhs_fn(c), rhs=rhs_fn(c),
                                                               start=(c == 0), stop=(c == NCH - 1)),
                             reads=hr + extra_reads, writes=[t_ps[bk]], inc=(c == NCH - 1))
                        if c == 3:
                            flush()
                    pending[0] = lambda bk=bk: evac(bk)

                group(lambda c: wg[:, c, 0:128], lambda c: hT[:, c, tok], lambda bk: bank(bk),
                      lambda bk: S.op("act", lambda: act.activation(out=g_qTs[par][:], in_=bank(bk), func=AF.Identity,
                                                                    scale=128 ** -0.5),
                                      reads=[t_ps[bk]], writes=[t_qT[par]]), [t_wg])
                yield
                group(lambda c: wg[:, c, 128:256], lambda c: hT[:, c, tok], lambda bk: bank(bk),
                      lambda bk: S.op("dve", lambda: dve.tensor_copy(g_kTs[par][:], bank(bk)),
                                      reads=[t_ps[bk]], writes=[t_kT[par]]), [t_wg])
                yield
                for ec in range(2):
                    group(lambda c, ec=ec: wg[:, c, 512 + ec * 128:512 + ec * 128 + 128], lambda c: hT[:, c, tok],
                          lambda bk: bank(bk),
                          lambda bk, ec=ec: silu_evac(bank(bk), g_silus[par][:, ec, :], g_tmpa[:], g_tmpb[:], t_tmp,
                                                      [t_ps[bk]], [t_silu[par]]), [t_wg])
                    yield
                for sb in range(4):
                    group(lambda c, sb=sb: hT[:, c, t0 + sb * 128:t0 + sb * 128 + 128], lambda c: wg[:, c, 256:512],
                          lambda bk: bank(bk)[:, 0:256],
                          lambda bk, sb=sb: S.op("dve", lambda: dve.tensor_copy(g_vs[par][:, sb, :], bank(bk)[:, 0:256]),
                                                 reads=[t_ps[bk]], writes=[t_v[par]]), [t_wg])
                    yield
                group(lambda c: wlr_b[:, c, :], lambda c: hT[:, c, tok], lambda bk: bank(bk)[0:16, :],
                      lambda bk: S.op("dve", lambda: dve.tensor_copy(g_lrs[par][:], bank(bk)[0:16, :]),
                                      reads=[t_ps[bk]], writes=[t_lr[par]]), [t_wlr])
                flush()
                yield

            def gpump(gen, n):
                if gen is None:
                    return
                for _ in range(n):
                    try:
                        next(gen)
                    except StopIteration:
                        return

            if gblocks:
                gpump(gla_inproj_gen(0), 100)
            for bi, (g, nb) in enumerate(gblocks):
                par = bi % 2
                g_qT, g_kT, g_silu, g_v, g_lr = g_qTs[par], g_kTs[par], g_silus[par], g_vs[par], g_lrs[par]
                tqT, tkT, tsilu, tv, tlr = t_qT[par], t_kT[par], t_silu[par], t_v[par], t_lr[par]
                nxt = gla_inproj_gen(bi + 1) if bi + 1 < len(gblocks) else None
                t0 = nb * 512
                gsl = slice(g * 128, g * 128 + 128)
                if nb == 0:
                    S.op("dve", lambda: dve.memset(g_Sf[:], 0.0), writes=[t_Sf])
                    S.op("dve", lambda: dve.memset(g_Sb[:, 0, :], 0.0), writes=[t_Sb[0]])
                for cc in range(4):
                    cs = slice(cc * 128, cc * 128 + 128)
                    S.op("pe", lambda cs=cs, cc=cc: mm(
                        bank(2)[:, cs], lhsT=g_lr[0:16, cs], rhs=wa2_f[0:16, gsl], start=(cc == 0), stop=False),
                        reads=[tlr, t_cst], writes=[t_ps[2]], inc=False)
                    S.op("pe", lambda cs=cs, cc=cc: mm(
                        bank(2)[:, cs], lhsT=ones_f[0:1, :], rhs=ba_f[0:1, gsl], start=False, stop=True),
                        reads=[t_cst], writes=[t_ps[2]], inc=(cc == 3))
                gpump(nxt, 1)
                S.op("act", lambda: act.activation(out=g_e[:], in_=bank(2), func=AF.Exp, scale=-1.0),
                     reads=[t_ps[2]], writes=[t_e])
                S.op("act", lambda: act.activation(out=g_sp[:].rearrange("p a b -> p (a b)"), in_=g_e[:], func=AF.Ln, bias=1.0),
                     reads=[t_e], writes=[t_sp])
                for cc in range(4):
                    cs = slice(cc * 128, cc * 128 + 128)
                    S.op("pe", lambda cs=cs, cc=cc: mm(bank(3)[:, cs], lhsT=g_sp[:, cc, :], rhs=Uincl_f,
                                                              start=(cc == 0), stop=True),
                         reads=[t_sp, t_cst], writes=[t_ps[3]], inc=(cc == 3))
                gpump(nxt, 1)
                S.op("act", lambda: act.activation(out=g_Eq[:], in_=bank(3), func=AF.Exp, scale=-1.0 / 16),
                     reads=[t_ps[3]], writes=[t_Eq])
                S.op("act", lambda: act.activation(out=g_Ek[:], in_=bank(3), func=AF.Exp, scale=1.0 / 16),
                     reads=[t_ps[3]], writes=[t_Ek])
                S.op("dve", lambda: dve.tensor_tensor(out=g_ke[:], in0=g_kT[:], in1=g_Ek[:], op=ALU.mult),
                     reads=[tkT, t_Ek], writes=[t_ke])
                for cc in range(4):
                    cs = slice(cc * 128, cc * 128 + 128)
                    S.op("dve", lambda cs=cs, cc=cc: dve.scalar_tensor_tensor(
                        out=g_klT[:, cs], in0=g_kT[:, cs], scalar=g_Eq[:, cc * 128 + 127:cc * 128 + 128], in1=g_Ek[:, cs],
                        op0=ALU.mult, op1=ALU.mult),
                        reads=[tkT, t_Eq, t_Ek], writes=[t_klT])
                S.op("dve", lambda: dve.tensor_tensor(out=g_qe[:], in0=g_qT[:], in1=g_Eq[:], op=ALU.mult),
                     reads=[tqT, t_Eq], writes=[t_qe])
                for cc in range(4):
                    cs = slice(cc * 128, cc * 128 + 128)
                    S.op("pe", lambda cs=cs, cc=cc: mm(bank(4)[:, cs], lhsT=g_klT[:, cs], rhs=ident_b,
                                                              start=(cc == 0), stop=True),
                         reads=[t_klT, t_cst], writes=[t_ps[4]], inc=(cc == 3))
                for cc in range(4):
                    cs = slice(cc * 128, cc * 128 + 128)
                    S.op("pe", lambda cs=cs, cc=cc: mm(bank(5)[:, cs], lhsT=g_ke[:, cs], rhs=g_qe[:, cs],
                                                              start=(cc == 0), stop=True),
                         reads=[t_ke, t_qe], writes=[t_ps[5]], inc=(cc == 3))
                gpump(nxt, 1)
                S.op("act", lambda: act.activation(out=g_kl[:].rearrange("p a b -> p (a b)"), in_=bank(4), func=AF.Copy),
                     reads=[t_ps[4]], writes=[t_kl])
                S.op("dve", lambda: dve.tensor_tensor(out=g_scm[:], in0=bank(5), in1=g_mask4[:].rearrange("p a b -> p (a b)"),
                                                      op=ALU.mult),
                     reads=[t_ps[5], t_mask4], writes=[t_scm])
                for cc in range(4):
                    S.op("pe", lambda cc=cc: mm(
                        bank(2 + cc // 2)[:, (cc % 2) * 256:(cc % 2) * 256 + 256], lhsT=g_kl[:, cc, :], rhs=g_v[:, cc, :],
                        start=(cc % 2 == 0), stop=True),
                        reads=[t_kl, tv], writes=[t_ps[2 + cc // 2]])
                gpump(nxt, 1)
                for cc in range(4):
                    cs = slice(cc * 128, cc * 128 + 128)
                    for ec in range(2):
                        es = slice(ec * 128, ec * 128 + 128)
                        S.op("pe", lambda ec=ec, es=es, cs=cs, cc=cc: mm(
                            bank(6 + ec)[:, cs], lhsT=g_Sb[:, cc, es], rhs=g_qe[:, cs], start=(cc == 0), stop=False),
                            reads=[t_Sb[cc], t_qe], writes=[t_ps[6 + ec]], inc=False)
                        S.op("pe", lambda ec=ec, es=es, cs=cs, cc=cc: mm(
                            bank(6 + ec)[:, cs], lhsT=g_v[:, cc, es], rhs=g_scm[:, cs], start=False, stop=True),
                            reads=[tv, t_scm], writes=[t_ps[6 + ec]])
                    S.op("dve", lambda cc=cc: dve.scalar_tensor_tensor(
                        out=g_Sf[:], in0=g_Sf[:], scalar=g_Eq[:, cc * 128 + 127:cc * 128 + 128],
                        in1=bank(2 + cc // 2)[:, (cc % 2) * 256:(cc % 2) * 256 + 256], op0=ALU.mult, op1=ALU.add),
                        reads=[t_Sf, t_Eq, t_ps[2 + cc // 2]], writes=[t_Sf])
                    S.op("pool", lambda cc=cc: pool.tensor_copy(g_Sb[:, (cc + 1) % 4, :], g_Sf[:]),
                         reads=[t_Sf], writes=[t_Sb[(cc + 1) % 4]])
                    gpump(nxt, 1)
                gpump(nxt, 100)
                for ec in range(2):
                    S.op("act", lambda ec=ec: act.activation(out=g_sq[:, ec, :], in_=bank(6 + ec), func=AF.Square),
                         reads=[t_ps[6 + ec]], writes=[t_sq2])
                for ec in range(2):
                    S.op("pe", lambda ec=ec: mm(bank(5), lhsT=ones_b, rhs=g_sq[:, ec, :], start=(ec == 0), stop=(ec == 1)),
                         reads=[t_sq2, t_cst], writes=[t_ps[5]], inc=(ec == 1))
                S.op("act", lambda: act.activation(out=g_r[:], in_=bank(5), func=AF.Ln, scale=1.0 / 256, bias=EPS),
                     reads=[t_ps[5]], writes=[t_r2])
                S.op("act", lambda: act.activation(out=g_r[:], in_=g_r[:], func=AF.Exp, scale=-0.5),
                     reads=[t_r2], writes=[t_r2])
                for ec in range(2):
                    S.op("dve", lambda ec=ec: dve.scalar_tensor_tensor(
                        out=g_tmpc[:], in0=bank(6 + ec), scalar=cols_f[:, 48 + ec:49 + ec], in1=g_r[:],
                        op0=ALU.mult, op1=ALU.mult),
                        reads=[t_ps[6 + ec], t_r2, t_cst], writes=[t_tmpc])
                    S.op("dve", lambda ec=ec: dve.tensor_tensor(out=g_y[:, ec, :], in0=g_tmpc[:], in1=g_silu[:, ec, :], op=ALU.mult),
                         reads=[t_tmpc, tsilu], writes=[t_y])
                S.dma("sp", mg_own[t0 // EB, g * 256:(g + 1) * 256, t0 % EB:t0 % EB + 512].rearrange("(e p) t -> p e t", p=128), g_y[:],
                      reads=[t_y])
        S.barrier()

        NKB = S_LEN // 128
        with ExitStack() as _es2:
            def A2(name, shape, dt):
                return _es2.enter_context(nc.sbuf_tensor(name, shape, dt))
            ws = A2("ws", [128, NCH, 512], BF16)
            s_kT = A2("s_kT", [128, S_LEN], BF16)
            s_v = A2("s_v", [128, NKB, 128], BF16)
            s_qTs = (A2("s_qTa", [128, TQ], BF16), A2("s_qTb", [128, TQ], BF16))
            s_gss = (A2("s_gsa", [128, TQ], BF16), A2("s_gsb", [128, TQ], BF16))
            s_ta = A2("s_ta", [128, 512], F32)
            s_tb = A2("s_tb", [128, 512], F32)
            e1s = (A2("s_e1a", [128, TQ], F32), A2("s_e1b", [128, TQ], F32), A2("s_e1c", [128, TQ], F32))
            sps = (A2("s_spa", [128, TQ], BF16), A2("s_spb", [128, TQ], BF16))
            gs_ = (A2("s_ga", [128, TQ], BF16), A2("s_gb", [128, TQ], BF16))
            ws_ = (A2("s_wa", [128, TQ], BF16), A2("s_wb", [128, TQ], BF16))
            s_y = A2("s_y", [128, TQ], BF16)
            t_ws, t_sy, t_stmp, t_msown = T(), T(), T(), T()
            t_skT = [T() for _ in range(S_LEN // 512)]
            t_sv = [T() for _ in range(S_LEN // 512)]
            t_sqT, t_sgs = (T(), T()), (T(), T())
            t_e1, t_sps, t_gs_, t_ws_ = (T(), T(), T()), (T(), T()), (T(), T()), (T(), T())
            ZB, BB, OB = 0, 2, 4
            blocks = [] if SKIP else [(hd, tb) for hd in range(4) for tb in range(S_LEN // TQ)]

            def inproj_gen(bi):
                hd, tb = blocks[bi]
                s_qT, s_gs = s_qTs[bi % 2], s_gss[bi % 2]
                tq_, tg_ = t_sqT[bi % 2], t_sgs[bi % 2]
                q0 = tb * TQ
                if tb == 0:
                    S.dma("pool", ws[:], wsb[hd].rearrange("(c p) n -> p c n", p=128), writes=[t_ws])
                pending = [None]

                def flush():
                    if pending[0] is not None:
                        pending[0]()
                        pending[0] = None

                for half in range(NBK):
                    t0 = q0 + half * 512
                    tok = slice(t0, t0 + 512)
                    loc = slice(half * 512, half * 512 + 512)
                    hr = h_reads(t0, t0 + 512)
                    for (col0, bk_, evac) in (
                        (0, 6, lambda loc=loc: S.op("dve", lambda: dve.tensor_scalar_mul(
                            out=s_qT[:, loc], in0=bank(6), scalar1=128 ** -0.5), reads=[t_ps[6]], writes=[tq_])),
                        (128, 7, lambda tok=tok, t0=t0: S.op("dve", lambda: dve.tensor_copy(s_kT[:, tok], bank(7)),
                                                             reads=[t_ps[7]], writes=[t_skT[t0 // 512]])),
                        (384, 6, lambda loc=loc: silu_evac(bank(6), s_gs[:, loc], s_ta[:], s_tb[:], t_stmp,
                                                           [t_ps[6]], [tg_])),
                    ):
                        for c in range(NCH):
                            S.op("pe", lambda c=c, col0=col0, bk_=bk_: mm(
                                bank(bk_), lhsT=ws[:, c, col0:col0 + 128], rhs=hT[:, c, tok],
                                start=(c == 0), stop=(c == NCH - 1)),
                                reads=hr + [t_ws], writes=[t_ps[bk_]], inc=(c == NCH - 1))
                            if c % 4 == 3:
                                if c == 3:
                                    flush()
                                yield
                        pending[0] = evac
                    for sb in range(4):
                        kb = (t0 // 128) + sb
                        for c in range(NCH):
                            S.op("pe", lambda c=c, kb=kb, sb=sb: mm(
                                bank(7)[:, sb * 128:sb * 128 + 128], lhsT=hT[:, c, kb * 128:kb * 128 + 128],
                                rhs=ws[:, c, 256:384], start=(c == 0 and sb == 0), stop=(c == NCH - 1)),
                                reads=hr + [t_ws], writes=[t_ps[7]], inc=(c == NCH - 1 and sb == 3))
                            if c % 4 == 3:
                                if c == 3 and sb == 0:
                                    flush()
                                yield
                    pending[0] = (lambda t0=t0: S.op("dve", lambda: dve.tensor_copy(
                        s_v[:, t0 // 128:t0 // 128 + 4, :], bank(7).rearrange("p (a b) -> p a b", a=4)),
                        reads=[t_ps[7]], writes=[t_sv[t0 // 512]]))
                flush()
                yield

            NYIELD = NBK * 28 + 1

            def pump(gen, n):
                if gen is None:
                    return
                for _ in range(n):
                    try:
                        next(gen)
                    except StopIteration:
                        return

            if blocks:
                pump(inproj_gen(0), 10 ** 6)
                for k in range(4):
                    issue_cc(mg_own[k], mg_all[k])
            for bi, (hd, tb) in enumerate(blocks):
                    q0 = tb * TQ
                    s_qT, s_gs = s_qTs[bi % 2], s_gss[bi % 2]
                    tq_, tg_ = t_sqT[bi % 2], t_sgs[bi % 2]
                    nxt = inproj_gen(bi + 1) if bi + 1 < len(blocks) else None
                    kbs = list(range((tb + 1) * (TQ // 128) - 1, -1, -1))
                    P = len(kbs)
                    npump = (NYIELD + P - 1) // P
                    startedB = [False] * NBK
                    startedO = [False] * NBK

                    def geom(p):
                        kb = kbs[p]
                        off = kb * 128 - q0
                        lo = max(0, off)
                        segs = []
                        for bki in range(NBK):
                            c0 = max(lo, bki * 512)
                            c1 = (bki + 1) * 512
                            if c0 < c1:
                                segs.append((bki, c0, c1))
                        return kb, off, lo, segs

                    def emit_Z(p):
                        kb, off, lo, segs = geom(p)
                        for (bki, c0, c1) in segs:
                            S.op("pe", lambda bki=bki, c0=c0, c1=c1, kb=kb: mm(
                                bank(ZB + bki)[:, c0 - bki * 512:c1 - bki * 512],
                                lhsT=s_kT[:, kb * 128:kb * 128 + 128], rhs=s_qT[:, c0:c1], start=True, stop=True),
                                reads=[t_skT[kb // 4], tq_], writes=[t_ps[ZB + bki]])

                    def zb_reads(segs, base):
                        return [t_ps[base + bki] for (bki, _, _) in segs]

                    def emit_E1(p):
                        kb, off, lo, segs = geom(p)
                        e1, te1 = e1s[p % 3], t_e1[p % 3]
                        S.op("act", lambda: act.activation(
                            out=e1[:, lo:TQ], in_=PS[:, ZB * 512 + lo:ZB * 512 + TQ], func=AF.Exp),
                            reads=zb_reads(segs, ZB), writes=[te1])
                        if off >= 0:
                            S.op("dve", lambda: dve.tensor_tensor(
                                out=e1[:, off:off + 128], in0=e1[:, off:off + 128], in1=Ustr_f, op=ALU.mult),
                                reads=[te1, t_cst], writes=[te1])

                    emit_Z(0)
                    emit_E1(0)
                    if P > 1:
                        emit_Z(1)
                    for p in range(P + 1):
                        if p >= 1:
                            kbp, offp, lop, segsp = geom(p - 1)
                            e1p, te1p = e1s[(p - 1) % 3], t_e1[(p - 1) % 3]
                            spp, tspp = sps[(p - 1) % 2], t_sps[(p - 1) % 2]
                            gp, tgp = gs_[(p - 1) % 2], t_gs_[(p - 1) % 2]
                            wp_, twp = ws_[(p - 1) % 2], t_ws_[(p - 1) % 2]
                            S.op("act", lambda gp=gp, lop=lop: act.activation(
                                out=gp[:, lop:TQ], in_=PS[:, BB * 512 + lop:BB * 512 + TQ], func=AF.Exp, scale=-1.0),
                                reads=zb_reads(segsp, BB), writes=[tgp])
                            if p - 1 < P - 1:
                                for (bki, c0, c1) in segsp:
                                    S.op("pe", lambda bki=bki, c0=c0, c1=c1, spp=spp: mm(
                                        bank(BB + bki)[:, c0 - bki * 512:c1 - bki * 512], lhsT=Ustr_b, rhs=spp[:, c0:c1],
                                        start=False, stop=True),
                                        reads=[tspp, t_cst], writes=[t_ps[BB + bki]])
                        if p < P:
                            kb, off, lo, segs = geom(p)
                            e1, te1 = e1s[p % 3], t_e1[p % 3]
                            sp_, tsp = sps[p % 2], t_sps[p % 2]
                            S.op("act", lambda e1=e1, sp_=sp_, lo=lo: act.activation(
                                out=sp_[:, lo:TQ], in_=e1[:, lo:TQ], func=AF.Ln, bias=1.0),
                                reads=[te1], writes=[tsp])
                            for (bki, c0, c1) in segs:
                                S.op("pe", lambda bki=bki, c0=c0, c1=c1, sp_=sp_, st=(not startedB[bki]): mm(
                                    bank(BB + bki)[:, c0 - bki * 512:c1 - bki * 512], lhsT=Lincl_b, rhs=sp_[:, c0:c1],
                                    start=st, stop=True),
                                    reads=[tsp, t_cst], writes=[t_ps[BB + bki]])
                                startedB[bki] = True
                        if p + 1 < P:
                            emit_E1(p + 1)
                        if p + 2 < P:
                            emit_Z(p + 2)
                        pump(nxt, npump // 2)
                        if p >= 1:
                            S.op("dve", lambda wp_=wp_, e1p=e1p, gp=gp, lop=lop: dve.tensor_tensor(
                                out=wp_[:, lop:TQ], in0=e1p[:, lop:TQ], in1=gp[:, lop:TQ], op=ALU.mult),
                                reads=[te1p, tgp], writes=[twp])
                            for (bki, c0, c1) in segsp:
                                S.op("pe", lambda bki=bki, c0=c0, c1=c1, wp_=wp_, kbp=kbp, st=(not startedO[bki]): mm(
                                    bank(OB + bki)[:, c0 - bki * 512:c1 - bki * 512], lhsT=s_v[:, kbp, :], rhs=wp_[:, c0:c1],
                                    start=st, stop=True),
                                    reads=[t_sv[kbp // 4], twp], writes=[t_ps[OB + bki]])
                                startedO[bki] = True
                        pump(nxt, npump - npump // 2)
                    pump(nxt, 10 ** 6)
                    S.op("dve", lambda: dve.tensor_tensor(out=s_y[:], in0=PS[:, OB * 512:OB * 512 + TQ], in1=s_gs[:], op=ALU.mult),
                         reads=[t_ps[OB + i] for i in range(NBK)] + [tg_], writes=[t_sy])
                    S.dma("sp", ms_own[hd, q0 // HS, :, q0 % HS:q0 % HS + TQ], s_y[:], reads=[t_sy], writes=[t_msown])
                    if tb == S_LEN // TQ - 1 and hd < 3:
                        S._wait("pool", S._deps("pool", [t_msown], []))
                        issue_cc(ms_own[hd].rearrange("hh p t -> (hh p) t"), ms_all[hd])
        S.barrier(skip=("cc",))

    BT = 256
    if mode == "AB":
        par = nc.sync.partition_id() % 2
        mg_v = mg_all.rearrange("k (c p) t -> k p c t", p=128)
        ms_v = ms_all.rearrange("h q t -> (h q) t").rearrange("(hr hh p) t -> hh p hr t", hr=8, hh=2)
    else:
        mixin_v = mixin.rearrange("(c p) t -> p c t", p=128)
    with ExitStack() as _es3:
        wo_b = _es3.enter_context(nc.sbuf_tensor("wo_b", [128, NCH, D], BF16))
        wgt_b = _es3.enter_context(nc.sbuf_tensor("wgt_b", [128, NCH, D], BF16))
        wp_b = _es3.enter_context(nc.sbuf_tensor("wp_b", [128, 2, D], BF16))
        mxa = _es3.enter_context(nc.sbuf_tensor("mxa", [128, NCH, BT], BF16))
        xra = _es3.enter_context(nc.sbuf_tensor("xra", [128, NCH, BT], F32))
        pbts = (_es3.enter_context(nc.sbuf_tensor("pbt", [128, 2, BT], BF16)), _es3.enter_context(nc.sbuf_tensor("pbt2", [128, 2, BT], BF16)))
        m1 = _es3.enter_context(nc.sbuf_tensor("m1", [128, NCH, BT], F32))
        sq3 = _es3.enter_context(nc.sbuf_tensor("sq3", [128, NCH, BT], BF16))
        r3 = _es3.enter_context(nc.sbuf_tensor("r3", [128, BT], F32))
        gte = _es3.enter_context(nc.sbuf_tensor("gte", [128, 2, BT], F32))
        oo = _es3.enter_context(nc.sbuf_tensor("oo", [128, 4, BT], F32))
        t_wo = [T() for _ in range(4)]
        t_wgt = [T() for _ in range(4)]
        t_wp = T()
        for j in range(4):
            S.dma("pool", wo_b[:, :, j * 512:(j + 1) * 512],
                  wo[:, j * 512:(j + 1) * 512].rearrange("(c p) n -> p c n", p=128), writes=[t_wo[j]])
        for j in range(4):
            S.dma("pool", wgt_b[:, :, j * 512:(j + 1) * 512],
                  wgt[:, j * 512:(j + 1) * 512].rearrange("(c p) n -> p c n", p=128), writes=[t_wgt[j]])
        S.dma("pool", wp_b[:], wp.rearrange("(c p) n -> p c n", p=128), writes=[t_wp])
        if not SKIP:
            issue_cc(ms_own[3].rearrange("hh p t -> (hh p) t"), ms_all[3])
        mxs, t_mx = (mxa, mxa), (T(),) * 2
        xrs, t_xr = (xra, xra), (T(),) * 2
        h1b = sq3
        t_pb, t_m1, t_sq3, t_r3 = T(), T(), T(), T()
        t_h1b = t_sq3
        t_gte = (T(), T())
        t_oo = [T() for _ in range(4)]
        xTo_v = xTo.rearrange("(c p) t -> p c t", p=128)
        pTo_v = pTo.rearrange("(c p) t -> p c t", p=128)
        outT_v = outT.rearrange("(c p) t -> p c t", p=128)
        rot3 = [0]

        def nb3():
            rot3[0] = (rot3[0] + 1) % 8
            return rot3[0]

        t_m1c = [T() for _ in range(NCH)]
        t_mxp = [T() for _ in range(5)]
        t_sqc = [T() for _ in range(NCH)]
        t_pbs = (T(), T())
        NB3 = HS // BT

        def emit_loads(nb):
            tsl = slice(nb * BT, (nb + 1) * BT)
            if mode == "AB":
                jj = (nb * BT) // EB
                cc0 = (nb * BT) % EB
                S.dma("sp", mxa[:, 0:8, :], mg_v[bass.ds(par * 2 + jj, 1), :, :, cc0:cc0 + BT].rearrange("o p c t -> p (o c) t"),
                      reads=[t_mix], writes=[t_mxp[0]])
                S.dma("sp", mxa[:, 8:16, :],
                      ms_v[bass.ds(par, 1), :, :, nb * BT:(nb + 1) * BT].rearrange("o p r t -> p (o r) t"),
                      reads=[t_mix], writes=[t_mxp[1]])
            else:
                S.dma("sp", mxa[:], mixin_v[:, :, tsl], writes=[t_mx[0]])
            S.dma("sp", xra[:], xTo_v[:, :, tsl], writes=[t_xr[0]])
            S.dma("pool", pbts[nb % 2][:], pTo_v[:, :, tsl], writes=[t_pbs[nb % 2]])

        for nb in range(NB3):
            emit_loads(nb)
            tsl = slice(nb * BT, (nb + 1) * BT)
            mx, tmx = mxa, t_mx[0]
            xr, txr = xra, t_xr[0]
            pbt, t_pb = pbts[nb % 2], t_pbs[nb % 2]
            for oc in range(NCH):
                bk = nb3()
                for c in range(NCH):
                    S.op("pe", lambda c=c, oc=oc, bk=bk: mm(
                        bank(bk)[:, 0:BT], lhsT=wo_b[:, c, oc * 128:(oc + 1) * 128], rhs=mx[:, c, :],
                        start=(c == 0), stop=(c == NCH - 1)),
                        reads=[t_wo[oc // 4], tmx] + t_mxp, writes=[t_ps[bk]], inc=(c == NCH - 1))
                S.op("act", lambda oc=oc, bk=bk: act.activation(out=sq3[:, oc, :], in_=bank(bk)[:, 0:BT], func=AF.Square),
                     reads=[t_ps[bk]], writes=[t_sqc[oc]])
                S.op("dve", lambda oc=oc, bk=bk: dve.tensor_scalar_mul(
                    out=m1[:, oc, :], in0=bank(bk)[:, 0:BT], scalar1=cols_f[:, 16 + oc:17 + oc]),
                    reads=[t_ps[bk], t_cst], writes=[t_m1c[oc]])
            bk = nb3()
            for c in range(NCH):
                S.op("pe", lambda c=c, bk=bk: mm(bank(bk)[:, 0:BT], lhsT=ones_b, rhs=sq3[:, c, :],
                                                       start=(c == 0), stop=(c == NCH - 1)),
                     reads=[t_sqc[c], t_cst], writes=[t_ps[bk]], inc=(c == NCH - 1))
            S.op("act", lambda bk=bk: act.activation(out=r3[:], in_=bank(bk)[:, 0:BT], func=AF.Ln, scale=1.0 / D, bias=EPS),
                 reads=[t_ps[bk]], writes=[t_r3])
            S.op("act", lambda: act.activation(out=r3[:], in_=r3[:], func=AF.Exp, scale=-0.5),
                 reads=[t_r3], writes=[t_r3])
            for oc in range(NCH):
                S.op("dve", lambda oc=oc: dve.tensor_tensor(out=m1[:, oc, :], in0=m1[:, oc, :], in1=r3[:], op=ALU.mult),
                     reads=[t_m1c[oc], t_r3], writes=[t_m1c[oc]])
                S.op("pool", lambda oc=oc: pool.tensor_tensor(out=m1[:, oc, :], in0=m1[:, oc, :], in1=xr[:, oc, :], op=ALU.add),
                     reads=[t_m1c[oc], txr], writes=[t_m1c[oc]])
                S.op("act", lambda oc=oc: act.activation(out=h1b[:, oc, :], in_=m1[:, oc, :], func=AF.Copy),
                     reads=[t_m1c[oc]], writes=[t_sqc[oc]])
            for oc in range(NCH):
                bk = nb3()
                for c in range(NCH):
                    S.op("pe", lambda c=c, oc=oc, bk=bk: mm(
                        bank(bk)[:, 0:BT], lhsT=wgt_b[:, c, oc * 128:(oc + 1) * 128], rhs=h1b[:, c, :],
                        start=(c == 0), stop=(c == NCH - 1)),
                        reads=[t_wgt[oc // 4], t_sqc[c]], writes=[t_ps[bk]], inc=(c == NCH - 1))
                gt, tgt = gte[:, oc % 2, :], t_gte[oc % 2]
                S.op("act", lambda oc=oc, bk=bk, gt=gt: act.activation(
                    out=gt, in_=bank(bk)[:, 0:BT], func=AF.Sigmoid, bias=cols_f[:, 32 + oc:33 + oc]),
                    reads=[t_ps[bk], t_cst], writes=[tgt])
                bk2 = nb3()
                for c in range(2):
                    S.op("pe", lambda c=c, oc=oc, bk2=bk2: mm(
                        bank(bk2)[:, 0:BT], lhsT=wp_b[:, c, oc * 128:(oc + 1) * 128], rhs=pbt[:, c, :],
                        start=(c == 0), stop=(c == 1)),
                        reads=[t_wp, t_pb], writes=[t_ps[bk2]], inc=(c == 1))
                S.op("dve", lambda oc=oc, bk2=bk2, gt=gt: dve.tensor_tensor(
                    out=gt, in0=gt, in1=bank(bk2)[:, 0:BT], op=ALU.mult),
                    reads=[tgt, t_ps[bk2]], writes=[tgt])
                S.op("dve", lambda oc=oc, gt=gt: dve.tensor_tensor(
                    out=oo[:, oc % 4, :], in0=gt, in1=m1[:, oc, :], op=ALU.add),
                    reads=[tgt, t_m1c[oc]], writes=[t_oo[oc % 4]])
                if oc % 4 == 3:
                    S.dma("pool", outT_v[:, oc - 3:oc + 1, tsl], oo[:], reads=t_oo)
        S.barrier()
    return nc


_PROG = {}


def kernel(x, p, g_pre, w_in, w_a2, b_a, g_gla_head, w_out, g_post, w_ple_gate, b_ple_gate, w_ple_proj):
    x = np.asarray(x, np.float32)
    B, S_LEN, _ = x.shape
    HS = S_LEN // 2
    f = lambda a: np.ascontiguousarray(np.asarray(a, np.float32))
    p, g_pre, w_in, w_a2, b_a = f(p), f(g_pre), f(w_in), f(w_a2), f(b_a)
    g_gla_head, w_out, g_post = f(g_gla_head), f(w_out), f(g_post)
    w_ple_gate, b_ple_gate, w_ple_proj = f(w_ple_gate), f(b_ple_gate), f(w_ple_proj)
    W = w_in[0]
    GQ, GK, GV, GG, LR, SQ, SK, SV, SG = 0, 512, 1024, 2048, 3072, 3088, 4112, 5136, 6160

    def colv(v):
        return v.reshape(-1, 128).T

    ii = np.arange(128)
    cst = np.zeros((128, 5, 128), np.float32)
    cst[:, 0, :] = (ii[:, None] == ii[None, :])
    cst[:, 1, :] = (ii[:, None] <= ii[None, :])
    cst[:, 2, :] = (ii[:, None] < ii[None, :])
    cst[:, 3, :] = (ii[:, None] >= ii[None, :])
    cst[:, 4, :] = 1.0
    cols = np.zeros((128, 64), np.float32)
    cols[:, 0:16] = colv(g_pre[0])
    cols[:, 16:32] = colv(g_post[0])
    cols[:, 32:48] = colv(b_ple_gate[0])
    cols[:, 48:50] = colv(g_gla_head[0])
    in_maps = []
    for core in range(8):
        b, hh = core // 2, core % 2
        wsb = np.stack([np.concatenate([W[:, SQ + 128 * H:SQ + 128 * H + 128], W[:, SK + 128 * H:SK + 128 * H + 128],
                                        W[:, SV + 128 * H:SV + 128 * H + 128], W[:, SG + 128 * H:SG + 128 * H + 128]], axis=1)
                        for H in range(4 * hh, 4 * hh + 4)])
        wgla = np.stack([np.concatenate([W[:, GQ + 128 * G:GQ + 128 * G + 128], W[:, GK + 128 * G:GK + 128 * G + 128],
                                         W[:, GV + 256 * G:GV + 256 * G + 256], W[:, GG + 256 * G:GG + 256 * G + 256]], axis=1)
                         for G in range(2 * hh, 2 * hh + 2)])
        xTb = np.ascontiguousarray(x[b].T)
        wo_perm = np.concatenate([w_out[0][0:1024]] + [w_out[0][1024 + (4 * r + h) * 128:1024 + (4 * r + h) * 128 + 128]
                                                       for h in range(4) for r in range(2)], axis=0)
        in_maps.append({
            "xT": xTb,
            "xTo": np.ascontiguousarray(xTb[:, hh * HS:(hh + 1) * HS]),
            "pTo": np.ascontiguousarray(p[0, b, hh * HS:(hh + 1) * HS].T),
            "wsb": np.ascontiguousarray(wsb),
            "wgla": np.ascontiguousarray(wgla),
            "wlr": np.ascontiguousarray(W[:, LR:LR + 16]),
            "wa2": np.ascontiguousarray(w_a2[0][:, 256 * hh:256 * hh + 256]),
            "ba": np.ascontiguousarray(b_a[0][None, 256 * hh:256 * hh + 256]),
            "cols": cols,
            "wo": np.ascontiguousarray(wo_perm),
            "wgt": w_ple_gate[0],
            "wp": w_ple_proj[0],
            "cst": cst,
        })
    if S_LEN not in _PROG:
        _PROG[S_LEN] = build_program(S_LEN, "AB")
    res = run_bass_kernel_spmd(_PROG[S_LEN], in_maps, core_ids=list(range(8)))
    out = np.empty((B, S_LEN, D), np.float32)
    for core in range(8):
        b, hh = core // 2, core % 2
        out[b, hh * HS:(hh + 1) * HS, :] = res.results[core]["outT"].T
    return out
```

```python
from contextlib import ExitStack
import numpy as np
import concourse.bass as bass
import concourse.mybir as mybir
from concourse.bass_utils import run_bass_kernel_spmd

F32 = mybir.dt.float32
BF16 = mybir.dt.bfloat16
AF = mybir.ActivationFunctionType
ALU = mybir.AluOpType

D = 2048
NCH = 16
EPS = 1e-6
TQ = 1024
NBK = TQ // 512


class T:
    __slots__ = ("w", "r", "x")

    def __init__(self, x=False):
        self.w = None
        self.r = []
        self.x = x


class Sched:
    ENG = ("pe", "act", "dve", "pool", "sp")

    def __init__(self, nc, n_dma_sems=28):
        self.nc = nc
        self.e = dict(pe=nc.tensor, act=nc.scalar, dve=nc.vector, pool=nc.gpsimd, sp=nc.sync)
        self.sems = {}
        self.cnt = {}
        for k in self.ENG:
            self.sems[k] = nc.alloc_semaphore("s_" + k)
            self.cnt[k] = 0
        self.dma_keys = []
        for i in range(n_dma_sems):
            k = "d%d" % i
            self.sems[k] = nc.alloc_semaphore("s_" + k)
            self.cnt[k] = 0
            self.dma_keys.append(k)
        self.sems["cc"] = nc.alloc_semaphore("s_cc")
        self.cnt["cc"] = 0
        self.dma_rr = 0
        self.seen = {k: {} for k in self.ENG}

    def _deps(self, eng, reads, writes):
        need = {}

        def add(d, same_ok):
            if d is None:
                return
            k, v = d
            if k == eng and same_ok:
                return
            if need.get(k, 0) < v:
                need[k] = v
        for t in reads:
            add(t.w, False)
            if t.x:
                for d in t.r:
                    add(d, True)
        for t in writes:
            add(t.w, True)
            for d in t.r:
                add(d, False)
        return need

    def _wait(self, eng, need):
        seen = self.seen[eng]
        for k, v in need.items():
            if seen.get(k, 0) < v:
                self.e[eng].wait_ge(self.sems[k], v)
                seen[k] = v

    def _record(self, d, reads, writes):
        for t in reads:
            t.r.append(d)
        for t in writes:
            t.w = d
            t.r = []

    def op(self, eng, fn, reads=(), writes=(), inc=True):
        self._wait(eng, self._deps(eng, reads, writes))
        ins = fn()
        if inc:
            self.cnt[eng] += 1
            ins.then_inc(self.sems[eng], 1)
            seq = self.cnt[eng]
        else:
            seq = self.cnt[eng] + 1
        self._record((eng, seq), reads, writes)
        return ins

    def dma(self, eng, out, in_, reads=(), writes=()):
        self._wait(eng, self._deps(eng, reads, writes))
        k = self.dma_keys[self.dma_rr]
        self.dma_rr = (self.dma_rr + 1) % len(self.dma_keys)
        ins = self.e[eng].dma_start(out=out, in_=in_)
        self.cnt[k] += 16
        ins.then_inc(self.sems[k], 16)
        self._record((k, self.cnt[k]), reads, writes)
        return ins

    def barrier(self, skip=()):
        for eng in self.ENG:
            need = {k: v for k, v in self.cnt.items() if v > 0 and k not in skip}
            self._wait(eng, need)


def build_program(S_LEN, mode="AB"):
    STOP = 9
    P3 = 9
    HS = S_LEN // 2
    nc = bass.Bass("TRN2", target_bir_lowering=False)
    S = Sched(nc)
    pe, act, dve, pool = nc.tensor, nc.scalar, nc.vector, nc.gpsimd

    def mm(out, **kw):
        return pe.matmul(out, skip_group_check=True, **kw)

    def din(name, shape, dt=F32):
        return nc.dram_tensor(name, shape, dt, kind="ExternalInput").ap()

    xT = din("xT", [D, S_LEN])
    xTo = din("xTo", [D, HS])
    pTo = din("pTo", [256, HS])
    wsb = din("wsb", [4, D, 512])
    wgla = din("wgla", [2, D, 768])
    wlr = din("wlr", [D, 16])
    wa2 = din("wa2", [16, 256])
    ba = din("ba", [1, 256])
    cols = din("cols", [128, 64])
    wo = din("wo", [D, D])
    wgt = din("wgt", [D, D])
    wp = din("wp", [256, D])
    cst = din("cst", [128, 5, 128])
    outT = nc.dram_tensor("outT", [D, HS], F32, kind="ExternalOutput").ap()
    if mode == "A":
        mix_own = nc.dram_tensor("mix_own", [4, 1024, S_LEN // 4], BF16, kind="ExternalOutput").ap()
    else:
        mix_own = nc.dram_tensor("mix_own", [4, 1024, S_LEN // 4], BF16).ap()
    mix_all = nc.dram_tensor("mix_all", [4, 2048, S_LEN // 4], BF16).ap()
    mg_own = nc.dram_tensor("mg_own", [4, 512, S_LEN // 4], BF16).ap()
    mg_all = nc.dram_tensor("mg_all", [4, 1024, S_LEN // 4], BF16).ap()
    ms_own = nc.dram_tensor("ms_own", [4, 2, 128, HS], BF16).ap()
    ms_all = nc.dram_tensor("ms_all", [4, 512, HS], BF16).ap()
    GROUPS = [[0, 1], [2, 3], [4, 5], [6, 7]]
    t_mix = T()

    def issue_cc(src, dst):
        pool.collective_compute("AllGather", ALU.bypass, replica_groups=GROUPS, ins=[src], outs=[dst]
                                ).then_inc(S.sems["cc"], 1)
        S.cnt["cc"] += 1
        t_mix.w = ("cc", S.cnt["cc"])
    EB = S_LEN // 4
    if mode == "B":
        mixin = din("mixin", [2048, HS], BF16)
    SKIP = (mode == "B")

    cst_f = nc.alloc_sbuf_tensor("cst_f", [128, 5, 128], F32)
    cst_b = nc.alloc_sbuf_tensor("cst_b", [128, 5, 128], BF16)
    cols_f = nc.alloc_sbuf_tensor("cols_f", [128, 64], F32)
    wa2_f = nc.alloc_sbuf_tensor("wa2_f", [16, 256], F32)
    ba_f = nc.alloc_sbuf_tensor("ba_f", [1, 256], F32)
    t_cst = T()
    S.dma("sp", cst_f[:], cst[:, :, :], writes=[t_cst])
    S.dma("sp", cols_f[:], cols[:, :], writes=[t_cst])
    S.dma("sp", wa2_f[:], wa2[:, :], writes=[t_cst])
    S.dma("sp", ba_f[:], ba[:, :], writes=[t_cst])
    S.op("dve", lambda: dve.tensor_copy(cst_b[:], cst_f[:]), reads=[t_cst], writes=[t_cst])
    ident_b = cst_b[:, 0, :]
    Uincl_f = cst_f[:, 1, :]
    Ustr_f = cst_f[:, 2, :]
    Ustr_b = cst_b[:, 2, :]
    Lincl_b = cst_b[:, 3, :]
    ones_b = cst_b[:, 4, :]
    ones_f = cst_f[:, 4, :]

    PS = nc.alloc_psum_tensor("ps", [128, 4096], F32)
    t_ps = [T(True) for _ in range(8)]

    def bank(i):
        return PS[:, i * 512:(i + 1) * 512]

    with nc.sbuf_tensor("hT", [128, NCH, S_LEN], BF16) as hT:
        NB0 = S_LEN // 256
        NLOOP0 = 0 if SKIP else NB0
        t_hT = [T() for _ in range(NB0)]
        t_hTp = [T() for _ in range(NB0)]

        def h_reads(t0, t1):
            rng = range(t0 // 256, (t1 + 255) // 256)
            return [t_hT[i] for i in rng] + [t_hTp[i] for i in rng]

        with ExitStack() as _es0:
            xb0 = _es0.enter_context(nc.sbuf_tensor("xb0", [128, NCH, 256], F32))
            xb1 = _es0.enter_context(nc.sbuf_tensor("xb1", [128, NCH, 256], F32))
            sq0 = _es0.enter_context(nc.sbuf_tensor("sq0", [128, NCH, 256], BF16))
            r0a = _es0.enter_context(nc.sbuf_tensor("r0", [128, 256], F32))
            r0b = _es0.enter_context(nc.sbuf_tensor("r1", [128, 256], F32))
            p0tmp = (_es0.enter_context(nc.sbuf_tensor("p0ta", [128, 256], F32)), _es0.enter_context(nc.sbuf_tensor("p0tb", [128, 256], F32)))
            t_p0tmp = (T(), T())
            xbs = (xb0, xb1)
            t_xb = (T(), T())
            t_sq = T()
            rs = (r0a, r0b)
            t_r = (T(), T())
            xT_v = xT.rearrange("(c p) t -> p c t", p=128)
            for nb in range(NLOOP0):
                xb = xbs[nb % 2]
                txb = t_xb[nb % 2]
                r = rs[nb % 2]
                tr = t_r[nb % 2]
                tsl = slice(nb * 256, (nb + 1) * 256)
                S.dma("sp", xb[:], xT_v[:, :, tsl], writes=[txb])
                S.op("act", lambda xb=xb: act.activation(out=sq0[:], in_=xb[:], func=AF.Square),
                     reads=[txb], writes=[t_sq])
                bk = nb % 2
                for c in range(NCH):
                    S.op("pe", lambda c=c, bk=bk: mm(bank(bk)[:, 0:256], lhsT=ones_b, rhs=sq0[:, c, :],
                                                           start=(c == 0), stop=(c == NCH - 1)),
                         reads=[t_sq, t_cst], writes=[t_ps[bk]], inc=(c == NCH - 1))
                S.op("act", lambda r=r, bk=bk: act.activation(out=r[:], in_=bank(bk)[:, 0:256], func=AF.Ln,
                                                               scale=1.0 / D, bias=EPS),
                     reads=[t_ps[bk]], writes=[tr])
                S.op("act", lambda r=r: act.activation(out=r[:], in_=r[:], func=AF.Exp, scale=-0.5),
                     reads=[tr], writes=[tr])
                for c in range(NCH):
                    if c % 3 == 2:
                        k = (c // 3) % 2
                        S.op("act", lambda c=c, xb=xb, k=k: act.activation(
                            out=p0tmp[k][:], in_=xb[:, c, :], func=AF.Identity, scale=cols_f[:, c:c + 1]),
                            reads=[txb, t_cst], writes=[t_p0tmp[k]])
                        S.op("pool", lambda c=c, r=r, k=k: pool.tensor_tensor(
                            out=hT[:, c, tsl], in0=p0tmp[k][:], in1=r[:], op=ALU.mult),
                            reads=[t_p0tmp[k], tr], writes=[t_hTp[nb]])
                        continue
                    S.op("dve", lambda c=c, xb=xb, r=r: dve.scalar_tensor_tensor(
                        out=hT[:, c, tsl], in0=xb[:, c, :], scalar=cols_f[:, c:c + 1], in1=r[:],
                        op0=ALU.mult, op1=ALU.mult),
                        reads=[txb, tr, t_cst], writes=[t_hT[nb]])
        S.barrier()
        if STOP <= 0:
            return nc

        def silu_evac(src_ap, dst_ap, tmp_a, tmp_b, t_tmp, reads, writes):
            S.op("act", lambda: act.activation(out=tmp_a, in_=src_ap, func=AF.Exp, scale=-1.0),
                 reads=reads, writes=[t_tmp])
            S.op("dve", lambda: dve.tensor_scalar_add(out=tmp_a, in0=tmp_a, scalar1=1.0),
                 reads=[t_tmp], writes=[t_tmp])
            S.op("dve", lambda: dve.reciprocal(out=tmp_b, in_=tmp_a), reads=[t_tmp], writes=[t_tmp])
            S.op("dve", lambda: dve.tensor_tensor(out=dst_ap, in0=src_ap, in1=tmp_b, op=ALU.mult),
                 reads=list(reads) + [t_tmp], writes=writes)

        with ExitStack() as _es1:
            def A1(name, shape, dt):
                return _es1.enter_context(nc.sbuf_tensor(name, shape, dt))
            wg = A1("wg", [128, NCH, 768], BF16)
            wlr_b = A1("wlr_b", [128, NCH, 16], BF16)
            g_qTs = (A1("g_qTa", [128, 512], BF16), A1("g_qTb", [128, 512], BF16))
            g_kTs = (A1("g_kTa", [128, 512], BF16), A1("g_kTb", [128, 512], BF16))
            g_silus = (A1("g_silua", [128, 2, 512], BF16), A1("g_silub", [128, 2, 512], BF16))
            g_vs = (A1("g_va", [128, 4, 256], BF16), A1("g_vb", [128, 4, 256], BF16))
            g_lr1 = A1("g_lra", [16, 512], F32)
            g_lrs = (g_lr1, g_lr1)
            g_tmpa = A1("g_tmpa", [128, 512], F32)
            g_tmpb = A1("g_tmpb", [128, 512], F32)
            g_mask4 = A1("g_mask4", [128, 4, 128], F32)
            g_e = A1("g_e", [128, 512], F32)
            g_tmpc = g_e
            g_sp = A1("g_sp", [128, 4, 128], F32)
            g_Eq = A1("g_Eq", [128, 512], F32)
            g_Ek = A1("g_Ek", [128, 512], F32)
            g_qe = A1("g_qe", [128, 512], BF16)
            g_ke = A1("g_ke", [128, 512], BF16)
            g_klT = A1("g_klT", [128, 512], BF16)
            g_kl = A1("g_kl", [128, 4, 128], BF16)
            g_scm = A1("g_scm", [128, 512], BF16)
            g_Sf = A1("g_Sf", [128, 256], F32)
            g_Sb = A1("g_Sb", [128, 4, 256], BF16)
            g_sq = A1("g_sq", [128, 2, 512], BF16)
            g_r = A1("g_r", [128, 512], F32)
            g_y = A1("g_y", [128, 2, 512], BF16)
            t_wg, t_wlr, t_mask4 = T(), T(), T()
            t_qT, t_kT, t_silu, t_v = ((T(), T()) for _ in range(4))
            t_lr = (T(),) * 2
            t_tmp = T()
            t_e, t_sp, t_Eq, t_Ek, t_qe, t_ke, t_klT, t_kl, t_scm = (T() for _ in range(9))
            t_Sf, t_sq2, t_r2, t_y = T(), T(), T(), T()
            t_tmpc = t_e
            t_Sb = [T() for _ in range(4)]
            S.dma("pool", wlr_b[:], wlr.rearrange("(c p) n -> p c n", p=128), writes=[t_wlr])
            for cc in range(4):
                S.op("pool", lambda cc=cc: pool.tensor_copy(g_mask4[:, cc, :], Uincl_f), reads=[t_cst], writes=[t_mask4])
            gblocks = [] if SKIP else [(g, nb) for g in range(2) for nb in range(S_LEN // 512)]
            rot = [0]

            def nextbank():
                rot[0] ^= 1
                return rot[0]

            def gla_inproj_gen(bi):
                g, nb = gblocks[bi]
                par = bi % 2
                t0 = nb * 512
                tok = slice(t0, t0 + 512)
                hr = h_reads(t0, t0 + 512)
                if nb == 0:
                    S.dma("pool", wg[:], wgla[g].rearrange("(c p) n -> p c n", p=128), writes=[t_wg])
                pending = [None]

                def flush():
                    if pending[0] is not None:
                        pending[0]()
                        pending[0] = None

                def group(lhs_fn, rhs_fn, out_fn, evac, extra_reads):
                    bk = nextbank()
                    for c in range(NCH):
                        S.op("pe", lambda c=c, bk=bk: mm(out_fn(bk), lhsT=lhs_fn(c), rhs=rhs_fn(c),
                                                               start=(c == 0), stop=(c == NCH - 1)),
                             reads=hr + extra_reads, writes=[t_ps[bk]], inc=(c == NCH - 1))
                        if c == 3:
                            flush()
                    pending[0] = lambda bk=bk: evac(bk)

                group(lambda c: wg[:, c, 0:128], lambda c: hT[:, c, tok], lambda bk: bank(bk),
                      lambda bk: S.op("act", lambda: act.activation(out=g_qTs[par][:], in_=bank(bk), func=AF.Identity,
                                                                    scale=128 ** -0.5),
                                      reads=[t_ps[bk]], writes=[t_qT[par]]), [t_wg])
                yield
                group(lambda c: wg[:, c, 128:256], lambda c: hT[:, c, tok], lambda bk: bank(bk),
                      lambda bk: S.op("dve", lambda: dve.tensor_copy(g_kTs[par][:], bank(bk)),
                                      reads=[t_ps[bk]], writes=[t_kT[par]]), [t_wg])
                yield
                for ec in range(2):
                    group(lambda c, ec=ec: wg[:, c, 512 + ec * 128:512 + ec * 128 + 128], lambda c: hT[:, c, tok],
                          lambda bk: bank(bk),
                          lambda bk, ec=ec: silu_evac(bank(bk), g_silus[par][:, ec, :], g_tmpa[:], g_tmpb[:], t_tmp,
                                                      [t_ps[bk]], [t_silu[par]]), [t_wg])
                    yield
                for sb in range(4):
                    group(lambda c, sb=sb: hT[:, c, t0 + sb * 128:t0 + sb * 128 + 128], lambda c: wg[:, c, 256:512],
                          lambda bk: bank(bk)[:, 0:256],
                          lambda bk, sb=sb: S.op("dve", lambda: dve.tensor_copy(g_vs[par][:, sb, :], bank(bk)[:, 0:256]),
                                                 reads=[t_ps[bk]], writes=[t_v[par]]), [t_wg])
                    yield
                group(lambda c: wlr_b[:, c, :], lambda c: hT[:, c, tok], lambda bk: bank(bk)[0:16, :],
                      lambda bk: S.op("dve", lambda: dve.tensor_copy(g_lrs[par][:], bank(bk)[0:16, :]),
                                      reads=[t_ps[bk]], writes=[t_lr[par]]), [t_wlr])
                flush()
                yield

            def gpump(gen, n):
                if gen is None:
                    return
                for _ in range(n):
                    try:
                        next(gen)
                    except StopIteration:
                        return

            if gblocks:
                gpump(gla_inproj_gen(0), 100)
            for bi, (g, nb) in enumerate(gblocks):
                par = bi % 2
                g_qT, g_kT, g_silu, g_v, g_lr = g_qTs[par], g_kTs[par], g_silus[par], g_vs[par], g_lrs[par]
                tqT, tkT, tsilu, tv, tlr = t_qT[par], t_kT[par], t_silu[par], t_v[par], t_lr[par]
                nxt = gla_inproj_gen(bi + 1) if bi + 1 < len(gblocks) else None
                t0 = nb * 512
                gsl = slice(g * 128, g * 128 + 128)
                if nb == 0:
                    S.op("dve", lambda: dve.memset(g_Sf[:], 0.0), writes=[t_Sf])
                    S.op("dve", lambda: dve.memset(g_Sb[:, 0, :], 0.0), writes=[t_Sb[0]])
                for cc in range(4):
                    cs = slice(cc * 128, cc * 128 + 128)
                    S.op("pe", lambda cs=cs, cc=cc: mm(
                        bank(2)[:, cs], lhsT=g_lr[0:16, cs], rhs=wa2_f[0:16, gsl], start=(cc == 0), stop=False),
                        reads=[tlr, t_cst], writes=[t_ps[2]], inc=False)
                    S.op("pe", lambda cs=cs, cc=cc: mm(
                        bank(2)[:, cs], lhsT=ones_f[0:1, :], rhs=ba_f[0:1, gsl], start=False, stop=True),
                        reads=[t_cst], writes=[t_ps[2]], inc=(cc == 3))
                gpump(nxt, 1)
                S.op("act", lambda: act.activation(out=g_e[:], in_=bank(2), func=AF.Exp, scale=-1.0),
                     reads=[t_ps[2]], writes=[t_e])
                S.op("act", lambda: act.activation(out=g_sp[:].rearrange("p a b -> p (a b)"), in_=g_e[:], func=AF.Ln, bias=1.0),
                     reads=[t_e], writes=[t_sp])
                for cc in range(4):
                    cs = slice(cc * 128, cc * 128 + 128)
                    S.op("pe", lambda cs=cs, cc=cc: mm(bank(3)[:, cs], lhsT=g_sp[:, cc, :], rhs=Uincl_f,
                                                              start=(cc == 0), stop=True),
                         reads=[t_sp, t_cst], writes=[t_ps[3]], inc=(cc == 3))
                gpump(nxt, 1)
                S.op("act", lambda: act.activation(out=g_Eq[:], in_=bank(3), func=AF.Exp, scale=-1.0 / 16),
                     reads=[t_ps[3]], writes=[t_Eq])
                S.op("act", lambda: act.activation(out=g_Ek[:], in_=bank(3), func=AF.Exp, scale=1.0 / 16),
                     reads=[t_ps[3]], writes=[t_Ek])
                S.op("dve", lambda: dve.tensor_tensor(out=g_ke[:], in0=g_kT[:], in1=g_Ek[:], op=ALU.mult),
                     reads=[tkT, t_Ek], writes=[t_ke])
                for cc in range(4):
                    cs = slice(cc * 128, cc * 128 + 128)
                    S.op("dve", lambda cs=cs, cc=cc: dve.scalar_tensor_tensor(
                        out=g_klT[:, cs], in0=g_kT[:, cs], scalar=g_Eq[:, cc * 128 + 127:cc * 128 + 128], in1=g_Ek[:, cs],
                        op0=ALU.mult, op1=ALU.mult),
                        reads=[tkT, t_Eq, t_Ek], writes=[t_klT])
                S.op("dve", lambda: dve.tensor_tensor(out=g_qe[:], in0=g_qT[:], in1=g_Eq[:], op=ALU.mult),
                     reads=[tqT, t_Eq], writes=[t_qe])
                for cc in range(4):
                    cs = slice(cc * 128, cc * 128 + 128)
                    S.op("pe", lambda cs=cs, cc=cc: mm(bank(4)[:, cs], lhsT=g_klT[:, cs], rhs=ident_b,
                                                              start=(cc == 0), stop=True),
                         reads=[t_klT, t_cst], writes=[t_ps[4]], inc=(cc == 3))
                for cc in range(4):
                    cs = slice(cc * 128, cc * 128 + 128)
                    S.op("pe", lambda cs=cs, cc=cc: mm(bank(5)[:, cs], lhsT=g_ke[:, cs], rhs=g_qe[:, cs],
                                                              start=(cc == 0), stop=True),
                         reads=[t_ke, t_qe], writes=[t_ps[5]], inc=(cc == 3))
                gpump(nxt, 1)
                S.op("act", lambda: act.activation(out=g_kl[:].rearrange("p a b -> p (a b)"), in_=bank(4), func=AF.Copy),
                     reads=[t_ps[4]], writes=[t_kl])
                S.op("dve", lambda: dve.tensor_tensor(out=g_scm[:], in0=bank(5), in1=g_mask4[:].rearrange("p a b -> p (a b)"),
                                                      op=ALU.mult),
                     reads=[t_ps[5], t_mask4], writes=[t_scm])
                for cc in range(4):
                    S.op("pe", lambda cc=cc: mm(
                        bank(2 + cc // 2)[:, (cc % 2) * 256:(cc % 2) * 256 + 256], lhsT=g_kl[:, cc, :], rhs=g_v[:, cc, :],
                        start=(cc % 2 == 0), stop=True),
                        reads=[t_kl, tv], writes=[t_ps[2 + cc // 2]])
                gpump(nxt, 1)
                for cc in range(4):
                    cs = slice(cc * 128, cc * 128 + 128)
                    for ec in range(2):
                        es = slice(ec * 128, ec * 128 + 128)
                        S.op("pe", lambda ec=ec, es=es, cs=cs, cc=cc: mm(
                            bank(6 + ec)[:, cs], lhsT=g_Sb[:, cc, es], rhs=g_qe[:, cs], start=(cc == 0), stop=False),
                            reads=[t_Sb[cc], t_qe], writes=[t_ps[6 + ec]], inc=False)
                        S.op("pe", lambda ec=ec, es=es, cs=cs, cc=cc: mm(
                            bank(6 + ec)[:, cs], lhsT=g_v[:, cc, es], rhs=g_scm[:, cs], start=False, stop=True),
                            reads=[tv, t_scm], writes=[t_ps[6 + ec]])
                    S.op("dve", lambda cc=cc: dve.scalar_tensor_tensor(
                        out=g_Sf[:], in0=g_Sf[:], scalar=g_Eq[:, cc * 128 + 127:cc * 128 + 128],
                        in1=bank(2 + cc // 2)[:, (cc % 2) * 256:(cc % 2) * 256 + 256], op0=ALU.mult, op1=ALU.add),
                        reads=[t_Sf, t_Eq, t_ps[2 + cc // 2]], writes=[t_Sf])
                    S.op("pool", lambda cc=cc: pool.tensor_copy(g_Sb[:, (cc + 1) % 4, :], g_Sf[:]),
                         reads=[t_Sf], writes=[t_Sb[(cc + 1) % 4]])
                    gpump(nxt, 1)
                gpump(nxt, 100)
                for ec in range(2):
                    S.op("act", lambda ec=ec: act.activation(out=g_sq[:, ec, :], in_=bank(6 + ec), func=AF.Square),
                         reads=[t_ps[6 + ec]], writes=[t_sq2])
                for ec in range(2):
                    S.op("pe", lambda ec=ec: mm(bank(5), lhsT=ones_b, rhs=g_sq[:, ec, :], start=(ec == 0), stop=(ec == 1)),
                         reads=[t_sq2, t_cst], writes=[t_ps[5]], inc=(ec == 1))
                S.op("act", lambda: act.activation(out=g_r[:], in_=bank(5), func=AF.Ln, scale=1.0 / 256, bias=EPS),
                     reads=[t_ps[5]], writes=[t_r2])
                S.op("act", lambda: act.activation(out=g_r[:], in_=g_r[:], func=AF.Exp, scale=-0.5),
                     reads=[t_r2], writes=[t_r2])
                for ec in range(2):
                    S.op("dve", lambda ec=ec: dve.scalar_tensor_tensor(
                        out=g_tmpc[:], in0=bank(6 + ec), scalar=cols_f[:, 48 + ec:49 + ec], in1=g_r[:],
                        op0=ALU.mult, op1=ALU.mult),
                        reads=[t_ps[6 + ec], t_r2, t_cst], writes=[t_tmpc])
                    S.op("dve", lambda ec=ec: dve.tensor_tensor(out=g_y[:, ec, :], in0=g_tmpc[:], in1=g_silu[:, ec, :], op=ALU.mult),
                         reads=[t_tmpc, tsilu], writes=[t_y])
                S.dma("sp", mg_own[t0 // EB, g * 256:(g + 1) * 256, t0 % EB:t0 % EB + 512].rearrange("(e p) t -> p e t", p=128), g_y[:],
                      reads=[t_y])
        S.barrier()

        NKB = S_LEN // 128
        with ExitStack() as _es2:
            def A2(name, shape, dt):
                return _es2.enter_context(nc.sbuf_tensor(name, shape, dt))
            ws = A2("ws", [128, NCH, 512], BF16)
            s_kT = A2("s_kT", [128, S_LEN], BF16)
            s_v = A2("s_v", [128, NKB, 128], BF16)
            s_qTs = (A2("s_qTa", [128, TQ], BF16), A2("s_qTb", [128, TQ], BF16))
            s_gss = (A2("s_gsa", [128, TQ], BF16), A2("s_gsb", [128, TQ], BF16))
            s_ta = A2("s_ta", [128, 512], F32)
            s_tb = A2("s_tb", [128, 512], F32)
            e1s = (A2("s_e1a", [128, TQ], F32), A2("s_e1b", [128, TQ], F32), A2("s_e1c", [128, TQ], F32))
            sps = (A2("s_spa", [128, TQ], BF16), A2("s_spb", [128, TQ], BF16))
            gs_ = (A2("s_ga", [128, TQ], BF16), A2("s_gb", [128, TQ], BF16))
            ws_ = (A2("s_wa", [128, TQ], BF16), A2("s_wb", [128, TQ], BF16))
            s_y = A2("s_y", [128, TQ], BF16)
            t_ws, t_sy, t_stmp, t_msown = T(), T(), T(), T()
            t_skT = [T() for _ in range(S_LEN // 512)]
            t_sv = [T() for _ in range(S_LEN // 512)]
            t_sqT, t_sgs = (T(), T()), (T(), T())
            t_e1, t_sps, t_gs_, t_ws_ = (T(), T(), T()), (T(), T()), (T(), T()), (T(), T())
            ZB, BB, OB = 0, 2, 4
            blocks = [] if SKIP else [(hd, tb) for hd in range(4) for tb in range(S_LEN // TQ)]

            def inproj_gen(bi):
                hd, tb = blocks[bi]
                s_qT, s_gs = s_qTs[bi % 2], s_gss[bi % 2]
                tq_, tg_ = t_sqT[bi % 2], t_sgs[bi % 2]
                q0 = tb * TQ
                if tb == 0:
                    S.dma("pool", ws[:], wsb[hd].rearrange("(c p) n -> p c n", p=128), writes=[t_ws])
                pending = [None]

                def flush():
                    if pending[0] is not None:
                        pending[0]()
                        pending[0] = None

                for half in range(NBK):
                    t0 = q0 + half * 512
                    tok = slice(t0, t0 + 512)
                    loc = slice(half * 512, half * 512 + 512)
                    hr = h_reads(t0, t0 + 512)
                    for (col0, bk_, evac) in (
                        (0, 6, lambda loc=loc: S.op("dve", lambda: dve.tensor_scalar_mul(
                            out=s_qT[:, loc], in0=bank(6), scalar1=128 ** -0.5), reads=[t_ps[6]], writes=[tq_])),
                        (128, 7, lambda tok=tok, t0=t0: S.op("dve", lambda: dve.tensor_copy(s_kT[:, tok], bank(7)),
                                                             reads=[t_ps[7]], writes=[t_skT[t0 // 512]])),
                        (384, 6, lambda loc=loc: silu_evac(bank(6), s_gs[:, loc], s_ta[:], s_tb[:], t_stmp,
                                                           [t_ps[6]], [tg_])),
                    ):
                        for c in range(NCH):
                            S.op("pe", lambda c=c, col0=col0, bk_=bk_: mm(
                                bank(bk_), lhsT=ws[:, c, col0:col0 + 128], rhs=hT[:, c, tok],
                                start=(c == 0), stop=(c == NCH - 1)),
                                reads=hr + [t_ws], writes=[t_ps[bk_]], inc=(c == NCH - 1))
                            if c % 4 == 3:
                                if c == 3:
                                    flush()
                                yield
                        pending[0] = evac
                    for sb in range(4):
                        kb = (t0 // 128) + sb
                        for c in range(NCH):
                            S.op("pe", lambda c=c, kb=kb, sb=sb: mm(
                                bank(7)[:, sb * 128:sb * 128 + 128], lhsT=hT[:, c, kb * 128:kb * 128 + 128],
                                rhs=ws[:, c, 256:384], start=(c == 0 and sb == 0), stop=(c == NCH - 1)),
                                reads=hr + [t_ws], writes=[t_ps[7]], inc=(c == NCH - 1 and sb == 3))
                            if c % 4 == 3:
                                if c == 3 and sb == 0:
                                    flush()
                                yield
                    pending[0] = (lambda t0=t0: S.op("dve", lambda: dve.tensor_copy(
                        s_v[:, t0 // 128:t0 // 128 + 4, :], bank(7).rearrange("p (a b) -> p a b", a=4)),
                        reads=[t_ps[7]], writes=[t_sv[t0 // 512]]))
                flush()
                yield

            NYIELD = NBK * 28 + 1

            def pump(gen, n):
                if gen is None:
                    return
                for _ in range(n):
                    try:
                        next(gen)
                    except StopIteration:
                        return

            if blocks:
                pump(inproj_gen(0), 10 ** 6)
                for k in range(4):
                    issue_cc(mg_own[k], mg_all[k])
            for bi, (hd, tb) in enumerate(blocks):
                    q0 = tb * TQ
                    s_qT, s_gs = s_qTs[bi % 2], s_gss[bi % 2]
                    tq_, tg_ = t_sqT[bi % 2], t_sgs[bi % 2]
                    nxt = inproj_gen(bi + 1) if bi + 1 < len(blocks) else None
                    kbs = list(range((tb + 1) * (TQ // 128) - 1, -1, -1))
                    P = len(kbs)
                    npump = (NYIELD + P - 1) // P
                    startedB = [False] * NBK
                    startedO = [False] * NBK

                    def geom(p):
                        kb = kbs[p]
                        off = kb * 128 - q0
                        lo = max(0, off)
                        segs = []
                        for bki in range(NBK):
                            c0 = max(lo, bki * 512)
                            c1 = (bki + 1) * 512
                            if c0 < c1:
                                segs.append((bki, c0, c1))
                        return kb, off, lo, segs

                    def emit_Z(p):
                        kb, off, lo, segs = geom(p)
                        for (bki, c0, c1) in segs:
                            S.op("pe", lambda bki=bki, c0=c0, c1=c1, kb=kb: mm(
                                bank(ZB + bki)[:, c0 - bki * 512:c1 - bki * 512],
                                lhsT=s_kT[:, kb * 128:kb * 128 + 128], rhs=s_qT[:, c0:c1], start=True, stop=True),
                                reads=[t_skT[kb // 4], tq_], writes=[t_ps[ZB + bki]])

                    def zb_reads(segs, base):
                        return [t_ps[base + bki] for (bki, _, _) in segs]

                    def emit_E1(p):
                        kb, off, lo, segs = geom(p)
                        e1, te1 = e1s[p % 3], t_e1[p % 3]
                        S.op("act", lambda: act.activation(
                            out=e1[:, lo:TQ], in_=PS[:, ZB * 512 + lo:ZB * 512 + TQ], func=AF.Exp),
                            reads=zb_reads(segs, ZB), writes=[te1])
                        if off >= 0:
                            S.op("dve", lambda: dve.tensor_tensor(
                                out=e1[:, off:off + 128], in0=e1[:, off:off + 128], in1=Ustr_f, op=ALU.mult),
                                reads=[te1, t_cst], writes=[te1])

                    emit_Z(0)
                    emit_E1(0)
                    if P > 1:
                        emit_Z(1)
                    for p in range(P + 1):
                        if p >= 1:
                            kbp, offp, lop, segsp = geom(p - 1)
                            e1p, te1p = e1s[(p - 1) % 3], t_e1[(p - 1) % 3]
                            spp, tspp = sps[(p - 1) % 2], t_sps[(p - 1) % 2]
                            gp, tgp = gs_[(p - 1) % 2], t_gs_[(p - 1) % 2]
                            wp_, twp = ws_[(p - 1) % 2], t_ws_[(p - 1) % 2]
                            S.op("act", lambda gp=gp, lop=lop: act.activation(
                                out=gp[:, lop:TQ], in_=PS[:, BB * 512 + lop:BB * 512 + TQ], func=AF.Exp, scale=-1.0),
                                reads=zb_reads(segsp, BB), writes=[tgp])
                            if p - 1 < P - 1:
                                for (bki, c0, c1) in segsp:
                                    S.op("pe", lambda bki=bki, c0=c0, c1=c1, spp=spp: mm(
                                        bank(BB + bki)[:, c0 - bki * 512:c1 - bki * 512], lhsT=Ustr_b, rhs=spp[:, c0:c1],
                                        start=False, stop=True),
                                        reads=[tspp, t_cst], writes=[t_ps[BB + bki]])
                        if p < P:
                            kb, off, lo, segs = geom(p)
                            e1, te1 = e1s[p % 3], t_e1[p % 3]
                            sp_, tsp = sps[p % 2], t_sps[p % 2]
                            S.op("act", lambda e1=e1, sp_=sp_, lo=lo: act.activation(
                                out=sp_[:, lo:TQ], in_=e1[:, lo:TQ], func=AF.Ln, bias=1.0),
                                reads=[te1], writes=[tsp])
                            for (bki, c0, c1) in segs:
                                S.op("pe", lambda bki=bki, c0=c0, c1=c1, sp_=sp_, st=(not startedB[bki]): mm(
                                    bank(BB + bki)[:, c0 - bki * 512:c1 - bki * 512], lhsT=Lincl_b, rhs=sp_[:, c0:c1],
                                    start=st, stop=True),
                                    reads=[tsp, t_cst], writes=[t_ps[BB + bki]])
                                startedB[bki] = True
                        if p + 1 < P:
                            emit_E1(p + 1)
                        if p + 2 < P:
                            emit_Z(p + 2)
                        pump(nxt, npump // 2)
                        if p >= 1:
                            S.op("dve", lambda wp_=wp_, e1p=e1p, gp=gp, lop=lop: dve.tensor_tensor(
                                out=wp_[:, lop:TQ], in0=e1p[:, lop:TQ], in1=gp[:, lop:TQ], op=ALU.mult),
                                reads=[te1p, tgp], writes=[twp])
                            for (bki, c0, c1) in segsp:
                                S.op("pe", lambda bki=bki, c0=c0, c1=c1, wp_=wp_, kbp=kbp, st=(not startedO[bki]): mm(
                                    bank(OB + bki)[:, c0 - bki * 512:c1 - bki * 512], lhsT=s_v[:, kbp, :], rhs=wp_[:, c0:c1],
                                    start=st, stop=True),
                                    reads=[t_sv[kbp // 4], twp], writes=[t_ps[OB + bki]])
                                startedO[bki] = True
                        pump(nxt, npump - npump // 2)
                    pump(nxt, 10 ** 6)
                    S.op("dve", lambda: dve.tensor_tensor(out=s_y[:], in0=PS[:, OB * 512:OB * 512 + TQ], in1=s_gs[:], op=ALU.mult),
                         reads=[t_ps[OB + i] for i in range(NBK)] + [tg_], writes=[t_sy])
                    S.dma("sp", ms_own[hd, q0 // HS, :, q0 % HS:q0 % HS + TQ], s_y[:], reads=[t_sy], writes=[t_msown])
                    if tb == S_LEN // TQ - 1 and hd < 3:
                        S._wait("pool", S._deps("pool", [t_msown], []))
                        issue_cc(ms_own[hd].rearrange("hh p t -> (hh p) t"), ms_all[hd])
        S.barrier(skip=("cc",))

    BT = 256
    if mode == "AB":
        par = nc.sync.partition_id() % 2
        mg_v = mg_all.rearrange("k (c p) t -> k p c t", p=128)
        ms_v = ms_all.rearrange("h q t -> (h q) t").rearrange("(hr hh p) t -> hh p hr t", hr=8, hh=2)
    else:
        mixin_v = mixin.rearrange("(c p) t -> p c t", p=128)
    with ExitStack() as _es3:
        wo_b = _es3.enter_context(nc.sbuf_tensor("wo_b", [128, NCH, D], BF16))
        wgt_b = _es3.enter_context(nc.sbuf_tensor("wgt_b", [128, NCH, D], BF16))
        wp_b = _es3.enter_context(nc.sbuf_tensor("wp_b", [128, 2, D], BF16))
        mxa = _es3.enter_context(nc.sbuf_tensor("mxa", [128, NCH, BT], BF16))
        xra = _es3.enter_context(nc.sbuf_tensor("xra", [128, NCH, BT], F32))
        pbts = (_es3.enter_context(nc.sbuf_tensor("pbt", [128, 2, BT], BF16)), _es3.enter_context(nc.sbuf_tensor("pbt2", [128, 2, BT], BF16)))
        m1 = _es3.enter_context(nc.sbuf_tensor("m1", [128, NCH, BT], F32))
        sq3 = _es3.enter_context(nc.sbuf_tensor("sq3", [128, NCH, BT], BF16))
        r3 = _es3.enter_context(nc.sbuf_tensor("r3", [128, BT], F32))
        gte = _es3.enter_context(nc.sbuf_tensor("gte", [128, 2, BT], F32))
        oo = _es3.enter_context(nc.sbuf_tensor("oo", [128, 4, BT], F32))
        t_wo = [T() for _ in range(4)]
        t_wgt = [T() for _ in range(4)]
        t_wp = T()
        for j in range(4):
            S.dma("pool", wo_b[:, :, j * 512:(j + 1) * 512],
                  wo[:, j * 512:(j + 1) * 512].rearrange("(c p) n -> p c n", p=128), writes=[t_wo[j]])
        for j in range(4):
            S.dma("pool", wgt_b[:, :, j * 512:(j + 1) * 512],
                  wgt[:, j * 512:(j + 1) * 512].rearrange("(c p) n -> p c n", p=128), writes=[t_wgt[j]])
        S.dma("pool", wp_b[:], wp.rearrange("(c p) n -> p c n", p=128), writes=[t_wp])
        if not SKIP:
            issue_cc(ms_own[3].rearrange("hh p t -> (hh p) t"), ms_all[3])
        mxs, t_mx = (mxa, mxa), (T(),) * 2
        xrs, t_xr = (xra, xra), (T(),) * 2
        h1b = sq3
        t_pb, t_m1, t_sq3, t_r3 = T(), T(), T(), T()
        t_h1b = t_sq3
        t_gte = (T(), T())
        t_oo = [T() for _ in range(4)]
        xTo_v = xTo.rearrange("(c p) t -> p c t", p=128)
        pTo_v = pTo.rearrange("(c p) t -> p c t", p=128)
        outT_v = outT.rearrange("(c p) t -> p c t", p=128)
        rot3 = [0]

        def nb3():
            rot3[0] = (rot3[0] + 1) % 8
            return rot3[0]

        t_m1c = [T() for _ in range(NCH)]
        t_mxp = [T() for _ in range(5)]
        t_sqc = [T() for _ in range(NCH)]
        t_pbs = (T(), T())
        NB3 = HS // BT

        def emit_loads(nb):
            tsl = slice(nb * BT, (nb + 1) * BT)
            if mode == "AB":
                jj = (nb * BT) // EB
                cc0 = (nb * BT) % EB
                S.dma("sp", mxa[:, 0:8, :], mg_v[bass.ds(par * 2 + jj, 1), :, :, cc0:cc0 + BT].rearrange("o p c t -> p (o c) t"),
                      reads=[t_mix], writes=[t_mxp[0]])
                S.dma("sp", mxa[:, 8:16, :],
                      ms_v[bass.ds(par, 1), :, :, nb * BT:(nb + 1) * BT].rearrange("o p r t -> p (o r) t"),
                      reads=[t_mix], writes=[t_mxp[1]])
            else:
                S.dma("sp", mxa[:], mixin_v[:, :, tsl], writes=[t_mx[0]])
            S.dma("sp", xra[:], xTo_v[:, :, tsl], writes=[t_xr[0]])
            S.dma("pool", pbts[nb % 2][:], pTo_v[:, :, tsl], writes=[t_pbs[nb % 2]])

        for nb in range(NB3):
            emit_loads(nb)
            tsl = slice(nb * BT, (nb + 1) * BT)
            mx, tmx = mxa, t_mx[0]
            xr, txr = xra, t_xr[0]
            pbt, t_pb = pbts[nb % 2], t_pbs[nb % 2]
            for oc in range(NCH):
                bk = nb3()
                for c in range(NCH):
                    S.op("pe", lambda c=c, oc=oc, bk=bk: mm(
                        bank(bk)[:, 0:BT], lhsT=wo_b[:, c, oc * 128:(oc + 1) * 128], rhs=mx[:, c, :],
                        start=(c == 0), stop=(c == NCH - 1)),
                        reads=[t_wo[oc // 4], tmx] + t_mxp, writes=[t_ps[bk]], inc=(c == NCH - 1))
                S.op("act", lambda oc=oc, bk=bk: act.activation(out=sq3[:, oc, :], in_=bank(bk)[:, 0:BT], func=AF.Square),
                     reads=[t_ps[bk]], writes=[t_sqc[oc]])
                S.op("dve", lambda oc=oc, bk=bk: dve.tensor_scalar_mul(
                    out=m1[:, oc, :], in0=bank(bk)[:, 0:BT], scalar1=cols_f[:, 16 + oc:17 + oc]),
                    reads=[t_ps[bk], t_cst], writes=[t_m1c[oc]])
            bk = nb3()
            for c in range(NCH):
                S.op("pe", lambda c=c, bk=bk: mm(bank(bk)[:, 0:BT], lhsT=ones_b, rhs=sq3[:, c, :],
                                                       start=(c == 0), stop=(c == NCH - 1)),
                     reads=[t_sqc[c], t_cst], writes=[t_ps[bk]], inc=(c == NCH - 1))
            S.op("act", lambda bk=bk: act.activation(out=r3[:], in_=bank(bk)[:, 0:BT], func=AF.Ln, scale=1.0 / D, bias=EPS),
                 reads=[t_ps[bk]], writes=[t_r3])
            S.op("act", lambda: act.activation(out=r3[:], in_=r3[:], func=AF.Exp, scale=-0.5),
                 reads=[t_r3], writes=[t_r3])
            for oc in range(NCH):
                S.op("dve", lambda oc=oc: dve.tensor_tensor(out=m1[:, oc, :], in0=m1[:, oc, :], in1=r3[:], op=ALU.mult),
                     reads=[t_m1c[oc], t_r3], writes=[t_m1c[oc]])
                aeng = "pool" if oc % 2 == 0 else "dve"
                S.op(aeng, lambda oc=oc, aeng=aeng: S.e[aeng].tensor_tensor(
                    out=m1[:, oc, :], in0=m1[:, oc, :], in1=xr[:, oc, :], op=ALU.add),
                    reads=[t_m1c[oc], txr], writes=[t_m1c[oc]])
                S.op("act", lambda oc=oc: act.activation(out=h1b[:, oc, :], in_=m1[:, oc, :], func=AF.Copy),
                     reads=[t_m1c[oc]], writes=[t_sqc[oc]])
            for oc in range(NCH):
                bk = nb3()
                for c in range(NCH):
                    S.op("pe", lambda c=c, oc=oc, bk=bk: mm(
                        bank(bk)[:, 0:BT], lhsT=wgt_b[:, c, oc * 128:(oc + 1) * 128], rhs=h1b[:, c, :],
                        start=(c == 0), stop=(c == NCH - 1)),
                        reads=[t_wgt[oc // 4], t_sqc[c]], writes=[t_ps[bk]], inc=(c == NCH - 1))
                gt, tgt = gte[:, oc % 2, :], t_gte[oc % 2]
                S.op("act", lambda oc=oc, bk=bk, gt=gt: act.activation(
                    out=gt, in_=bank(bk)[:, 0:BT], func=AF.Sigmoid, bias=cols_f[:, 32 + oc:33 + oc]),
                    reads=[t_ps[bk], t_cst], writes=[tgt])
                bk2 = nb3()
                for c in range(2):
                    S.op("pe", lambda c=c, oc=oc, bk2=bk2: mm(
                        bank(bk2)[:, 0:BT], lhsT=wp_b[:, c, oc * 128:(oc + 1) * 128], rhs=pbt[:, c, :],
                        start=(c == 0), stop=(c == 1)),
                        reads=[t_wp, t_pb], writes=[t_ps[bk2]], inc=(c == 1))
                S.op("dve", lambda oc=oc, bk2=bk2, gt=gt: dve.tensor_tensor(
                    out=gt, in0=gt, in1=bank(bk2)[:, 0:BT], op=ALU.mult),
                    reads=[tgt, t_ps[bk2]], writes=[tgt])
                S.op("dve", lambda oc=oc, gt=gt: dve.tensor_tensor(
                    out=oo[:, oc % 4, :], in0=gt, in1=m1[:, oc, :], op=ALU.add),
                    reads=[tgt, t_m1c[oc]], writes=[t_oo[oc % 4]])
                if oc % 4 == 3:
                    S.dma("pool", outT_v[:, oc - 3:oc + 1, tsl], oo[:], reads=t_oo)
        S.barrier()
    return nc


_PROG = {}


def kernel(x, p, g_pre, w_in, w_a2, b_a, g_gla_head, w_out, g_post, w_ple_gate, b_ple_gate, w_ple_proj):
    x = np.asarray(x, np.float32)
    B, S_LEN, _ = x.shape
    HS = S_LEN // 2
    f = lambda a: np.ascontiguousarray(np.asarray(a, np.float32))
    p, g_pre, w_in, w_a2, b_a = f(p), f(g_pre), f(w_in), f(w_a2), f(b_a)
    g_gla_head, w_out, g_post = f(g_gla_head), f(w_out), f(g_post)
    w_ple_gate, b_ple_gate, w_ple_proj = f(w_ple_gate), f(b_ple_gate), f(w_ple_proj)
    W = w_in[0]
    GQ, GK, GV, GG, LR, SQ, SK, SV, SG = 0, 512, 1024, 2048, 3072, 3088, 4112, 5136, 6160

    def colv(v):
        return v.reshape(-1, 128).T

    ii = np.arange(128)
    cst = np.zeros((128, 5, 128), np.float32)
    cst[:, 0, :] = (ii[:, None] == ii[None, :])
    cst[:, 1, :] = (ii[:, None] <= ii[None, :])
    cst[:, 2, :] = (ii[:, None] < ii[None, :])
    cst[:, 3, :] = (ii[:, None] >= ii[None, :])
    cst[:, 4, :] = 1.0
    cols = np.zeros((128, 64), np.float32)
    cols[:, 0:16] = colv(g_pre[0])
    cols[:, 16:32] = colv(g_post[0])
    cols[:, 32:48] = colv(b_ple_gate[0])
    cols[:, 48:50] = colv(g_gla_head[0])
    in_maps = []
    for core in range(8):
        b, hh = core // 2, core % 2
        wsb = np.stack([np.concatenate([W[:, SQ + 128 * H:SQ + 128 * H + 128], W[:, SK + 128 * H:SK + 128 * H + 128],
                                        W[:, SV + 128 * H:SV + 128 * H + 128], W[:, SG + 128 * H:SG + 128 * H + 128]], axis=1)
                        for H in range(4 * hh, 4 * hh + 4)])
        wgla = np.stack([np.concatenate([W[:, GQ + 128 * G:GQ + 128 * G + 128], W[:, GK + 128 * G:GK + 128 * G + 128],
                                         W[:, GV + 256 * G:GV + 256 * G + 256], W[:, GG + 256 * G:GG + 256 * G + 256]], axis=1)
                         for G in range(2 * hh, 2 * hh + 2)])
        xTb = np.ascontiguousarray(x[b].T)
        wo_perm = np.concatenate([w_out[0][0:1024]] + [w_out[0][1024 + (4 * r + h) * 128:1024 + (4 * r + h) * 128 + 128]
                                                       for h in range(4) for r in range(2)], axis=0)
        in_maps.append({
            "xT": xTb,
            "xTo": np.ascontiguousarray(xTb[:, hh * HS:(hh + 1) * HS]),
            "pTo": np.ascontiguousarray(p[0, b, hh * HS:(hh + 1) * HS].T),
            "wsb": np.ascontiguousarray(wsb),
            "wgla": np.ascontiguousarray(wgla),
            "wlr": np.ascontiguousarray(W[:, LR:LR + 16]),
            "wa2": np.ascontiguousarray(w_a2[0][:, 256 * hh:256 * hh + 256]),
            "ba": np.ascontiguousarray(b_a[0][None, 256 * hh:256 * hh + 256]),
            "cols": cols,
            "wo": np.ascontiguousarray(wo_perm),
            "wgt": w_ple_gate[0],
            "wp": w_ple_proj[0],
            "cst": cst,
        })
    if S_LEN not in _PROG:
        _PROG[S_LEN] = build_program(S_LEN, "AB")
    res = run_bass_kernel_spmd(_PROG[S_LEN], in_maps, core_ids=list(range(8)))
    out = np.empty((B, S_LEN, D), np.float32)
    for core in range(8):
        b, hh = core // 2, core % 2
        out[b, hh * HS:(hh + 1) * HS, :] = res.results[core]["outT"].T
    return out
```

```python
from contextlib import ExitStack
import numpy as np
import concourse.bass as bass
import concourse.mybir as mybir
from concourse.bass_utils import run_bass_kernel_spmd

F32 = mybir.dt.float32
BF16 = mybir.dt.bfloat16
AF = mybir.ActivationFunctionType
ALU = mybir.AluOpType

D = 2048
NCH = 16
EPS = 1e-6
TQ = 1024
NBK = TQ // 512


class T:
    __slots__ = ("w", "r", "x")

    def __init__(self, x=False):
        self.w = None
        self.r = []
        self.x = x


class Sched:
    ENG = ("pe", "act", "dve", "pool", "sp")

    def __init__(self, nc, n_dma_sems=28):
        self.nc = nc
        self.e = dict(pe=nc.tensor, act=nc.scalar, dve=nc.vector, pool=nc.gpsimd, sp=nc.sync)
        self.sems = {}
        self.cnt = {}
        for k in self.ENG:
            self.sems[k] = nc.alloc_semaphore("s_" + k)
            self.cnt[k] = 0
        self.dma_keys = []
        for i in range(n_dma_sems):
            k = "d%d" % i
            self.sems[k] = nc.alloc_semaphore("s_" + k)
            self.cnt[k] = 0
            self.dma_keys.append(k)
        self.sems["cc"] = nc.alloc_semaphore("s_cc")
        self.cnt["cc"] = 0
        self.dma_rr = 0
        self.seen = {k: {} for k in self.ENG}

    def _deps(self, eng, reads, writes):
        need = {}

        def add(d, same_ok):
            if d is None:
                return
            k, v = d
            if k == eng and same_ok:
                return
            if need.get(k, 0) < v:
                need[k] = v
        for t in reads:
            add(t.w, False)
            if t.x:
                for d in t.r:
                    add(d, True)
        for t in writes:
            add(t.w, True)
            for d in t.r:
                add(d, False)
        return need

    def _wait(self, eng, need):
        seen = self.seen[eng]
        for k, v in need.items():
            if seen.get(k, 0) < v:
                self.e[eng].wait_ge(self.sems[k], v)
                seen[k] = v

    def _record(self, d, reads, writes):
        for t in reads:
            t.r.append(d)
        for t in writes:
            t.w = d
            t.r = []

    def op(self, eng, fn, reads=(), writes=(), inc=True):
        self._wait(eng, self._deps(eng, reads, writes))
        ins = fn()
        if inc:
            self.cnt[eng] += 1
            ins.then_inc(self.sems[eng], 1)
            seq = self.cnt[eng]
        else:
            seq = self.cnt[eng] + 1
        self._record((eng, seq), reads, writes)
        return ins

    def dma(self, eng, out, in_, reads=(), writes=()):
        self._wait(eng, self._deps(eng, reads, writes))
        k = self.dma_keys[self.dma_rr]
        self.dma_rr = (self.dma_rr + 1) % len(self.dma_keys)
        ins = self.e[eng].dma_start(out=out, in_=in_)
        self.cnt[k] += 16
        ins.then_inc(self.sems[k], 16)
        self._record((k, self.cnt[k]), reads, writes)
        return ins

    def barrier(self, skip=()):
        for eng in self.ENG:
            need = {k: v for k, v in self.cnt.items() if v > 0 and k not in skip}
            self._wait(eng, need)


def build_program(S_LEN, mode="AB"):
    STOP = 9
    P3 = 9
    HS = S_LEN // 2
    nc = bass.Bass("TRN2", target_bir_lowering=False)
    S = Sched(nc)
    pe, act, dve, pool = nc.tensor, nc.scalar, nc.vector, nc.gpsimd

    def mm(out, **kw):
        return pe.matmul(out, skip_group_check=True, **kw)

    def din(name, shape, dt=F32):
        return nc.dram_tensor(name, shape, dt, kind="ExternalInput").ap()

    xT = din("xT", [S_LEN // 256, 128, NCH * 256])
    xTo = din("xTo", [HS // 256, 128, NCH * 256])
    pTo = din("pTo", [256, HS])
    wsb = din("wsb", [4, D, 512])
    wgla = din("wgla", [2, D, 768])
    wlr = din("wlr", [D, 16])
    wa2 = din("wa2", [16, 256])
    ba = din("ba", [1, 256])
    cols = din("cols", [128, 64])
    wo = din("wo", [D, D])
    wgt = din("wgt", [D, D])
    wp = din("wp", [256, D])
    cst = din("cst", [128, 5, 128])
    outT = nc.dram_tensor("outT", [D, HS], F32, kind="ExternalOutput").ap()
    if mode == "A":
        mix_own = nc.dram_tensor("mix_own", [4, 1024, S_LEN // 4], BF16, kind="ExternalOutput").ap()
    else:
        mix_own = nc.dram_tensor("mix_own", [4, 1024, S_LEN // 4], BF16).ap()
    mix_all = nc.dram_tensor("mix_all", [4, 2048, S_LEN // 4], BF16).ap()
    mg_own = nc.dram_tensor("mg_own", [4, 512, S_LEN // 4], BF16).ap()
    mg_all = nc.dram_tensor("mg_all", [4, 1024, S_LEN // 4], BF16).ap()
    ms_own = nc.dram_tensor("ms_own", [4, 2, 128, HS], BF16).ap()
    ms_all = nc.dram_tensor("ms_all", [4, 512, HS], BF16).ap()
    GROUPS = [[0, 1], [2, 3], [4, 5], [6, 7]]
    t_mix = T()

    def issue_cc(src, dst):
        pool.collective_compute("AllGather", ALU.bypass, replica_groups=GROUPS, ins=[src], outs=[dst]
                                ).then_inc(S.sems["cc"], 1)
        S.cnt["cc"] += 1
        t_mix.w = ("cc", S.cnt["cc"])
    EB = S_LEN // 4
    if mode == "B":
        mixin = din("mixin", [2048, HS], BF16)
    SKIP = (mode == "B")

    cst_f = nc.alloc_sbuf_tensor("cst_f", [128, 5, 128], F32)
    cst_b = nc.alloc_sbuf_tensor("cst_b", [128, 5, 128], BF16)
    cols_f = nc.alloc_sbuf_tensor("cols_f", [128, 64], F32)
    wa2_f = nc.alloc_sbuf_tensor("wa2_f", [16, 256], F32)
    ba_f = nc.alloc_sbuf_tensor("ba_f", [1, 256], F32)
    t_cst = T()
    S.dma("sp", cst_f[:], cst[:, :, :], writes=[t_cst])
    S.dma("sp", cols_f[:], cols[:, :], writes=[t_cst])
    S.dma("sp", wa2_f[:], wa2[:, :], writes=[t_cst])
    S.dma("sp", ba_f[:], ba[:, :], writes=[t_cst])
    S.op("dve", lambda: dve.tensor_copy(cst_b[:], cst_f[:]), reads=[t_cst], writes=[t_cst])
    ident_b = cst_b[:, 0, :]
    Uincl_f = cst_f[:, 1, :]
    Ustr_f = cst_f[:, 2, :]
    Ustr_b = cst_b[:, 2, :]
    Lincl_b = cst_b[:, 3, :]
    ones_b = cst_b[:, 4, :]
    ones_f = cst_f[:, 4, :]

    PS = nc.alloc_psum_tensor("ps", [128, 4096], F32)
    t_ps = [T(True) for _ in range(8)]

    def bank(i):
        return PS[:, i * 512:(i + 1) * 512]

    with nc.sbuf_tensor("hT", [128, NCH, S_LEN], BF16) as hT:
        NB0 = S_LEN // 256
        NLOOP0 = 0 if SKIP else NB0
        t_hT = [T() for _ in range(NB0)]
        t_hTp = [T() for _ in range(NB0)]

        def h_reads(t0, t1):
            rng = range(t0 // 256, (t1 + 255) // 256)
            return [t_hT[i] for i in rng] + [t_hTp[i] for i in rng]

        with ExitStack() as _es0:
            xb0 = _es0.enter_context(nc.sbuf_tensor("xb0", [128, NCH, 256], F32))
            xb1 = _es0.enter_context(nc.sbuf_tensor("xb1", [128, NCH, 256], F32))
            sq0 = _es0.enter_context(nc.sbuf_tensor("sq0", [128, NCH, 256], BF16))
            r0a = _es0.enter_context(nc.sbuf_tensor("r0", [128, 256], F32))
            r0b = _es0.enter_context(nc.sbuf_tensor("r1", [128, 256], F32))
            p0tmp = (_es0.enter_context(nc.sbuf_tensor("p0ta", [128, 256], F32)), _es0.enter_context(nc.sbuf_tensor("p0tb", [128, 256], F32)))
            t_p0tmp = (T(), T())
            xbs = (xb0, xb1)
            t_xb = (T(), T())
            t_sq = T()
            rs = (r0a, r0b)
            t_r = (T(), T())
            for nb in range(NLOOP0):
                xb = xbs[nb % 2]
                txb = t_xb[nb % 2]
                r = rs[nb % 2]
                tr = t_r[nb % 2]
                tsl = slice(nb * 256, (nb + 1) * 256)
                S.dma("sp", xb[:], xT[nb].rearrange("p (c t) -> p c t", c=NCH), writes=[txb])
                S.op("act", lambda xb=xb: act.activation(out=sq0[:], in_=xb[:], func=AF.Square),
                     reads=[txb], writes=[t_sq])
                bk = nb % 2
                for c in range(NCH):
                    S.op("pe", lambda c=c, bk=bk: mm(bank(bk)[:, 0:256], lhsT=ones_b, rhs=sq0[:, c, :],
                                                           start=(c == 0), stop=(c == NCH - 1)),
                         reads=[t_sq, t_cst], writes=[t_ps[bk]], inc=(c == NCH - 1))
                S.op("act", lambda r=r, bk=bk: act.activation(out=r[:], in_=bank(bk)[:, 0:256], func=AF.Ln,
                                                               scale=1.0 / D, bias=EPS),
                     reads=[t_ps[bk]], writes=[tr])
                S.op("act", lambda r=r: act.activation(out=r[:], in_=r[:], func=AF.Exp, scale=-0.5),
                     reads=[tr], writes=[tr])
                for c in range(NCH):
                    if c % 3 == 2:
                        k = (c // 3) % 2
                        S.op("act", lambda c=c, xb=xb, k=k: act.activation(
                            out=p0tmp[k][:], in_=xb[:, c, :], func=AF.Identity, scale=cols_f[:, c:c + 1]),
                            reads=[txb, t_cst], writes=[t_p0tmp[k]])
                        S.op("pool", lambda c=c, r=r, k=k: pool.tensor_tensor(
                            out=hT[:, c, tsl], in0=p0tmp[k][:], in1=r[:], op=ALU.mult),
                            reads=[t_p0tmp[k], tr], writes=[t_hTp[nb]])
                        continue
                    S.op("dve", lambda c=c, xb=xb, r=r: dve.scalar_tensor_tensor(
                        out=hT[:, c, tsl], in0=xb[:, c, :], scalar=cols_f[:, c:c + 1], in1=r[:],
                        op0=ALU.mult, op1=ALU.mult),
                        reads=[txb, tr, t_cst], writes=[t_hT[nb]])
        S.barrier()
        if STOP <= 0:
            return nc

        def silu_evac(src_ap, dst_ap, tmp_a, tmp_b, t_tmp, reads, writes):
            S.op("act", lambda: act.activation(out=tmp_a, in_=src_ap, func=AF.Exp, scale=-1.0),
                 reads=reads, writes=[t_tmp])
            S.op("dve", lambda: dve.tensor_scalar_add(out=tmp_a, in0=tmp_a, scalar1=1.0),
                 reads=[t_tmp], writes=[t_tmp])
            S.op("dve", lambda: dve.reciprocal(out=tmp_b, in_=tmp_a), reads=[t_tmp], writes=[t_tmp])
            S.op("dve", lambda: dve.tensor_tensor(out=dst_ap, in0=src_ap, in1=tmp_b, op=ALU.mult),
                 reads=list(reads) + [t_tmp], writes=writes)

        with ExitStack() as _es1:
            def A1(name, shape, dt):
                return _es1.enter_context(nc.sbuf_tensor(name, shape, dt))
            wg = A1("wg", [128, NCH, 768], BF16)
            wlr_b = A1("wlr_b", [128, NCH, 16], BF16)
            g_qTs = (A1("g_qTa", [128, 512], BF16), A1("g_qTb", [128, 512], BF16))
            g_kTs = (A1("g_kTa", [128, 512], BF16), A1("g_kTb", [128, 512], BF16))
            g_silus = (A1("g_silua", [128, 2, 512], BF16), A1("g_silub", [128, 2, 512], BF16))
            g_vs = (A1("g_va", [128, 4, 256], BF16), A1("g_vb", [128, 4, 256], BF16))
            g_lr1 = A1("g_lra", [16, 512], F32)
            g_lrs = (g_lr1, g_lr1)
            g_tmpa = A1("g_tmpa", [128, 512], F32)
            g_tmpb = A1("g_tmpb", [128, 512], F32)
            g_mask4 = A1("g_mask4", [128, 4, 128], F32)
            g_e = A1("g_e", [128, 512], F32)
            g_tmpc = g_e
            g_sp = A1("g_sp", [128, 4, 128], F32)
            g_Eq = A1("g_Eq", [128, 512], F32)
            g_Ek = A1("g_Ek", [128, 512], F32)
            g_qe = A1("g_qe", [128, 512], BF16)
            g_ke = A1("g_ke", [128, 512], BF16)
            g_klT = A1("g_klT", [128, 512], BF16)
            g_kl = A1("g_kl", [128, 4, 128], BF16)
            g_scm = A1("g_scm", [128, 512], BF16)
            g_Sf = A1("g_Sf", [128, 256], F32)
            g_Sb = A1("g_Sb", [128, 4, 256], BF16)
            g_sq = A1("g_sq", [128, 2, 512], BF16)
            g_r = A1("g_r", [128, 512], F32)
            g_y = A1("g_y", [128, 2, 512], BF16)
            t_wg, t_wlr, t_mask4 = T(), T(), T()
            t_qT, t_kT, t_silu, t_v = ((T(), T()) for _ in range(4))
            t_lr = (T(),) * 2
            t_tmp = T()
            t_e, t_sp, t_Eq, t_Ek, t_qe, t_ke, t_klT, t_kl, t_scm = (T() for _ in range(9))
            t_Sf, t_sq2, t_r2, t_y = T(), T(), T(), T()
            t_tmpc = t_e
            t_Sb = [T() for _ in range(4)]
            S.dma("pool", wlr_b[:], wlr.rearrange("(c p) n -> p c n", p=128), writes=[t_wlr])
            for cc in range(4):
                S.op("pool", lambda cc=cc: pool.tensor_copy(g_mask4[:, cc, :], Uincl_f), reads=[t_cst], writes=[t_mask4])
            gblocks = [] if SKIP else [(g, nb) for g in range(2) for nb in range(S_LEN // 512)]
            rot = [0]

            def nextbank():
                rot[0] ^= 1
                return rot[0]

            def gla_inproj_gen(bi):
                g, nb = gblocks[bi]
                par = bi % 2
                t0 = nb * 512
                tok = slice(t0, t0 + 512)
                hr = h_reads(t0, t0 + 512)
                if nb == 0:
                    S.dma("pool", wg[:], wgla[g].rearrange("(c p) n -> p c n", p=128), writes=[t_wg])
                pending = [None]

                def flush():
                    if pending[0] is not None:
                        pending[0]()
                        pending[0] = None

                def group(lhs_fn, rhs_fn, out_fn, evac, extra_reads):
                    bk = nextbank()
                    for c in range(NCH):
                        S.op("pe", lambda c=c, bk=bk: mm(out_fn(bk), lhsT=lhs_fn(c), rhs=rhs_fn(c),
                                                               start=(c == 0), stop=(c == NCH - 1)),
                             reads=hr + extra_reads, writes=[t_ps[bk]], inc=(c == NCH - 1))
                        if c == 3:
                            flush()
                    pending[0] = lambda bk=bk: evac(bk)

                group(lambda c: wg[:, c, 0:128], lambda c: hT[:, c, tok], lambda bk: bank(bk),
                      lambda bk: S.op("act", lambda: act.activation(out=g_qTs[par][:], in_=bank(bk), func=AF.Identity,
                                                                    scale=128 ** -0.5),
                                      reads=[t_ps[bk]], writes=[t_qT[par]]), [t_wg])
                yield
                group(lambda c: wg[:, c, 128:256], lambda c: hT[:, c, tok], lambda bk: bank(bk),
                      lambda bk: S.op("dve", lambda: dve.tensor_copy(g_kTs[par][:], bank(bk)),
                                      reads=[t_ps[bk]], writes=[t_kT[par]]), [t_wg])
                yield
                for ec in range(2):
                    group(lambda c, ec=ec: wg[:, c, 512 + ec * 128:512 + ec * 128 + 128], lambda c: hT[:, c, tok],
                          lambda bk: bank(bk),
                          lambda bk, ec=ec: silu_evac(bank(bk), g_silus[par][:, ec, :], g_tmpa[:], g_tmpb[:], t_tmp,
                                                      [t_ps[bk]], [t_silu[par]]), [t_wg])
                    yield
                for sb in range(4):
                    group(lambda c, sb=sb: hT[:, c, t0 + sb * 128:t0 + sb * 128 + 128], lambda c: wg[:, c, 256:512],
                          lambda bk: bank(bk)[:, 0:256],
                          lambda bk, sb=sb: S.op("dve", lambda: dve.tensor_copy(g_vs[par][:, sb, :], bank(bk)[:, 0:256]),
                                                 reads=[t_ps[bk]], writes=[t_v[par]]), [t_wg])
                    yield
                group(lambda c: wlr_b[:, c, :], lambda c: hT[:, c, tok], lambda bk: bank(bk)[0:16, :],
                      lambda bk: S.op("dve", lambda: dve.tensor_copy(g_lrs[par][:], bank(bk)[0:16, :]),
                                      reads=[t_ps[bk]], writes=[t_lr[par]]), [t_wlr])
                flush()
                yield

            def gpump(gen, n):
                if gen is None:
                    return
                for _ in range(n):
                    try:
                        next(gen)
                    except StopIteration:
                        return

            if gblocks:
                gpump(gla_inproj_gen(0), 100)
            for bi, (g, nb) in enumerate(gblocks):
                par = bi % 2
                g_qT, g_kT, g_silu, g_v, g_lr = g_qTs[par], g_kTs[par], g_silus[par], g_vs[par], g_lrs[par]
                tqT, tkT, tsilu, tv, tlr = t_qT[par], t_kT[par], t_silu[par], t_v[par], t_lr[par]
                nxt = gla_inproj_gen(bi + 1) if bi + 1 < len(gblocks) else None
                t0 = nb * 512
                gsl = slice(g * 128, g * 128 + 128)
                if nb == 0:
                    S.op("dve", lambda: dve.memset(g_Sf[:], 0.0), writes=[t_Sf])
                    S.op("dve", lambda: dve.memset(g_Sb[:, 0, :], 0.0), writes=[t_Sb[0]])
                for cc in range(4):
                    cs = slice(cc * 128, cc * 128 + 128)
                    S.op("pe", lambda cs=cs, cc=cc: mm(
                        bank(2)[:, cs], lhsT=g_lr[0:16, cs], rhs=wa2_f[0:16, gsl], start=(cc == 0), stop=False),
                        reads=[tlr, t_cst], writes=[t_ps[2]], inc=False)
                    S.op("pe", lambda cs=cs, cc=cc: mm(
                        bank(2)[:, cs], lhsT=ones_f[0:1, :], rhs=ba_f[0:1, gsl], start=False, stop=True),
                        reads=[t_cst], writes=[t_ps[2]], inc=(cc == 3))
                gpump(nxt, 1)
                S.op("act", lambda: act.activation(out=g_e[:], in_=bank(2), func=AF.Exp, scale=-1.0),
                     reads=[t_ps[2]], writes=[t_e])
                S.op("act", lambda: act.activation(out=g_sp[:].rearrange("p a b -> p (a b)"), in_=g_e[:], func=AF.Ln, bias=1.0),
                     reads=[t_e], writes=[t_sp])
                for cc in range(4):
                    cs = slice(cc * 128, cc * 128 + 128)
                    S.op("pe", lambda cs=cs, cc=cc: mm(bank(3)[:, cs], lhsT=g_sp[:, cc, :], rhs=Uincl_f,
                                                              start=(cc == 0), stop=True),
                         reads=[t_sp, t_cst], writes=[t_ps[3]], inc=(cc == 3))
                gpump(nxt, 1)
                S.op("act", lambda: act.activation(out=g_Eq[:], in_=bank(3), func=AF.Exp, scale=-1.0 / 16),
                     reads=[t_ps[3]], writes=[t_Eq])
                S.op("act", lambda: act.activation(out=g_Ek[:], in_=bank(3), func=AF.Exp, scale=1.0 / 16),
                     reads=[t_ps[3]], writes=[t_Ek])
                S.op("dve", lambda: dve.tensor_tensor(out=g_ke[:], in0=g_kT[:], in1=g_Ek[:], op=ALU.mult),
                     reads=[tkT, t_Ek], writes=[t_ke])
                for cc in range(4):
                    cs = slice(cc * 128, cc * 128 + 128)
                    S.op("dve", lambda cs=cs, cc=cc: dve.scalar_tensor_tensor(
                        out=g_klT[:, cs], in0=g_kT[:, cs], scalar=g_Eq[:, cc * 128 + 127:cc * 128 + 128], in1=g_Ek[:, cs],
                        op0=ALU.mult, op1=ALU.mult),
                        reads=[tkT, t_Eq, t_Ek], writes=[t_klT])
                S.op("dve", lambda: dve.tensor_tensor(out=g_qe[:], in0=g_qT[:], in1=g_Eq[:], op=ALU.mult),
                     reads=[tqT, t_Eq], writes=[t_qe])
                for cc in range(4):
                    cs = slice(cc * 128, cc * 128 + 128)
                    S.op("pe", lambda cs=cs, cc=cc: mm(bank(4)[:, cs], lhsT=g_klT[:, cs], rhs=ident_b,
                                                              start=(cc == 0), stop=True),
                         reads=[t_klT, t_cst], writes=[t_ps[4]], inc=(cc == 3))
                for cc in range(4):
                    cs = slice(cc * 128, cc * 128 + 128)
                    S.op("pe", lambda cs=cs, cc=cc: mm(bank(5)[:, cs], lhsT=g_ke[:, cs], rhs=g_qe[:, cs],
                                                              start=(cc == 0), stop=True),
                         reads=[t_ke, t_qe], writes=[t_ps[5]], inc=(cc == 3))
                gpump(nxt, 1)
                S.op("act", lambda: act.activation(out=g_kl[:].rearrange("p a b -> p (a b)"), in_=bank(4), func=AF.Copy),
                     reads=[t_ps[4]], writes=[t_kl])
                S.op("dve", lambda: dve.tensor_tensor(out=g_scm[:], in0=bank(5), in1=g_mask4[:].rearrange("p a b -> p (a b)"),
                                                      op=ALU.mult),
                     reads=[t_ps[5], t_mask4], writes=[t_scm])
                for cc in range(4):
                    S.op("pe", lambda cc=cc: mm(
                        bank(2 + cc // 2)[:, (cc % 2) * 256:(cc % 2) * 256 + 256], lhsT=g_kl[:, cc, :], rhs=g_v[:, cc, :],
                        start=(cc % 2 == 0), stop=True),
                        reads=[t_kl, tv], writes=[t_ps[2 + cc // 2]])
                gpump(nxt, 1)
                for cc in range(4):
                    cs = slice(cc * 128, cc * 128 + 128)
                    for ec in range(2):
                        es = slice(ec * 128, ec * 128 + 128)
                        S.op("pe", lambda ec=ec, es=es, cs=cs, cc=cc: mm(
                            bank(6 + ec)[:, cs], lhsT=g_Sb[:, cc, es], rhs=g_qe[:, cs], start=(cc == 0), stop=False),
                            reads=[t_Sb[cc], t_qe], writes=[t_ps[6 + ec]], inc=False)
                        S.op("pe", lambda ec=ec, es=es, cs=cs, cc=cc: mm(
                            bank(6 + ec)[:, cs], lhsT=g_v[:, cc, es], rhs=g_scm[:, cs], start=False, stop=True),
                            reads=[tv, t_scm], writes=[t_ps[6 + ec]])
                    S.op("dve", lambda cc=cc: dve.scalar_tensor_tensor(
                        out=g_Sf[:], in0=g_Sf[:], scalar=g_Eq[:, cc * 128 + 127:cc * 128 + 128],
                        in1=bank(2 + cc // 2)[:, (cc % 2) * 256:(cc % 2) * 256 + 256], op0=ALU.mult, op1=ALU.add),
                        reads=[t_Sf, t_Eq, t_ps[2 + cc // 2]], writes=[t_Sf])
                    S.op("pool", lambda cc=cc: pool.tensor_copy(g_Sb[:, (cc + 1) % 4, :], g_Sf[:]),
                         reads=[t_Sf], writes=[t_Sb[(cc + 1) % 4]])
                    gpump(nxt, 1)
                gpump(nxt, 100)
                for ec in range(2):
                    S.op("act", lambda ec=ec: act.activation(out=g_sq[:, ec, :], in_=bank(6 + ec), func=AF.Square),
                         reads=[t_ps[6 + ec]], writes=[t_sq2])
                for ec in range(2):
                    S.op("pe", lambda ec=ec: mm(bank(5), lhsT=ones_b, rhs=g_sq[:, ec, :], start=(ec == 0), stop=(ec == 1)),
                         reads=[t_sq2, t_cst], writes=[t_ps[5]], inc=(ec == 1))
                S.op("act", lambda: act.activation(out=g_r[:], in_=bank(5), func=AF.Ln, scale=1.0 / 256, bias=EPS),
                     reads=[t_ps[5]], writes=[t_r2])
                S.op("act", lambda: act.activation(out=g_r[:], in_=g_r[:], func=AF.Exp, scale=-0.5),
                     reads=[t_r2], writes=[t_r2])
                for ec in range(2):
                    S.op("dve", lambda ec=ec: dve.scalar_tensor_tensor(
                        out=g_tmpc[:], in0=bank(6 + ec), scalar=cols_f[:, 48 + ec:49 + ec], in1=g_r[:],
                        op0=ALU.mult, op1=ALU.mult),
                        reads=[t_ps[6 + ec], t_r2, t_cst], writes=[t_tmpc])
                    S.op("dve", lambda ec=ec: dve.tensor_tensor(out=g_y[:, ec, :], in0=g_tmpc[:], in1=g_silu[:, ec, :], op=ALU.mult),
                         reads=[t_tmpc, tsilu], writes=[t_y])
                S.dma("sp", mg_own[t0 // EB, g * 256:(g + 1) * 256, t0 % EB:t0 % EB + 512].rearrange("(e p) t -> p e t", p=128), g_y[:],
                      reads=[t_y])
        S.barrier()

        NKB = S_LEN // 128
        with ExitStack() as _es2:
            def A2(name, shape, dt):
                return _es2.enter_context(nc.sbuf_tensor(name, shape, dt))
            ws = A2("ws", [128, NCH, 512], BF16)
            s_kT = A2("s_kT", [128, S_LEN], BF16)
            s_v = A2("s_v", [128, NKB, 128], BF16)
            s_qTs = (A2("s_qTa", [128, TQ], BF16), A2("s_qTb", [128, TQ], BF16))
            s_gss = (A2("s_gsa", [128, TQ], BF16), A2("s_gsb", [128, TQ], BF16))
            s_ta = A2("s_ta", [128, 512], F32)
            s_tb = A2("s_tb", [128, 512], F32)
            e1s = (A2("s_e1a", [128, TQ], F32), A2("s_e1b", [128, TQ], F32), A2("s_e1c", [128, TQ], F32))
            sps = (A2("s_spa", [128, TQ], BF16), A2("s_spb", [128, TQ], BF16))
            gs_ = (A2("s_ga", [128, TQ], BF16), A2("s_gb", [128, TQ], BF16))
            ws_ = (A2("s_wa", [128, TQ], BF16), A2("s_wb", [128, TQ], BF16))
            s_y = A2("s_y", [128, TQ], BF16)
            t_ws, t_sy, t_stmp, t_msown = T(), T(), T(), T()
            t_skT = [T() for _ in range(S_LEN // 512)]
            t_sv = [T() for _ in range(S_LEN // 512)]
            t_sqT, t_sgs = (T(), T()), (T(), T())
            t_e1, t_sps, t_gs_, t_ws_ = (T(), T(), T()), (T(), T()), (T(), T()), (T(), T())
            ZB, BB, OB = 0, 2, 4
            blocks = [] if SKIP else [(hd, tb) for hd in range(4) for tb in range(S_LEN // TQ)]

            def inproj_gen(bi):
                hd, tb = blocks[bi]
                s_qT, s_gs = s_qTs[bi % 2], s_gss[bi % 2]
                tq_, tg_ = t_sqT[bi % 2], t_sgs[bi % 2]
                q0 = tb * TQ
                if tb == 0:
                    S.dma("pool", ws[:], wsb[hd].rearrange("(c p) n -> p c n", p=128), writes=[t_ws])
                pending = [None]

                def flush():
                    if pending[0] is not None:
                        pending[0]()
                        pending[0] = None

                for half in range(NBK):
                    t0 = q0 + half * 512
                    tok = slice(t0, t0 + 512)
                    loc = slice(half * 512, half * 512 + 512)
                    hr = h_reads(t0, t0 + 512)
                    for (col0, bk_, evac) in (
                        (0, 6, lambda loc=loc: S.op("dve", lambda: dve.tensor_scalar_mul(
                            out=s_qT[:, loc], in0=bank(6), scalar1=128 ** -0.5), reads=[t_ps[6]], writes=[tq_])),
                        (128, 7, lambda tok=tok, t0=t0: S.op("dve", lambda: dve.tensor_copy(s_kT[:, tok], bank(7)),
                                                             reads=[t_ps[7]], writes=[t_skT[t0 // 512]])),
                        (384, 6, lambda loc=loc: silu_evac(bank(6), s_gs[:, loc], s_ta[:], s_tb[:], t_stmp,
                                                           [t_ps[6]], [tg_])),
                    ):
                        for c in range(NCH):
                            S.op("pe", lambda c=c, col0=col0, bk_=bk_: mm(
                                bank(bk_), lhsT=ws[:, c, col0:col0 + 128], rhs=hT[:, c, tok],
                                start=(c == 0), stop=(c == NCH - 1)),
                                reads=hr + [t_ws], writes=[t_ps[bk_]], inc=(c == NCH - 1))
                            if c % 4 == 3:
                                if c == 3:
                                    flush()
                                yield
                        pending[0] = evac
                    for sb in range(4):
                        kb = (t0 // 128) + sb
                        for c in range(NCH):
                            S.op("pe", lambda c=c, kb=kb, sb=sb: mm(
                                bank(7)[:, sb * 128:sb * 128 + 128], lhsT=hT[:, c, kb * 128:kb * 128 + 128],
                                rhs=ws[:, c, 256:384], start=(c == 0 and sb == 0), stop=(c == NCH - 1)),
                                reads=hr + [t_ws], writes=[t_ps[7]], inc=(c == NCH - 1 and sb == 3))
                            if c % 4 == 3:
                                if c == 3 and sb == 0:
                                    flush()
                                yield
                    pending[0] = (lambda t0=t0: S.op("dve", lambda: dve.tensor_copy(
                        s_v[:, t0 // 128:t0 // 128 + 4, :], bank(7).rearrange("p (a b) -> p a b", a=4)),
                        reads=[t_ps[7]], writes=[t_sv[t0 // 512]]))
                flush()
                yield

            NYIELD = NBK * 28 + 1

            def pump(gen, n):
                if gen is None:
                    return
                for _ in range(n):
                    try:
                        next(gen)
                    except StopIteration:
                        return

            if blocks:
                pump(inproj_gen(0), 10 ** 6)
                for k in range(4):
                    issue_cc(mg_own[k], mg_all[k])
            for bi, (hd, tb) in enumerate(blocks):
                    q0 = tb * TQ
                    s_qT, s_gs = s_qTs[bi % 2], s_gss[bi % 2]
                    tq_, tg_ = t_sqT[bi % 2], t_sgs[bi % 2]
                    nxt = inproj_gen(bi + 1) if bi + 1 < len(blocks) else None
                    kbs = list(range((tb + 1) * (TQ // 128) - 1, -1, -1))
                    P = len(kbs)
                    npump = (NYIELD + P - 1) // P
                    startedB = [False] * NBK
                    startedO = [False] * NBK

                    def geom(p):
                        kb = kbs[p]
                        off = kb * 128 - q0
                        lo = max(0, off)
                        segs = []
                        for bki in range(NBK):
                            c0 = max(lo, bki * 512)
                            c1 = (bki + 1) * 512
                            if c0 < c1:
                                segs.append((bki, c0, c1))
                        return kb, off, lo, segs

                    def emit_Z(p):
                        kb, off, lo, segs = geom(p)
                        for (bki, c0, c1) in segs:
                            S.op("pe", lambda bki=bki, c0=c0, c1=c1, kb=kb: mm(
                                bank(ZB + bki)[:, c0 - bki * 512:c1 - bki * 512],
                                lhsT=s_kT[:, kb * 128:kb * 128 + 128], rhs=s_qT[:, c0:c1], start=True, stop=True),
                                reads=[t_skT[kb // 4], tq_], writes=[t_ps[ZB + bki]])

                    def zb_reads(segs, base):
                        return [t_ps[base + bki] for (bki, _, _) in segs]

                    def emit_E1(p):
                        kb, off, lo, segs = geom(p)
                        e1, te1 = e1s[p % 3], t_e1[p % 3]
                        S.op("act", lambda: act.activation(
                            out=e1[:, lo:TQ], in_=PS[:, ZB * 512 + lo:ZB * 512 + TQ], func=AF.Exp),
                            reads=zb_reads(segs, ZB), writes=[te1])
                        if off >= 0:
                            S.op("dve", lambda: dve.tensor_tensor(
                                out=e1[:, off:off + 128], in0=e1[:, off:off + 128], in1=Ustr_f, op=ALU.mult),
                                reads=[te1, t_cst], writes=[te1])

                    emit_Z(0)
                    emit_E1(0)
                    if P > 1:
                        emit_Z(1)
                    for p in range(P + 1):
                        if p >= 1:
                            kbp, offp, lop, segsp = geom(p - 1)
                            e1p, te1p = e1s[(p - 1) % 3], t_e1[(p - 1) % 3]
                            spp, tspp = sps[(p - 1) % 2], t_sps[(p - 1) % 2]
                            gp, tgp = gs_[(p - 1) % 2], t_gs_[(p - 1) % 2]
                            wp_, twp = ws_[(p - 1) % 2], t_ws_[(p - 1) % 2]
                            S.op("act", lambda gp=gp, lop=lop: act.activation(
                                out=gp[:, lop:TQ], in_=PS[:, BB * 512 + lop:BB * 512 + TQ], func=AF.Exp, scale=-1.0),
                                reads=zb_reads(segsp, BB), writes=[tgp])
                            if p - 1 < P - 1:
                                for (bki, c0, c1) in segsp:
                                    S.op("pe", lambda bki=bki, c0=c0, c1=c1, spp=spp: mm(
                                        bank(BB + bki)[:, c0 - bki * 512:c1 - bki * 512], lhsT=Ustr_b, rhs=spp[:, c0:c1],
                                        start=False, stop=True),
                                        reads=[tspp, t_cst], writes=[t_ps[BB + bki]])
                        if p < P:
                            kb, off, lo, segs = geom(p)
                            e1, te1 = e1s[p % 3], t_e1[p % 3]
                            sp_, tsp = sps[p % 2], t_sps[p % 2]
                            S.op("act", lambda e1=e1, sp_=sp_, lo=lo: act.activation(
                                out=sp_[:, lo:TQ], in_=e1[:, lo:TQ], func=AF.Ln, bias=1.0),
                                reads=[te1], writes=[tsp])
                            for (bki, c0, c1) in segs:
                                S.op("pe", lambda bki=bki, c0=c0, c1=c1, sp_=sp_, st=(not startedB[bki]): mm(
                                    bank(BB + bki)[:, c0 - bki * 512:c1 - bki * 512], lhsT=Lincl_b, rhs=sp_[:, c0:c1],
                                    start=st, stop=True),
                                    reads=[tsp, t_cst], writes=[t_ps[BB + bki]])
                                startedB[bki] = True
                        if p + 1 < P:
                            emit_E1(p + 1)
                        if p + 2 < P:
                            emit_Z(p + 2)
                        pump(nxt, npump // 2)
                        if p >= 1:
                            S.op("dve", lambda wp_=wp_, e1p=e1p, gp=gp, lop=lop: dve.tensor_tensor(
                                out=wp_[:, lop:TQ], in0=e1p[:, lop:TQ], in1=gp[:, lop:TQ], op=ALU.mult),
                                reads=[te1p, tgp], writes=[twp])
                            for (bki, c0, c1) in segsp:
                                S.op("pe", lambda bki=bki, c0=c0, c1=c1, wp_=wp_, kbp=kbp, st=(not startedO[bki]): mm(
                                    bank(OB + bki)[:, c0 - bki * 512:c1 - bki * 512], lhsT=s_v[:, kbp, :], rhs=wp_[:, c0:c1],
                                    start=st, stop=True),
                                    reads=[t_sv[kbp // 4], twp], writes=[t_ps[OB + bki]])
                                startedO[bki] = True
                        pump(nxt, npump - npump // 2)
                    pump(nxt, 10 ** 6)
                    S.op("dve", lambda: dve.tensor_tensor(out=s_y[:], in0=PS[:, OB * 512:OB * 512 + TQ], in1=s_gs[:], op=ALU.mult),
                         reads=[t_ps[OB + i] for i in range(NBK)] + [tg_], writes=[t_sy])
                    S.dma("sp", ms_own[hd, q0 // HS, :, q0 % HS:q0 % HS + TQ], s_y[:], reads=[t_sy], writes=[t_msown])
                    if tb == S_LEN // TQ - 1 and hd < 3:
                        S._wait("pool", S._deps("pool", [t_msown], []))
                        issue_cc(ms_own[hd].rearrange("hh p t -> (hh p) t"), ms_all[hd])
        S.barrier(skip=("cc",))

    BT = 256
    if mode == "AB":
        par = nc.sync.partition_id() % 2
        mg_v = mg_all.rearrange("k (c p) t -> k p c t", p=128)
        ms_v = ms_all.rearrange("h q t -> (h q) t").rearrange("(hr hh p) t -> hh p hr t", hr=8, hh=2)
    else:
        mixin_v = mixin.rearrange("(c p) t -> p c t", p=128)
    with ExitStack() as _es3:
        wo_b = _es3.enter_context(nc.sbuf_tensor("wo_b", [128, NCH, D], BF16))
        wgt_b = _es3.enter_context(nc.sbuf_tensor("wgt_b", [128, NCH, D], BF16))
        wp_b = _es3.enter_context(nc.sbuf_tensor("wp_b", [128, 2, D], BF16))
        mxa = _es3.enter_context(nc.sbuf_tensor("mxa", [128, NCH, BT], BF16))
        xra = _es3.enter_context(nc.sbuf_tensor("xra", [128, NCH, BT], F32))
        pbts = (_es3.enter_context(nc.sbuf_tensor("pbt", [128, 2, BT], BF16)), _es3.enter_context(nc.sbuf_tensor("pbt2", [128, 2, BT], BF16)))
        m1 = _es3.enter_context(nc.sbuf_tensor("m1", [128, NCH, BT], F32))
        sq3 = _es3.enter_context(nc.sbuf_tensor("sq3", [128, NCH, BT], BF16))
        r3 = _es3.enter_context(nc.sbuf_tensor("r3", [128, BT], F32))
        gte = _es3.enter_context(nc.sbuf_tensor("gte", [128, 2, BT], F32))
        oo = _es3.enter_context(nc.sbuf_tensor("oo", [128, 4, BT], F32))
        t_wo = [T() for _ in range(4)]
        t_wgt = [T() for _ in range(4)]
        t_wp = T()
        for j in range(4):
            S.dma("pool", wo_b[:, :, j * 512:(j + 1) * 512],
                  wo[:, j * 512:(j + 1) * 512].rearrange("(c p) n -> p c n", p=128), writes=[t_wo[j]])
        for j in range(4):
            S.dma("pool", wgt_b[:, :, j * 512:(j + 1) * 512],
                  wgt[:, j * 512:(j + 1) * 512].rearrange("(c p) n -> p c n", p=128), writes=[t_wgt[j]])
        S.dma("pool", wp_b[:], wp.rearrange("(c p) n -> p c n", p=128), writes=[t_wp])
        if not SKIP:
            issue_cc(ms_own[3].rearrange("hh p t -> (hh p) t"), ms_all[3])
        mxs, t_mx = (mxa, mxa), (T(),) * 2
        xrs, t_xr = (xra, xra), (T(),) * 2
        h1b = sq3
        t_pb, t_m1, t_sq3, t_r3 = T(), T(), T(), T()
        t_h1b = t_sq3
        t_gte = (T(), T())
        t_oo = [T() for _ in range(4)]
        pTo_v = pTo.rearrange("(c p) t -> p c t", p=128)
        outT_v = outT.rearrange("(c p) t -> p c t", p=128)
        rot3 = [0]

        def nb3():
            rot3[0] = (rot3[0] + 1) % 8
            return rot3[0]

        t_m1c = [T() for _ in range(NCH)]
        t_mxp = [T() for _ in range(5)]
        t_sqc = [T() for _ in range(NCH)]
        t_pbs = (T(), T())
        NB3 = HS // BT

        def emit_loads(nb):
            tsl = slice(nb * BT, (nb + 1) * BT)
            if mode == "AB":
                jj = (nb * BT) // EB
                cc0 = (nb * BT) % EB
                S.dma("sp", mxa[:, 0:8, :], mg_v[bass.ds(par * 2 + jj, 1), :, :, cc0:cc0 + BT].rearrange("o p c t -> p (o c) t"),
                      reads=[t_mix], writes=[t_mxp[0]])
                S.dma("sp", mxa[:, 8:16, :],
                      ms_v[bass.ds(par, 1), :, :, nb * BT:(nb + 1) * BT].rearrange("o p r t -> p (o r) t"),
                      reads=[t_mix], writes=[t_mxp[1]])
            else:
                S.dma("sp", mxa[:], mixin_v[:, :, tsl], writes=[t_mx[0]])
            S.dma("sp", xra[:], xTo[nb].rearrange("p (c t) -> p c t", c=NCH), writes=[t_xr[0]])
            S.dma("pool", pbts[nb % 2][:], pTo_v[:, :, tsl], writes=[t_pbs[nb % 2]])

        for nb in range(NB3):
            emit_loads(nb)
            tsl = slice(nb * BT, (nb + 1) * BT)
            mx, tmx = mxa, t_mx[0]
            xr, txr = xra, t_xr[0]
            pbt, t_pb = pbts[nb % 2], t_pbs[nb % 2]
            for oc in range(NCH):
                bk = nb3()
                for c in range(NCH):
                    S.op("pe", lambda c=c, oc=oc, bk=bk: mm(
                        bank(bk)[:, 0:BT], lhsT=wo_b[:, c, oc * 128:(oc + 1) * 128], rhs=mx[:, c, :],
                        start=(c == 0), stop=(c == NCH - 1)),
                        reads=[t_wo[oc // 4], tmx] + t_mxp, writes=[t_ps[bk]], inc=(c == NCH - 1))
                S.op("act", lambda oc=oc, bk=bk: act.activation(out=sq3[:, oc, :], in_=bank(bk)[:, 0:BT], func=AF.Square),
                     reads=[t_ps[bk]], writes=[t_sqc[oc]])
                S.op("dve", lambda oc=oc, bk=bk: dve.tensor_scalar_mul(
                    out=m1[:, oc, :], in0=bank(bk)[:, 0:BT], scalar1=cols_f[:, 16 + oc:17 + oc]),
                    reads=[t_ps[bk], t_cst], writes=[t_m1c[oc]])
            bk = nb3()
            for c in range(NCH):
                S.op("pe", lambda c=c, bk=bk: mm(bank(bk)[:, 0:BT], lhsT=ones_b, rhs=sq3[:, c, :],
                                                       start=(c == 0), stop=(c == NCH - 1)),
                     reads=[t_sqc[c], t_cst], writes=[t_ps[bk]], inc=(c == NCH - 1))
            S.op("act", lambda bk=bk: act.activation(out=r3[:], in_=bank(bk)[:, 0:BT], func=AF.Ln, scale=1.0 / D, bias=EPS),
                 reads=[t_ps[bk]], writes=[t_r3])
            S.op("act", lambda: act.activation(out=r3[:], in_=r3[:], func=AF.Exp, scale=-0.5),
                 reads=[t_r3], writes=[t_r3])
            for oc in range(NCH):
                S.op("dve", lambda oc=oc: dve.tensor_tensor(out=m1[:, oc, :], in0=m1[:, oc, :], in1=r3[:], op=ALU.mult),
                     reads=[t_m1c[oc], t_r3], writes=[t_m1c[oc]])
                S.op("pool", lambda oc=oc: pool.tensor_tensor(out=m1[:, oc, :], in0=m1[:, oc, :], in1=xr[:, oc, :], op=ALU.add),
                     reads=[t_m1c[oc], txr], writes=[t_m1c[oc]])
                S.op("act", lambda oc=oc: act.activation(out=h1b[:, oc, :], in_=m1[:, oc, :], func=AF.Copy),
                     reads=[t_m1c[oc]], writes=[t_sqc[oc]])
            for oc in range(NCH):
                bk = nb3()
                for c in range(NCH):
                    S.op("pe", lambda c=c, oc=oc, bk=bk: mm(
                        bank(bk)[:, 0:BT], lhsT=wgt_b[:, c, oc * 128:(oc + 1) * 128], rhs=h1b[:, c, :],
                        start=(c == 0), stop=(c == NCH - 1)),
                        reads=[t_wgt[oc // 4], t_sqc[c]], writes=[t_ps[bk]], inc=(c == NCH - 1))
                gt, tgt = gte[:, oc % 2, :], t_gte[oc % 2]
                S.op("act", lambda oc=oc, bk=bk, gt=gt: act.activation(
                    out=gt, in_=bank(bk)[:, 0:BT], func=AF.Sigmoid, bias=cols_f[:, 32 + oc:33 + oc]),
                    reads=[t_ps[bk], t_cst], writes=[tgt])
                bk2 = nb3()
                for c in range(2):
                    S.op("pe", lambda c=c, oc=oc, bk2=bk2: mm(
                        bank(bk2)[:, 0:BT], lhsT=wp_b[:, c, oc * 128:(oc + 1) * 128], rhs=pbt[:, c, :],
                        start=(c == 0), stop=(c == 1)),
                        reads=[t_wp, t_pb], writes=[t_ps[bk2]], inc=(c == 1))
                S.op("dve", lambda oc=oc, bk2=bk2, gt=gt: dve.tensor_tensor(
                    out=gt, in0=gt, in1=bank(bk2)[:, 0:BT], op=ALU.mult),
                    reads=[tgt, t_ps[bk2]], writes=[tgt])
                S.op("dve", lambda oc=oc, gt=gt: dve.tensor_tensor(
                    out=oo[:, oc % 4, :], in0=gt, in1=m1[:, oc, :], op=ALU.add),
                    reads=[tgt, t_m1c[oc]], writes=[t_oo[oc % 4]])
                if oc % 4 == 3:
                    S.dma("pool", outT_v[:, oc - 3:oc + 1, tsl], oo[:], reads=t_oo)
        S.barrier()
    return nc


_PROG = {}


def kernel(x, p, g_pre, w_in, w_a2, b_a, g_gla_head, w_out, g_post, w_ple_gate, b_ple_gate, w_ple_proj):
    x = np.asarray(x, np.float32)
    B, S_LEN, _ = x.shape
    HS = S_LEN // 2
    f = lambda a: np.ascontiguousarray(np.asarray(a, np.float32))
    p, g_pre, w_in, w_a2, b_a = f(p), f(g_pre), f(w_in), f(w_a2), f(b_a)
    g_gla_head, w_out, g_post = f(g_gla_head), f(w_out), f(g_post)
    w_ple_gate, b_ple_gate, w_ple_proj = f(w_ple_gate), f(b_ple_gate), f(w_ple_proj)
    W = w_in[0]
    GQ, GK, GV, GG, LR, SQ, SK, SV, SG = 0, 512, 1024, 2048, 3072, 3088, 4112, 5136, 6160

    def colv(v):
        return v.reshape(-1, 128).T

    ii = np.arange(128)
    cst = np.zeros((128, 5, 128), np.float32)
    cst[:, 0, :] = (ii[:, None] == ii[None, :])
    cst[:, 1, :] = (ii[:, None] <= ii[None, :])
    cst[:, 2, :] = (ii[:, None] < ii[None, :])
    cst[:, 3, :] = (ii[:, None] >= ii[None, :])
    cst[:, 4, :] = 1.0
    cols = np.zeros((128, 64), np.float32)
    cols[:, 0:16] = colv(g_pre[0])
    cols[:, 16:32] = colv(g_post[0])
    cols[:, 32:48] = colv(b_ple_gate[0])
    cols[:, 48:50] = colv(g_gla_head[0])
    in_maps = []
    for core in range(8):
        b, hh = core // 2, core % 2
        wsb = np.stack([np.concatenate([W[:, SQ + 128 * H:SQ + 128 * H + 128], W[:, SK + 128 * H:SK + 128 * H + 128],
                                        W[:, SV + 128 * H:SV + 128 * H + 128], W[:, SG + 128 * H:SG + 128 * H + 128]], axis=1)
                        for H in range(4 * hh, 4 * hh + 4)])
        wgla = np.stack([np.concatenate([W[:, GQ + 128 * G:GQ + 128 * G + 128], W[:, GK + 128 * G:GK + 128 * G + 128],
                                         W[:, GV + 256 * G:GV + 256 * G + 256], W[:, GG + 256 * G:GG + 256 * G + 256]], axis=1)
                         for G in range(2 * hh, 2 * hh + 2)])
        xtile = np.ascontiguousarray(x[b].reshape(S_LEN // 256, 256, NCH, 128).transpose(0, 3, 2, 1)).reshape(
            S_LEN // 256, 128, NCH * 256)
        wo_perm = np.concatenate([w_out[0][0:1024]] + [w_out[0][1024 + (4 * r + h) * 128:1024 + (4 * r + h) * 128 + 128]
                                                       for h in range(4) for r in range(2)], axis=0)
        in_maps.append({
            "xT": xtile,
            "xTo": np.ascontiguousarray(xtile[hh * (HS // 256):(hh + 1) * (HS // 256)]),
            "pTo": np.ascontiguousarray(p[0, b, hh * HS:(hh + 1) * HS].T),
            "wsb": np.ascontiguousarray(wsb),
            "wgla": np.ascontiguousarray(wgla),
            "wlr": np.ascontiguousarray(W[:, LR:LR + 16]),
            "wa2": np.ascontiguousarray(w_a2[0][:, 256 * hh:256 * hh + 256]),
            "ba": np.ascontiguousarray(b_a[0][None, 256 * hh:256 * hh + 256]),
            "cols": cols,
            "wo": np.ascontiguousarray(wo_perm),
            "wgt": w_ple_gate[0],
            "wp": w_ple_proj[0],
            "cst": cst,
        })
    if S_LEN not in _PROG:
        _PROG[S_LEN] = build_program(S_LEN, "AB")
    res = run_bass_kernel_spmd(_PROG[S_LEN], in_maps, core_ids=list(range(8)))
    out = np.empty((B, S_LEN, D), np.float32)
    for core in range(8):
        b, hh = core // 2, core % 2
        out[b, hh * HS:(hh + 1) * HS, :] = res.results[core]["outT"].T
    return out
```

```python
from contextlib import ExitStack
import numpy as np
import concourse.bass as bass
import concourse.mybir as mybir
from concourse.bass_utils import run_bass_kernel_spmd

F32 = mybir.dt.float32
BF16 = mybir.dt.bfloat16
AF = mybir.ActivationFunctionType
ALU = mybir.AluOpType

D = 2048
NCH = 16
EPS = 1e-6
TQ = 1024
NBK = TQ // 512


class T:
    __slots__ = ("w", "r", "x")

    def __init__(self, x=False):
        self.w = None
        self.r = []
        self.x = x


class Sched:
    ENG = ("pe", "act", "dve", "pool", "sp")

    def __init__(self, nc, n_dma_sems=28):
        self.nc = nc
        self.e = dict(pe=nc.tensor, act=nc.scalar, dve=nc.vector, pool=nc.gpsimd, sp=nc.sync)
        self.sems = {}
        self.cnt = {}
        for k in self.ENG:
            self.sems[k] = nc.alloc_semaphore("s_" + k)
            self.cnt[k] = 0
        self.dma_keys = []
        for i in range(n_dma_sems):
            k = "d%d" % i
            self.sems[k] = nc.alloc_semaphore("s_" + k)
            self.cnt[k] = 0
            self.dma_keys.append(k)
        self.sems["cc"] = nc.alloc_semaphore("s_cc")
        self.cnt["cc"] = 0
        self.dma_rr = 0
        self.seen = {k: {} for k in self.ENG}

    def _deps(self, eng, reads, writes):
        need = {}

        def add(d, same_ok):
            if d is None:
                return
            k, v = d
            if k == eng and same_ok:
                return
            if need.get(k, 0) < v:
                need[k] = v
        for t in reads:
            add(t.w, False)
            if t.x:
                for d in t.r:
                    add(d, True)
        for t in writes:
            add(t.w, True)
            for d in t.r:
                add(d, False)
        return need

    def _wait(self, eng, need):
        seen = self.seen[eng]
        for k, v in need.items():
            if seen.get(k, 0) < v:
                self.e[eng].wait_ge(self.sems[k], v)
                seen[k] = v

    def _record(self, d, reads, writes):
        for t in reads:
            t.r.append(d)
        for t in writes:
            t.w = d
            t.r = []

    def op(self, eng, fn, reads=(), writes=(), inc=True):
        self._wait(eng, self._deps(eng, reads, writes))
        ins = fn()
        if inc:
            self.cnt[eng] += 1
            ins.then_inc(self.sems[eng], 1)
            seq = self.cnt[eng]
        else:
            seq = self.cnt[eng] + 1
        self._record((eng, seq), reads, writes)
        return ins

    def dma(self, eng, out, in_, reads=(), writes=()):
        self._wait(eng, self._deps(eng, reads, writes))
        k = self.dma_keys[self.dma_rr]
        self.dma_rr = (self.dma_rr + 1) % len(self.dma_keys)
        ins = self.e[eng].dma_start(out=out, in_=in_)
        self.cnt[k] += 16
        ins.then_inc(self.sems[k], 16)
        self._record((k, self.cnt[k]), reads, writes)
        return ins

    def barrier(self, skip=()):
        for eng in self.ENG:
            need = {k: v for k, v in self.cnt.items() if v > 0 and k not in skip}
            self._wait(eng, need)


def build_program(S_LEN, mode="AB"):
    STOP = 9
    P3 = 9
    HS = S_LEN // 2
    nc = bass.Bass("TRN2", target_bir_lowering=False)
    S = Sched(nc)
    pe, act, dve, pool = nc.tensor, nc.scalar, nc.vector, nc.gpsimd

    def mm(out, **kw):
        return pe.matmul(out, skip_group_check=True, **kw)

    def din(name, shape, dt=F32):
        return nc.dram_tensor(name, shape, dt, kind="ExternalInput").ap()

    xT = din("xT", [S_LEN // 256, 128, NCH * 256])
    xTo = din("xTo", [HS // 256, 128, NCH * 256])
    pTo = din("pTo", [256, HS])
    wsb = din("wsb", [4, D, 512])
    wgla = din("wgla", [2, D, 768])
    wlr = din("wlr", [D, 16])
    wa2 = din("wa2", [16, 256])
    ba = din("ba", [1, 256])
    cols = din("cols", [128, 64])
    wo = din("wo", [D, D])
    wgt = din("wgt", [D, D])
    wp = din("wp", [256, D])
    cst = din("cst", [128, 5, 128])
    outT = nc.dram_tensor("outT", [D, HS], F32, kind="ExternalOutput").ap()
    if mode == "A":
        mix_own = nc.dram_tensor("mix_own", [4, 1024, S_LEN // 4], BF16, kind="ExternalOutput").ap()
    else:
        mix_own = nc.dram_tensor("mix_own", [4, 1024, S_LEN // 4], BF16).ap()
    mix_all = nc.dram_tensor("mix_all", [4, 2048, S_LEN // 4], BF16).ap()
    mg_own = nc.dram_tensor("mg_own", [4, 512, S_LEN // 4], BF16).ap()
    mg_all = nc.dram_tensor("mg_all", [4, 1024, S_LEN // 4], BF16).ap()
    ms_own = nc.dram_tensor("ms_own", [4, 2, 128, HS], BF16).ap()
    ms_all = nc.dram_tensor("ms_all", [4, 512, HS], BF16).ap()
    GROUPS = [[0, 1], [2, 3], [4, 5], [6, 7]]
    wo16 = nc.dram_tensor("wo16", [D, D], BF16).ap()
    wgt16 = nc.dram_tensor("wgt16", [D, D], BF16).ap()
    t_w16 = {"wo": [T() for _ in range(4)], "wgt": [T() for _ in range(4)]}
    t_mix = T()

    def issue_cc(src, dst):
        pool.collective_compute("AllGather", ALU.bypass, replica_groups=GROUPS, ins=[src], outs=[dst]
                                ).then_inc(S.sems["cc"], 1)
        S.cnt["cc"] += 1
        t_mix.w = ("cc", S.cnt["cc"])
    EB = S_LEN // 4
    if mode == "B":
        mixin = din("mixin", [2048, HS], BF16)
    SKIP = (mode == "B")

    cst_f = nc.alloc_sbuf_tensor("cst_f", [128, 5, 128], F32)
    cst_b = nc.alloc_sbuf_tensor("cst_b", [128, 5, 128], BF16)
    cols_f = nc.alloc_sbuf_tensor("cols_f", [128, 64], F32)
    wa2_f = nc.alloc_sbuf_tensor("wa2_f", [16, 256], F32)
    ba_f = nc.alloc_sbuf_tensor("ba_f", [1, 256], F32)
    t_cst = T()
    S.dma("sp", cst_f[:], cst[:, :, :], writes=[t_cst])
    S.dma("sp", cols_f[:], cols[:, :], writes=[t_cst])
    S.dma("sp", wa2_f[:], wa2[:, :], writes=[t_cst])
    S.dma("sp", ba_f[:], ba[:, :], writes=[t_cst])
    S.op("dve", lambda: dve.tensor_copy(cst_b[:], cst_f[:]), reads=[t_cst], writes=[t_cst])
    ident_b = cst_b[:, 0, :]
    Uincl_f = cst_f[:, 1, :]
    Ustr_f = cst_f[:, 2, :]
    Ustr_b = cst_b[:, 2, :]
    Lincl_b = cst_b[:, 3, :]
    ones_b = cst_b[:, 4, :]
    ones_f = cst_f[:, 4, :]

    PS = nc.alloc_psum_tensor("ps", [128, 4096], F32)
    t_ps = [T(True) for _ in range(8)]

    def bank(i):
        return PS[:, i * 512:(i + 1) * 512]

    with nc.sbuf_tensor("hT", [128, NCH, S_LEN], BF16) as hT:
        NB0 = S_LEN // 256
        NLOOP0 = 0 if SKIP else NB0
        t_hT = [T() for _ in range(NB0)]
        t_hTp = [T() for _ in range(NB0)]

        def h_reads(t0, t1):
            rng = range(t0 // 256, (t1 + 255) // 256)
            return [t_hT[i] for i in rng] + [t_hTp[i] for i in rng]

        with ExitStack() as _es0:
            xb0 = _es0.enter_context(nc.sbuf_tensor("xb0", [128, NCH, 256], F32))
            xb1 = _es0.enter_context(nc.sbuf_tensor("xb1", [128, NCH, 256], F32))
            sq0 = _es0.enter_context(nc.sbuf_tensor("sq0", [128, NCH, 256], BF16))
            r0a = _es0.enter_context(nc.sbuf_tensor("r0", [128, 256], F32))
            r0b = _es0.enter_context(nc.sbuf_tensor("r1", [128, 256], F32))
            p0tmp = (_es0.enter_context(nc.sbuf_tensor("p0ta", [128, 256], F32)), _es0.enter_context(nc.sbuf_tensor("p0tb", [128, 256], F32)))
            t_p0tmp = (T(), T())
            xbs = (xb0, xb1)
            t_xb = (T(), T())
            t_sq = T()
            rs = (r0a, r0b)
            t_r = (T(), T())
            for nb in range(NLOOP0):
                xb = xbs[nb % 2]
                txb = t_xb[nb % 2]
                r = rs[nb % 2]
                tr = t_r[nb % 2]
                tsl = slice(nb * 256, (nb + 1) * 256)
                S.dma("sp", xb[:], xT[nb].rearrange("p (c t) -> p c t", c=NCH), writes=[txb])
                S.op("act", lambda xb=xb: act.activation(out=sq0[:], in_=xb[:], func=AF.Square),
                     reads=[txb], writes=[t_sq])
                bk = nb % 2
                for c in range(NCH):
                    S.op("pe", lambda c=c, bk=bk: mm(bank(bk)[:, 0:256], lhsT=ones_b, rhs=sq0[:, c, :],
                                                           start=(c == 0), stop=(c == NCH - 1)),
                         reads=[t_sq, t_cst], writes=[t_ps[bk]], inc=(c == NCH - 1))
                S.op("act", lambda r=r, bk=bk: act.activation(out=r[:], in_=bank(bk)[:, 0:256], func=AF.Ln,
                                                               scale=1.0 / D, bias=EPS),
                     reads=[t_ps[bk]], writes=[tr])
                S.op("act", lambda r=r: act.activation(out=r[:], in_=r[:], func=AF.Exp, scale=-0.5),
                     reads=[tr], writes=[tr])
                for c in range(NCH):
                    if c % 3 == 2:
                        k = (c // 3) % 2
                        S.op("act", lambda c=c, xb=xb, k=k: act.activation(
                            out=p0tmp[k][:], in_=xb[:, c, :], func=AF.Identity, scale=cols_f[:, c:c + 1]),
                            reads=[txb, t_cst], writes=[t_p0tmp[k]])
                        S.op("pool", lambda c=c, r=r, k=k: pool.tensor_tensor(
                            out=hT[:, c, tsl], in0=p0tmp[k][:], in1=r[:], op=ALU.mult),
                            reads=[t_p0tmp[k], tr], writes=[t_hTp[nb]])
                        continue
                    S.op("dve", lambda c=c, xb=xb, r=r: dve.scalar_tensor_tensor(
                        out=hT[:, c, tsl], in0=xb[:, c, :], scalar=cols_f[:, c:c + 1], in1=r[:],
                        op0=ALU.mult, op1=ALU.mult),
                        reads=[txb, tr, t_cst], writes=[t_hT[nb]])
        S.barrier()
        if STOP <= 0:
            return nc

        def silu_evac(src_ap, dst_ap, tmp_a, tmp_b, t_tmp, reads, writes):
            S.op("act", lambda: act.activation(out=tmp_a, in_=src_ap, func=AF.Exp, scale=-1.0),
                 reads=reads, writes=[t_tmp])
            S.op("dve", lambda: dve.tensor_scalar_add(out=tmp_a, in0=tmp_a, scalar1=1.0),
                 reads=[t_tmp], writes=[t_tmp])
            S.op("dve", lambda: dve.reciprocal(out=tmp_b, in_=tmp_a), reads=[t_tmp], writes=[t_tmp])
            S.op("dve", lambda: dve.tensor_tensor(out=dst_ap, in0=src_ap, in1=tmp_b, op=ALU.mult),
                 reads=list(reads) + [t_tmp], writes=writes)

        with ExitStack() as _es1:
            def A1(name, shape, dt):
                return _es1.enter_context(nc.sbuf_tensor(name, shape, dt))
            wg = A1("wg", [128, NCH, 768], BF16)
            wlr_b = A1("wlr_b", [128, NCH, 16], BF16)
            g_qTs = (A1("g_qTa", [128, 512], BF16), A1("g_qTb", [128, 512], BF16))
            g_kTs = (A1("g_kTa", [128, 512], BF16), A1("g_kTb", [128, 512], BF16))
            g_silus = (A1("g_silua", [128, 2, 512], BF16), A1("g_silub", [128, 2, 512], BF16))
            g_vs = (A1("g_va", [128, 4, 256], BF16), A1("g_vb", [128, 4, 256], BF16))
            g_lr1 = A1("g_lra", [16, 512], F32)
            g_lrs = (g_lr1, g_lr1)
            g_tmpa = A1("g_tmpa", [128, 512], F32)
            g_tmpb = A1("g_tmpb", [128, 512], F32)
            g_mask4 = A1("g_mask4", [128, 4, 128], F32)
            g_e = A1("g_e", [128, 512], F32)
            g_tmpc = g_e
            g_sp = A1("g_sp", [128, 4, 128], F32)
            g_Eq = A1("g_Eq", [128, 512], F32)
            g_Ek = A1("g_Ek", [128, 512], F32)
            g_qe = A1("g_qe", [128, 512], BF16)
            g_ke = A1("g_ke", [128, 512], BF16)
            g_klT = A1("g_klT", [128, 512], BF16)
            g_kl = A1("g_kl", [128, 4, 128], BF16)
            g_scm = A1("g_scm", [128, 512], BF16)
            g_Sf = A1("g_Sf", [128, 256], F32)
            g_Sb = A1("g_Sb", [128, 4, 256], BF16)
            g_sq = A1("g_sq", [128, 2, 512], BF16)
            g_r = A1("g_r", [128, 512], F32)
            g_y = A1("g_y", [128, 2, 512], BF16)
            t_wg, t_wlr, t_mask4 = T(), T(), T()
            t_qT, t_kT, t_silu, t_v = ((T(), T()) for _ in range(4))
            t_lr = (T(),) * 2
            t_tmp = T()
            t_e, t_sp, t_Eq, t_Ek, t_qe, t_ke, t_klT, t_kl, t_scm = (T() for _ in range(9))
            t_Sf, t_sq2, t_r2, t_y = T(), T(), T(), T()
            t_tmpc = t_e
            t_Sb = [T() for _ in range(4)]
            S.dma("pool", wlr_b[:], wlr.rearrange("(c p) n -> p c n", p=128), writes=[t_wlr])
            for cc in range(4):
                S.op("pool", lambda cc=cc: pool.tensor_copy(g_mask4[:, cc, :], Uincl_f), reads=[t_cst], writes=[t_mask4])
            gblocks = [] if SKIP else [(g, nb) for g in range(2) for nb in range(S_LEN // 512)]
            rot = [0]

            def nextbank():
                rot[0] ^= 1
                return rot[0]

            def gla_inproj_gen(bi):
                g, nb = gblocks[bi]
                par = bi % 2
                t0 = nb * 512
                tok = slice(t0, t0 + 512)
                hr = h_reads(t0, t0 + 512)
                if nb == 0:
                    S.dma("pool", wg[:], wgla[g].rearrange("(c p) n -> p c n", p=128), writes=[t_wg])
                pending = [None]

                def flush():
                    if pending[0] is not None:
                        pending[0]()
                        pending[0] = None

                def group(lhs_fn, rhs_fn, out_fn, evac, extra_reads):
                    bk = nextbank()
                    for c in range(NCH):
                        S.op("pe", lambda c=c, bk=bk: mm(out_fn(bk), lhsT=lhs_fn(c), rhs=rhs_fn(c),
                                                               start=(c == 0), stop=(c == NCH - 1)),
                             reads=hr + extra_reads, writes=[t_ps[bk]], inc=(c == NCH - 1))
                        if c == 3:
                            flush()
                    pending[0] = lambda bk=bk: evac(bk)

                group(lambda c: wg[:, c, 0:128], lambda c: hT[:, c, tok], lambda bk: bank(bk),
                      lambda bk: S.op("act", lambda: act.activation(out=g_qTs[par][:], in_=bank(bk), func=AF.Identity,
                                                                    scale=128 ** -0.5),
                                      reads=[t_ps[bk]], writes=[t_qT[par]]), [t_wg])
                yield
                group(lambda c: wg[:, c, 128:256], lambda c: hT[:, c, tok], lambda bk: bank(bk),
                      lambda bk: S.op("dve", lambda: dve.tensor_copy(g_kTs[par][:], bank(bk)),
                                      reads=[t_ps[bk]], writes=[t_kT[par]]), [t_wg])
                yield
                for ec in range(2):
                    group(lambda c, ec=ec: wg[:, c, 512 + ec * 128:512 + ec * 128 + 128], lambda c: hT[:, c, tok],
                          lambda bk: bank(bk),
                          lambda bk, ec=ec: silu_evac(bank(bk), g_silus[par][:, ec, :], g_tmpa[:], g_tmpb[:], t_tmp,
                                                      [t_ps[bk]], [t_silu[par]]), [t_wg])
                    yield
                for sb in range(4):
                    group(lambda c, sb=sb: hT[:, c, t0 + sb * 128:t0 + sb * 128 + 128], lambda c: wg[:, c, 256:512],
                          lambda bk: bank(bk)[:, 0:256],
                          lambda bk, sb=sb: S.op("dve", lambda: dve.tensor_copy(g_vs[par][:, sb, :], bank(bk)[:, 0:256]),
                                                 reads=[t_ps[bk]], writes=[t_v[par]]), [t_wg])
                    yield
                group(lambda c: wlr_b[:, c, :], lambda c: hT[:, c, tok], lambda bk: bank(bk)[0:16, :],
                      lambda bk: S.op("dve", lambda: dve.tensor_copy(g_lrs[par][:], bank(bk)[0:16, :]),
                                      reads=[t_ps[bk]], writes=[t_lr[par]]), [t_wlr])
                flush()
                yield

            def gpump(gen, n):
                if gen is None:
                    return
                for _ in range(n):
                    try:
                        next(gen)
                    except StopIteration:
                        return

            if gblocks:
                gpump(gla_inproj_gen(0), 100)
            for bi, (g, nb) in enumerate(gblocks):
                par = bi % 2
                g_qT, g_kT, g_silu, g_v, g_lr = g_qTs[par], g_kTs[par], g_silus[par], g_vs[par], g_lrs[par]
                tqT, tkT, tsilu, tv, tlr = t_qT[par], t_kT[par], t_silu[par], t_v[par], t_lr[par]
                nxt = gla_inproj_gen(bi + 1) if bi + 1 < len(gblocks) else None
                t0 = nb * 512
                gsl = slice(g * 128, g * 128 + 128)
                if nb == 0:
                    S.op("dve", lambda: dve.memset(g_Sf[:], 0.0), writes=[t_Sf])
                    S.op("dve", lambda: dve.memset(g_Sb[:, 0, :], 0.0), writes=[t_Sb[0]])
                for cc in range(4):
                    cs = slice(cc * 128, cc * 128 + 128)
                    S.op("pe", lambda cs=cs, cc=cc: mm(
                        bank(2)[:, cs], lhsT=g_lr[0:16, cs], rhs=wa2_f[0:16, gsl], start=(cc == 0), stop=False),
                        reads=[tlr, t_cst], writes=[t_ps[2]], inc=False)
                    S.op("pe", lambda cs=cs, cc=cc: mm(
                        bank(2)[:, cs], lhsT=ones_f[0:1, :], rhs=ba_f[0:1, gsl], start=False, stop=True),
                        reads=[t_cst], writes=[t_ps[2]], inc=(cc == 3))
                gpump(nxt, 1)
                S.op("act", lambda: act.activation(out=g_e[:], in_=bank(2), func=AF.Exp, scale=-1.0),
                     reads=[t_ps[2]], writes=[t_e])
                S.op("act", lambda: act.activation(out=g_sp[:].rearrange("p a b -> p (a b)"), in_=g_e[:], func=AF.Ln, bias=1.0),
                     reads=[t_e], writes=[t_sp])
                for cc in range(4):
                    cs = slice(cc * 128, cc * 128 + 128)
                    S.op("pe", lambda cs=cs, cc=cc: mm(bank(3)[:, cs], lhsT=g_sp[:, cc, :], rhs=Uincl_f,
                                                              start=(cc == 0), stop=True),
                         reads=[t_sp, t_cst], writes=[t_ps[3]], inc=(cc == 3))
                gpump(nxt, 1)
                S.op("act", lambda: act.activation(out=g_Eq[:], in_=bank(3), func=AF.Exp, scale=-1.0 / 16),
                     reads=[t_ps[3]], writes=[t_Eq])
                S.op("act", lambda: act.activation(out=g_Ek[:], in_=bank(3), func=AF.Exp, scale=1.0 / 16),
                     reads=[t_ps[3]], writes=[t_Ek])
                S.op("dve", lambda: dve.tensor_tensor(out=g_ke[:], in0=g_kT[:], in1=g_Ek[:], op=ALU.mult),
                     reads=[tkT, t_Ek], writes=[t_ke])
                for cc in range(4):
                    cs = slice(cc * 128, cc * 128 + 128)
                    S.op("dve", lambda cs=cs, cc=cc: dve.scalar_tensor_tensor(
                        out=g_klT[:, cs], in0=g_kT[:, cs], scalar=g_Eq[:, cc * 128 + 127:cc * 128 + 128], in1=g_Ek[:, cs],
                        op0=ALU.mult, op1=ALU.mult),
                        reads=[tkT, t_Eq, t_Ek], writes=[t_klT])
                S.op("dve", lambda: dve.tensor_tensor(out=g_qe[:], in0=g_qT[:], in1=g_Eq[:], op=ALU.mult),
                     reads=[tqT, t_Eq], writes=[t_qe])
                for cc in range(4):
                    cs = slice(cc * 128, cc * 128 + 128)
                    S.op("pe", lambda cs=cs, cc=cc: mm(bank(4)[:, cs], lhsT=g_klT[:, cs], rhs=ident_b,
                                                              start=(cc == 0), stop=True),
                         reads=[t_klT, t_cst], writes=[t_ps[4]], inc=(cc == 3))
                for cc in range(4):
                    cs = slice(cc * 128, cc * 128 + 128)
                    S.op("pe", lambda cs=cs, cc=cc: mm(bank(5)[:, cs], lhsT=g_ke[:, cs], rhs=g_qe[:, cs],
                                                              start=(cc == 0), stop=True),
                         reads=[t_ke, t_qe], writes=[t_ps[5]], inc=(cc == 3))
                gpump(nxt, 1)
                S.op("act", lambda: act.activation(out=g_kl[:].rearrange("p a b -> p (a b)"), in_=bank(4), func=AF.Copy),
                     reads=[t_ps[4]], writes=[t_kl])
                S.op("dve", lambda: dve.tensor_tensor(out=g_scm[:], in0=bank(5), in1=g_mask4[:].rearrange("p a b -> p (a b)"),
                                                      op=ALU.mult),
                     reads=[t_ps[5], t_mask4], writes=[t_scm])
                for cc in range(4):
                    S.op("pe", lambda cc=cc: mm(
                        bank(2 + cc // 2)[:, (cc % 2) * 256:(cc % 2) * 256 + 256], lhsT=g_kl[:, cc, :], rhs=g_v[:, cc, :],
                        start=(cc % 2 == 0), stop=True),
                        reads=[t_kl, tv], writes=[t_ps[2 + cc // 2]])
                gpump(nxt, 1)
                for cc in range(4):
                    cs = slice(cc * 128, cc * 128 + 128)
                    for ec in range(2):
                        es = slice(ec * 128, ec * 128 + 128)
                        S.op("pe", lambda ec=ec, es=es, cs=cs, cc=cc: mm(
                            bank(6 + ec)[:, cs], lhsT=g_Sb[:, cc, es], rhs=g_qe[:, cs], start=(cc == 0), stop=False),
                            reads=[t_Sb[cc], t_qe], writes=[t_ps[6 + ec]], inc=False)
                        S.op("pe", lambda ec=ec, es=es, cs=cs, cc=cc: mm(
                            bank(6 + ec)[:, cs], lhsT=g_v[:, cc, es], rhs=g_scm[:, cs], start=False, stop=True),
                            reads=[tv, t_scm], writes=[t_ps[6 + ec]])
                    S.op("dve", lambda cc=cc: dve.scalar_tensor_tensor(
                        out=g_Sf[:], in0=g_Sf[:], scalar=g_Eq[:, cc * 128 + 127:cc * 128 + 128],
                        in1=bank(2 + cc // 2)[:, (cc % 2) * 256:(cc % 2) * 256 + 256], op0=ALU.mult, op1=ALU.add),
                        reads=[t_Sf, t_Eq, t_ps[2 + cc // 2]], writes=[t_Sf])
                    S.op("pool", lambda cc=cc: pool.tensor_copy(g_Sb[:, (cc + 1) % 4, :], g_Sf[:]),
                         reads=[t_Sf], writes=[t_Sb[(cc + 1) % 4]])
                    gpump(nxt, 1)
                gpump(nxt, 100)
                for ec in range(2):
                    S.op("act", lambda ec=ec: act.activation(out=g_sq[:, ec, :], in_=bank(6 + ec), func=AF.Square),
                         reads=[t_ps[6 + ec]], writes=[t_sq2])
                for ec in range(2):
                    S.op("pe", lambda ec=ec: mm(bank(5), lhsT=ones_b, rhs=g_sq[:, ec, :], start=(ec == 0), stop=(ec == 1)),
                         reads=[t_sq2, t_cst], writes=[t_ps[5]], inc=(ec == 1))
                S.op("act", lambda: act.activation(out=g_r[:], in_=bank(5), func=AF.Ln, scale=1.0 / 256, bias=EPS),
                     reads=[t_ps[5]], writes=[t_r2])
                S.op("act", lambda: act.activation(out=g_r[:], in_=g_r[:], func=AF.Exp, scale=-0.5),
                     reads=[t_r2], writes=[t_r2])
                for ec in range(2):
                    S.op("dve", lambda ec=ec: dve.scalar_tensor_tensor(
                        out=g_tmpc[:], in0=bank(6 + ec), scalar=cols_f[:, 48 + ec:49 + ec], in1=g_r[:],
                        op0=ALU.mult, op1=ALU.mult),
                        reads=[t_ps[6 + ec], t_r2, t_cst], writes=[t_tmpc])
                    S.op("dve", lambda ec=ec: dve.tensor_tensor(out=g_y[:, ec, :], in0=g_tmpc[:], in1=g_silu[:, ec, :], op=ALU.mult),
                         reads=[t_tmpc, tsilu], writes=[t_y])
                S.dma("sp", mg_own[t0 // EB, g * 256:(g + 1) * 256, t0 % EB:t0 % EB + 512].rearrange("(e p) t -> p e t", p=128), g_y[:],
                      reads=[t_y])
        S.barrier()

        NKB = S_LEN // 128
        with ExitStack() as _es2:
            def A2(name, shape, dt):
                return _es2.enter_context(nc.sbuf_tensor(name, shape, dt))
            ws = A2("ws", [128, NCH, 512], BF16)
            s_kT = A2("s_kT", [128, S_LEN], BF16)
            s_v = A2("s_v", [128, NKB, 128], BF16)
            s_qTs = (A2("s_qTa", [128, TQ], BF16), A2("s_qTb", [128, TQ], BF16))
            s_gss = (A2("s_gsa", [128, TQ], BF16), A2("s_gsb", [128, TQ], BF16))
            s_ta = A2("s_ta", [128, 512], F32)
            s_tb = A2("s_tb", [128, 512], F32)
            e1s = (A2("s_e1a", [128, TQ], F32), A2("s_e1b", [128, TQ], F32), A2("s_e1c", [128, TQ], F32))
            sps = (A2("s_spa", [128, TQ], BF16), A2("s_spb", [128, TQ], BF16))
            gs_ = (A2("s_ga", [128, TQ], BF16), A2("s_gb", [128, TQ], BF16))
            ws_ = (A2("s_wa", [128, TQ], BF16), A2("s_wb", [128, TQ], BF16))
            s_y = A2("s_y", [128, TQ], BF16)
            t_ws, t_sy, t_stmp, t_msown = T(), T(), T(), T()
            t_skT = [T() for _ in range(S_LEN // 512)]
            t_sv = [T() for _ in range(S_LEN // 512)]
            t_sqT, t_sgs = (T(), T()), (T(), T())
            t_e1, t_sps, t_gs_, t_ws_ = (T(), T(), T()), (T(), T()), (T(), T()), (T(), T())
            ZB, BB, OB = 0, 2, 4
            blocks = [] if SKIP else [(hd, tb) for hd in range(4) for tb in range(S_LEN // TQ)]

            def inproj_gen(bi):
                hd, tb = blocks[bi]
                s_qT, s_gs = s_qTs[bi % 2], s_gss[bi % 2]
                tq_, tg_ = t_sqT[bi % 2], t_sgs[bi % 2]
                q0 = tb * TQ
                if tb == 0:
                    S.dma("pool", ws[:], wsb[hd].rearrange("(c p) n -> p c n", p=128), writes=[t_ws])
                pending = [None]

                def flush():
                    if pending[0] is not None:
                        pending[0]()
                        pending[0] = None

                for half in range(NBK):
                    t0 = q0 + half * 512
                    tok = slice(t0, t0 + 512)
                    loc = slice(half * 512, half * 512 + 512)
                    hr = h_reads(t0, t0 + 512)
                    for (col0, bk_, evac) in (
                        (0, 6, lambda loc=loc: S.op("dve", lambda: dve.tensor_scalar_mul(
                            out=s_qT[:, loc], in0=bank(6), scalar1=128 ** -0.5), reads=[t_ps[6]], writes=[tq_])),
                        (128, 7, lambda tok=tok, t0=t0: S.op("dve", lambda: dve.tensor_copy(s_kT[:, tok], bank(7)),
                                                             reads=[t_ps[7]], writes=[t_skT[t0 // 512]])),
                        (384, 6, lambda loc=loc: silu_evac(bank(6), s_gs[:, loc], s_ta[:], s_tb[:], t_stmp,
                                                           [t_ps[6]], [tg_])),
                    ):
                        for c in range(NCH):
                            S.op("pe", lambda c=c, col0=col0, bk_=bk_: mm(
                                bank(bk_), lhsT=ws[:, c, col0:col0 + 128], rhs=hT[:, c, tok],
                                start=(c == 0), stop=(c == NCH - 1)),
                                reads=hr + [t_ws], writes=[t_ps[bk_]], inc=(c == NCH - 1))
                            if c % 4 == 3:
                                if c == 3:
                                    flush()
                                yield
                        pending[0] = evac
                    for sb in range(4):
                        kb = (t0 // 128) + sb
                        for c in range(NCH):
                            S.op("pe", lambda c=c, kb=kb, sb=sb: mm(
                                bank(7)[:, sb * 128:sb * 128 + 128], lhsT=hT[:, c, kb * 128:kb * 128 + 128],
                                rhs=ws[:, c, 256:384], start=(c == 0 and sb == 0), stop=(c == NCH - 1)),
                                reads=hr + [t_ws], writes=[t_ps[7]], inc=(c == NCH - 1 and sb == 3))
                            if c % 4 == 3:
                                if c == 3 and sb == 0:
                                    flush()
                                yield
                    pending[0] = (lambda t0=t0: S.op("dve", lambda: dve.tensor_copy(
                        s_v[:, t0 // 128:t0 // 128 + 4, :], bank(7).rearrange("p (a b) -> p a b", a=4)),
                        reads=[t_ps[7]], writes=[t_sv[t0 // 512]]))
                flush()
                yield

            NYIELD = NBK * 28 + 1

            def pump(gen, n):
                if gen is None:
                    return
                for _ in range(n):
                    try:
                        next(gen)
                    except StopIteration:
                        return

            if blocks:
                pump(inproj_gen(0), 10 ** 6)
                for k in range(4):
                    issue_cc(mg_own[k], mg_all[k])
                for nm, src, dst in (("wo", wo, wo16), ("wgt", wgt, wgt16)):
                    for j in range(4):
                        S.dma("pool", dst[j * 512:(j + 1) * 512, :], src[j * 512:(j + 1) * 512, :],
                              writes=[t_w16[nm][j]])
            for bi, (hd, tb) in enumerate(blocks):
                    q0 = tb * TQ
                    s_qT, s_gs = s_qTs[bi % 2], s_gss[bi % 2]
                    tq_, tg_ = t_sqT[bi % 2], t_sgs[bi % 2]
                    nxt = inproj_gen(bi + 1) if bi + 1 < len(blocks) else None
                    kbs = list(range((tb + 1) * (TQ // 128) - 1, -1, -1))
                    P = len(kbs)
                    npump = (NYIELD + P - 1) // P
                    startedB = [False] * NBK
                    startedO = [False] * NBK

                    def geom(p):
                        kb = kbs[p]
                        off = kb * 128 - q0
                        lo = max(0, off)
                        segs = []
                        for bki in range(NBK):
                            c0 = max(lo, bki * 512)
                            c1 = (bki + 1) * 512
                            if c0 < c1:
                                segs.append((bki, c0, c1))
                        return kb, off, lo, segs

                    def emit_Z(p):
                        kb, off, lo, segs = geom(p)
                        for (bki, c0, c1) in segs:
                            S.op("pe", lambda bki=bki, c0=c0, c1=c1, kb=kb: mm(
                                bank(ZB + bki)[:, c0 - bki * 512:c1 - bki * 512],
                                lhsT=s_kT[:, kb * 128:kb * 128 + 128], rhs=s_qT[:, c0:c1], start=True, stop=True),
                                reads=[t_skT[kb // 4], tq_], writes=[t_ps[ZB + bki]])

                    def zb_reads(segs, base):
                        return [t_ps[base + bki] for (bki, _, _) in segs]

                    def emit_E1(p):
                        kb, off, lo, segs = geom(p)
                        e1, te1 = e1s[p % 3], t_e1[p % 3]
                        S.op("act", lambda: act.activation(
                            out=e1[:, lo:TQ], in_=PS[:, ZB * 512 + lo:ZB * 512 + TQ], func=AF.Exp),
                            reads=zb_reads(segs, ZB), writes=[te1])
                        if off >= 0:
                            S.op("dve", lambda: dve.tensor_tensor(
                                out=e1[:, off:off + 128], in0=e1[:, off:off + 128], in1=Ustr_f, op=ALU.mult),
                                reads=[te1, t_cst], writes=[te1])

                    emit_Z(0)
                    emit_E1(0)
                    if P > 1:
                        emit_Z(1)
                    for p in range(P + 1):
                        if p >= 1:
                            kbp, offp, lop, segsp = geom(p - 1)
                            e1p, te1p = e1s[(p - 1) % 3], t_e1[(p - 1) % 3]
                            spp, tspp = sps[(p - 1) % 2], t_sps[(p - 1) % 2]
                            gp, tgp = gs_[(p - 1) % 2], t_gs_[(p - 1) % 2]
                            wp_, twp = ws_[(p - 1) % 2], t_ws_[(p - 1) % 2]
                            S.op("act", lambda gp=gp, lop=lop: act.activation(
                                out=gp[:, lop:TQ], in_=PS[:, BB * 512 + lop:BB * 512 + TQ], func=AF.Exp, scale=-1.0),
                                reads=zb_reads(segsp, BB), writes=[tgp])
                            if p - 1 < P - 1:
                                for (bki, c0, c1) in segsp:
                                    S.op("pe", lambda bki=bki, c0=c0, c1=c1, spp=spp: mm(
                                        bank(BB + bki)[:, c0 - bki * 512:c1 - bki * 512], lhsT=Ustr_b, rhs=spp[:, c0:c1],
                                        start=False, stop=True),
                                        reads=[tspp, t_cst], writes=[t_ps[BB + bki]])
                        if p < P:
                            kb, off, lo, segs = geom(p)
                            e1, te1 = e1s[p % 3], t_e1[p % 3]
                            sp_, tsp = sps[p % 2], t_sps[p % 2]
                            S.op("act", lambda e1=e1, sp_=sp_, lo=lo: act.activation(
                                out=sp_[:, lo:TQ], in_=e1[:, lo:TQ], func=AF.Ln, bias=1.0),
                                reads=[te1], writes=[tsp])
                            for (bki, c0, c1) in segs:
                                S.op("pe", lambda bki=bki, c0=c0, c1=c1, sp_=sp_, st=(not startedB[bki]): mm(
                                    bank(BB + bki)[:, c0 - bki * 512:c1 - bki * 512], lhsT=Lincl_b, rhs=sp_[:, c0:c1],
                                    start=st, stop=True),
                                    reads=[tsp, t_cst], writes=[t_ps[BB + bki]])
                                startedB[bki] = True
                        if p + 1 < P:
                            emit_E1(p + 1)
                        if p + 2 < P:
                            emit_Z(p + 2)
                        pump(nxt, npump // 2)
                        if p >= 1:
                            S.op("dve", lambda wp_=wp_, e1p=e1p, gp=gp, lop=lop: dve.tensor_tensor(
                                out=wp_[:, lop:TQ], in0=e1p[:, lop:TQ], in1=gp[:, lop:TQ], op=ALU.mult),
                                reads=[te1p, tgp], writes=[twp])
                            for (bki, c0, c1) in segsp:
                                S.op("pe", lambda bki=bki, c0=c0, c1=c1, wp_=wp_, kbp=kbp, st=(not startedO[bki]): mm(
                                    bank(OB + bki)[:, c0 - bki * 512:c1 - bki * 512], lhsT=s_v[:, kbp, :], rhs=wp_[:, c0:c1],
                                    start=st, stop=True),
                                    reads=[t_sv[kbp // 4], twp], writes=[t_ps[OB + bki]])
                                startedO[bki] = True
                        pump(nxt, npump - npump // 2)
                    pump(nxt, 10 ** 6)
                    S.op("dve", lambda: dve.tensor_tensor(out=s_y[:], in0=PS[:, OB * 512:OB * 512 + TQ], in1=s_gs[:], op=ALU.mult),
                         reads=[t_ps[OB + i] for i in range(NBK)] + [tg_], writes=[t_sy])
                    S.dma("sp", ms_own[hd, q0 // HS, :, q0 % HS:q0 % HS + TQ], s_y[:], reads=[t_sy], writes=[t_msown])
                    if tb == S_LEN // TQ - 1 and hd < 3:
                        S._wait("pool", S._deps("pool", [t_msown], []))
                        issue_cc(ms_own[hd].rearrange("hh p t -> (hh p) t"), ms_all[hd])
        S.barrier(skip=("cc",))

    BT = 256
    if mode == "AB":
        par = nc.sync.partition_id() % 2
        mg_v = mg_all.rearrange("k (c p) t -> k p c t", p=128)
        ms_v = ms_all.rearrange("h q t -> (h q) t").rearrange("(hr hh p) t -> hh p hr t", hr=8, hh=2)
    else:
        mixin_v = mixin.rearrange("(c p) t -> p c t", p=128)
    with ExitStack() as _es3:
        wo_b = _es3.enter_context(nc.sbuf_tensor("wo_b", [128, NCH, D], BF16))
        wgt_b = _es3.enter_context(nc.sbuf_tensor("wgt_b", [128, NCH, D], BF16))
        wp_b = _es3.enter_context(nc.sbuf_tensor("wp_b", [128, 2, D], BF16))
        mxa = _es3.enter_context(nc.sbuf_tensor("mxa", [128, NCH, BT], BF16))
        xra = _es3.enter_context(nc.sbuf_tensor("xra", [128, NCH, BT], F32))
        pbts = (_es3.enter_context(nc.sbuf_tensor("pbt", [128, 2, BT], BF16)), _es3.enter_context(nc.sbuf_tensor("pbt2", [128, 2, BT], BF16)))
        m1 = _es3.enter_context(nc.sbuf_tensor("m1", [128, NCH, BT], F32))
        sq3 = _es3.enter_context(nc.sbuf_tensor("sq3", [128, NCH, BT], BF16))
        r3 = _es3.enter_context(nc.sbuf_tensor("r3", [128, BT], F32))
        gte = _es3.enter_context(nc.sbuf_tensor("gte", [128, 2, BT], F32))
        oo = _es3.enter_context(nc.sbuf_tensor("oo", [128, 4, BT], F32))
        t_wo = [T() for _ in range(4)]
        t_wgt = [T() for _ in range(4)]
        t_wp = T()
        for j in range(4):
            S.dma("sp", wo_b[:, 4 * j:4 * j + 4, :],
                  wo16[j * 512:(j + 1) * 512, :].rearrange("(c p) n -> p c n", p=128),
                  reads=[t_w16["wo"][j]], writes=[t_wo[j]])
        for j in range(4):
            S.dma("sp", wgt_b[:, 4 * j:4 * j + 4, :],
                  wgt16[j * 512:(j + 1) * 512, :].rearrange("(c p) n -> p c n", p=128),
                  reads=[t_w16["wgt"][j]], writes=[t_wgt[j]])
        S.dma("pool", wp_b[:], wp.rearrange("(c p) n -> p c n", p=128), writes=[t_wp])
        if not SKIP:
            issue_cc(ms_own[3].rearrange("hh p t -> (hh p) t"), ms_all[3])
        mxs, t_mx = (mxa, mxa), (T(),) * 2
        xrs, t_xr = (xra, xra), (T(),) * 2
        h1b = sq3
        t_pb, t_m1, t_sq3, t_r3 = T(), T(), T(), T()
        t_h1b = t_sq3
        t_gte = (T(), T())
        t_oo = [T() for _ in range(4)]
        pTo_v = pTo.rearrange("(c p) t -> p c t", p=128)
        outT_v = outT.rearrange("(c p) t -> p c t", p=128)
        rot3 = [0]

        def nb3():
            rot3[0] = (rot3[0] + 1) % 8
            return rot3[0]

        t_m1c = [T() for _ in range(NCH)]
        t_mxp = [T() for _ in range(5)]
        t_sqc = [T() for _ in range(NCH)]
        t_pbs = (T(), T())
        NB3 = HS // BT

        def emit_loads(nb):
            tsl = slice(nb * BT, (nb + 1) * BT)
            if mode == "AB":
                jj = (nb * BT) // EB
                cc0 = (nb * BT) % EB
                S.dma("sp", mxa[:, 0:8, :], mg_v[bass.ds(par * 2 + jj, 1), :, :, cc0:cc0 + BT].rearrange("o p c t -> p (o c) t"),
                      reads=[t_mix], writes=[t_mxp[0]])
                S.dma("sp", mxa[:, 8:16, :],
                      ms_v[bass.ds(par, 1), :, :, nb * BT:(nb + 1) * BT].rearrange("o p r t -> p (o r) t"),
                      reads=[t_mix], writes=[t_mxp[1]])
            else:
                S.dma("sp", mxa[:], mixin_v[:, :, tsl], writes=[t_mx[0]])
            S.dma("sp", xra[:], xTo[nb].rearrange("p (c t) -> p c t", c=NCH), writes=[t_xr[0]])
            S.dma("pool", pbts[nb % 2][:], pTo_v[:, :, tsl], writes=[t_pbs[nb % 2]])

        for nb in range(NB3):
            emit_loads(nb)
            tsl = slice(nb * BT, (nb + 1) * BT)
            mx, tmx = mxa, t_mx[0]
            xr, txr = xra, t_xr[0]
            pbt, t_pb = pbts[nb % 2], t_pbs[nb % 2]
            for oc in range(NCH):
                bk = nb3()
                for c in range(NCH):
                    S.op("pe", lambda c=c, oc=oc, bk=bk: mm(
                        bank(bk)[:, 0:BT], lhsT=wo_b[:, c, oc * 128:(oc + 1) * 128], rhs=mx[:, c, :],
                        start=(c == 0), stop=(c == NCH - 1)),
                        reads=[t_wo[c // 4], tmx] + t_mxp, writes=[t_ps[bk]], inc=(c == NCH - 1))
                S.op("act", lambda oc=oc, bk=bk: act.activation(out=sq3[:, oc, :], in_=bank(bk)[:, 0:BT], func=AF.Square),
                     reads=[t_ps[bk]], writes=[t_sqc[oc]])
                S.op("dve", lambda oc=oc, bk=bk: dve.tensor_scalar_mul(
                    out=m1[:, oc, :], in0=bank(bk)[:, 0:BT], scalar1=cols_f[:, 16 + oc:17 + oc]),
                    reads=[t_ps[bk], t_cst], writes=[t_m1c[oc]])
            bk = nb3()
            for c in range(NCH):
                S.op("pe", lambda c=c, bk=bk: mm(bank(bk)[:, 0:BT], lhsT=ones_b, rhs=sq3[:, c, :],
                                                       start=(c == 0), stop=(c == NCH - 1)),
                     reads=[t_sqc[c], t_cst], writes=[t_ps[bk]], inc=(c == NCH - 1))
            S.op("act", lambda bk=bk: act.activation(out=r3[:], in_=bank(bk)[:, 0:BT], func=AF.Ln, scale=1.0 / D, bias=EPS),
                 reads=[t_ps[bk]], writes=[t_r3])
            S.op("act", lambda: act.activation(out=r3[:], in_=r3[:], func=AF.Exp, scale=-0.5),
                 reads=[t_r3], writes=[t_r3])
            for oc in range(NCH):
                S.op("dve", lambda oc=oc: dve.tensor_tensor(out=m1[:, oc, :], in0=m1[:, oc, :], in1=r3[:], op=ALU.mult),
                     reads=[t_m1c[oc], t_r3], writes=[t_m1c[oc]])
                S.op("pool", lambda oc=oc: pool.tensor_tensor(out=m1[:, oc, :], in0=m1[:, oc, :], in1=xr[:, oc, :], op=ALU.add),
                     reads=[t_m1c[oc], txr], writes=[t_m1c[oc]])
                S.op("act", lambda oc=oc: act.activation(out=h1b[:, oc, :], in_=m1[:, oc, :], func=AF.Copy),
                     reads=[t_m1c[oc]], writes=[t_sqc[oc]])
            for oc in range(NCH):
                bk = nb3()
                for c in range(NCH):
                    S.op("pe", lambda c=c, oc=oc, bk=bk: mm(
                        bank(bk)[:, 0:BT], lhsT=wgt_b[:, c, oc * 128:(oc + 1) * 128], rhs=h1b[:, c, :],
                        start=(c == 0), stop=(c == NCH - 1)),
                        reads=[t_wgt[c // 4], t_sqc[c]], writes=[t_ps[bk]], inc=(c == NCH - 1))
                gt, tgt = gte[:, oc % 2, :], t_gte[oc % 2]
                S.op("act", lambda oc=oc, bk=bk, gt=gt: act.activation(
                    out=gt, in_=bank(bk)[:, 0:BT], func=AF.Sigmoid, bias=cols_f[:, 32 + oc:33 + oc]),
                    reads=[t_ps[bk], t_cst], writes=[tgt])
                bk2 = nb3()
                for c in range(2):
                    S.op("pe", lambda c=c, oc=oc, bk2=bk2: mm(
                        bank(bk2)[:, 0:BT], lhsT=wp_b[:, c, oc * 128:(oc + 1) * 128], rhs=pbt[:, c, :],
                        start=(c == 0), stop=(c == 1)),
                        reads=[t_wp, t_pb], writes=[t_ps[bk2]], inc=(c == 1))
                S.op("dve", lambda oc=oc, bk2=bk2, gt=gt: dve.tensor_tensor(
                    out=gt, in0=gt, in1=bank(bk2)[:, 0:BT], op=ALU.mult),
                    reads=[tgt, t_ps[bk2]], writes=[tgt])
                S.op("dve", lambda oc=oc, gt=gt: dve.tensor_tensor(
                    out=oo[:, oc % 4, :], in0=gt, in1=m1[:, oc, :], op=ALU.add),
                    reads=[tgt, t_m1c[oc]], writes=[t_oo[oc % 4]])
                if oc % 4 == 3:
                    S.dma("pool", outT_v[:, oc - 3:oc + 1, tsl], oo[:], reads=t_oo)
        S.barrier()
    return nc


_PROG = {}


def kernel(x, p, g_pre, w_in, w_a2, b_a, g_gla_head, w_out, g_post, w_ple_gate, b_ple_gate, w_ple_proj):
    x = np.asarray(x, np.float32)
    B, S_LEN, _ = x.shape
    HS = S_LEN // 2
    f = lambda a: np.ascontiguousarray(np.asarray(a, np.float32))
    p, g_pre, w_in, w_a2, b_a = f(p), f(g_pre), f(w_in), f(w_a2), f(b_a)
    g_gla_head, w_out, g_post = f(g_gla_head), f(w_out), f(g_post)
    w_ple_gate, b_ple_gate, w_ple_proj = f(w_ple_gate), f(b_ple_gate), f(w_ple_proj)
    W = w_in[0]
    GQ, GK, GV, GG, LR, SQ, SK, SV, SG = 0, 512, 1024, 2048, 3072, 3088, 4112, 5136, 6160

    def colv(v):
        return v.reshape(-1, 128).T

    ii = np.arange(128)
    cst = np.zeros((128, 5, 128), np.float32)
    cst[:, 0, :] = (ii[:, None] == ii[None, :])
    cst[:, 1, :] = (ii[:, None] <= ii[None, :])
    cst[:, 2, :] = (ii[:, None] < ii[None, :])
    cst[:, 3, :] = (ii[:, None] >= ii[None, :])
    cst[:, 4, :] = 1.0
    cols = np.zeros((128, 64), np.float32)
    cols[:, 0:16] = colv(g_pre[0])
    cols[:, 16:32] = colv(g_post[0])
    cols[:, 32:48] = colv(b_ple_gate[0])
    cols[:, 48:50] = colv(g_gla_head[0])
    in_maps = []
    for core in range(8):
        b, hh = core // 2, core % 2
        wsb = np.stack([np.concatenate([W[:, SQ + 128 * H:SQ + 128 * H + 128], W[:, SK + 128 * H:SK + 128 * H + 128],
                                        W[:, SV + 128 * H:SV + 128 * H + 128], W[:, SG + 128 * H:SG + 128 * H + 128]], axis=1)
                        for H in range(4 * hh, 4 * hh + 4)])
        wgla = np.stack([np.concatenate([W[:, GQ + 128 * G:GQ + 128 * G + 128], W[:, GK + 128 * G:GK + 128 * G + 128],
                                         W[:, GV + 256 * G:GV + 256 * G + 256], W[:, GG + 256 * G:GG + 256 * G + 256]], axis=1)
                         for G in range(2 * hh, 2 * hh + 2)])
        xtile = np.ascontiguousarray(x[b].reshape(S_LEN // 256, 256, NCH, 128).transpose(0, 3, 2, 1)).reshape(
            S_LEN // 256, 128, NCH * 256)
        wo_perm = np.concatenate([w_out[0][0:1024]] + [w_out[0][1024 + (4 * r + h) * 128:1024 + (4 * r + h) * 128 + 128]
                                                       for h in range(4) for r in range(2)], axis=0)
        in_maps.append({
            "xT": xtile,
            "xTo": np.ascontiguousarray(xtile[hh * (HS // 256):(hh + 1) * (HS // 256)]),
            "pTo": np.ascontiguousarray(p[0, b, hh * HS:(hh + 1) * HS].T),
            "wsb": np.ascontiguousarray(wsb),
            "wgla": np.ascontiguousarray(wgla),
            "wlr": np.ascontiguousarray(W[:, LR:LR + 16]),
            "wa2": np.ascontiguousarray(w_a2[0][:, 256 * hh:256 * hh + 256]),
            "ba": np.ascontiguousarray(b_a[0][None, 256 * hh:256 * hh + 256]),
            "cols": cols,
            "wo": np.ascontiguousarray(wo_perm),
            "wgt": w_ple_gate[0],
            "wp": w_ple_proj[0],
            "cst": cst,
        })
    if S_LEN not in _PROG:
        _PROG[S_LEN] = build_program(S_LEN, "AB")
    res = run_bass_kernel_spmd(_PROG[S_LEN], in_maps, core_ids=list(range(8)))
    out = np.empty((B, S_LEN, D), np.float32)
    for core in range(8):
        b, hh = core // 2, core % 2
        out[b, hh * HS:(hh + 1) * HS, :] = res.results[core]["outT"].T
    return out
```

```python
from contextlib import ExitStack
import numpy as np
import concourse.bass as bass
import concourse.mybir as mybir
from concourse.bass_utils import run_bass_kernel_spmd

F32 = mybir.dt.float32
BF16 = mybir.dt.bfloat16
AF = mybir.ActivationFunctionType
ALU = mybir.AluOpType

D = 2048
NCH = 16
EPS = 1e-6
TQ = 1024
NBK = TQ // 512


class T:
    __slots__ = ("w", "r", "x")

    def __init__(self, x=False):
        self.w = None
        self.r = []
        self.x = x


class Sched:
    ENG = ("pe", "act", "dve", "pool", "sp")

    def __init__(self, nc, n_dma_sems=28):
        self.nc = nc
        self.e = dict(pe=nc.tensor, act=nc.scalar, dve=nc.vector, pool=nc.gpsimd, sp=nc.sync)
        self.sems = {}
        self.cnt = {}
        for k in self.ENG:
            self.sems[k] = nc.alloc_semaphore("s_" + k)
            self.cnt[k] = 0
        self.dma_keys = []
        for i in range(n_dma_sems):
            k = "d%d" % i
            self.sems[k] = nc.alloc_semaphore("s_" + k)
            self.cnt[k] = 0
            self.dma_keys.append(k)
        self.sems["cc"] = nc.alloc_semaphore("s_cc")
        self.cnt["cc"] = 0
        self.dma_rr = 0
        self.seen = {k: {} for k in self.ENG}

    def _deps(self, eng, reads, writes):
        need = {}

        def add(d, same_ok):
            if d is None:
                return
            k, v = d
            if k == eng and same_ok:
                return
            if need.get(k, 0) < v:
                need[k] = v
        for t in reads:
            add(t.w, False)
            if t.x:
                for d in t.r:
                    add(d, True)
        for t in writes:
            add(t.w, True)
            for d in t.r:
                add(d, False)
        return need

    def _wait(self, eng, need):
        seen = self.seen[eng]
        for k, v in need.items():
            if seen.get(k, 0) < v:
                self.e[eng].wait_ge(self.sems[k], v)
                seen[k] = v

    def _record(self, d, reads, writes):
        for t in reads:
            t.r.append(d)
        for t in writes:
            t.w = d
            t.r = []

    def op(self, eng, fn, reads=(), writes=(), inc=True):
        self._wait(eng, self._deps(eng, reads, writes))
        ins = fn()
        if inc:
            self.cnt[eng] += 1
            ins.then_inc(self.sems[eng], 1)
            seq = self.cnt[eng]
        else:
            seq = self.cnt[eng] + 1
        self._record((eng, seq), reads, writes)
        return ins

    def dma(self, eng, out, in_, reads=(), writes=()):
        self._wait(eng, self._deps(eng, reads, writes))
        k = self.dma_keys[self.dma_rr]
        self.dma_rr = (self.dma_rr + 1) % len(self.dma_keys)
        ins = self.e[eng].dma_start(out=out, in_=in_)
        self.cnt[k] += 16
        ins.then_inc(self.sems[k], 16)
        self._record((k, self.cnt[k]), reads, writes)
        return ins

    def barrier(self, skip=()):
        for eng in self.ENG:
            need = {k: v for k, v in self.cnt.items() if v > 0 and k not in skip}
            self._wait(eng, need)


def build_program(S_LEN, mode="AB"):
    STOP = 9
    P3 = 9
    HS = S_LEN // 2
    nc = bass.Bass("TRN2", target_bir_lowering=False)
    S = Sched(nc)
    pe, act, dve, pool = nc.tensor, nc.scalar, nc.vector, nc.gpsimd

    def mm(out, **kw):
        return pe.matmul(out, skip_group_check=True, **kw)

    def din(name, shape, dt=F32):
        return nc.dram_tensor(name, shape, dt, kind="ExternalInput").ap()

    xT = din("xT", [S_LEN // 256, 128, NCH * 256])
    xTo = din("xTo", [HS // 256, 128, NCH * 256])
    pTo = din("pTo", [256, HS])
    wsb = din("wsb", [4, D, 512])
    wgla = din("wgla", [2, D, 768])
    wlr = din("wlr", [D, 16])
    wa2 = din("wa2", [16, 256])
    ba = din("ba", [1, 256])
    cols = din("cols", [128, 64])
    wo = din("wo", [D, D])
    wgt = din("wgt", [D, D])
    wp = din("wp", [256, D])
    cst = din("cst", [128, 5, 128])
    outT = nc.dram_tensor("outT", [D, HS], F32, kind="ExternalOutput").ap()
    if mode == "A":
        mix_own = nc.dram_tensor("mix_own", [4, 1024, S_LEN // 4], BF16, kind="ExternalOutput").ap()
    else:
        mix_own = nc.dram_tensor("mix_own", [4, 1024, S_LEN // 4], BF16).ap()
    mix_all = nc.dram_tensor("mix_all", [4, 2048, S_LEN // 4], BF16).ap()
    mg_own = nc.dram_tensor("mg_own", [4, 512, S_LEN // 4], BF16).ap()
    mg_all = nc.dram_tensor("mg_all", [4, 1024, S_LEN // 4], BF16).ap()
    ms_own = nc.dram_tensor("ms_own", [4, 2, 128, HS], BF16).ap()
    ms_all = nc.dram_tensor("ms_all", [4, 512, HS], BF16).ap()
    GROUPS = [[0, 1], [2, 3], [4, 5], [6, 7]]
    wo16 = nc.dram_tensor("wo16", [D, D], BF16).ap()
    wgt16 = nc.dram_tensor("wgt16", [D, D], BF16).ap()
    t_w16 = {"wo": [T() for _ in range(4)], "wgt": [T() for _ in range(4)]}
    t_mix = T()

    def issue_cc(src, dst):
        pool.collective_compute("AllGather", ALU.bypass, replica_groups=GROUPS, ins=[src], outs=[dst]
                                ).then_inc(S.sems["cc"], 1)
        S.cnt["cc"] += 1
        t_mix.w = ("cc", S.cnt["cc"])
    EB = S_LEN // 4
    if mode == "B":
        mixin = din("mixin", [2048, HS], BF16)
    SKIP = (mode == "B")

    cst_f = nc.alloc_sbuf_tensor("cst_f", [128, 5, 128], F32)
    cst_b = nc.alloc_sbuf_tensor("cst_b", [128, 5, 128], BF16)
    cols_f = nc.alloc_sbuf_tensor("cols_f", [128, 64], F32)
    wa2_f = nc.alloc_sbuf_tensor("wa2_f", [16, 256], F32)
    ba_f = nc.alloc_sbuf_tensor("ba_f", [1, 256], F32)
    t_cst = T()
    S.dma("sp", cst_f[:], cst[:, :, :], writes=[t_cst])
    S.dma("sp", cols_f[:], cols[:, :], writes=[t_cst])
    S.dma("sp", wa2_f[:], wa2[:, :], writes=[t_cst])
    S.dma("sp", ba_f[:], ba[:, :], writes=[t_cst])
    S.op("dve", lambda: dve.tensor_copy(cst_b[:], cst_f[:]), reads=[t_cst], writes=[t_cst])
    ident_b = cst_b[:, 0, :]
    Uincl_f = cst_f[:, 1, :]
    Ustr_f = cst_f[:, 2, :]
    Ustr_b = cst_b[:, 2, :]
    Lincl_b = cst_b[:, 3, :]
    ones_b = cst_b[:, 4, :]
    ones_f = cst_f[:, 4, :]

    PS = nc.alloc_psum_tensor("ps", [128, 4096], F32)
    t_ps = [T(True) for _ in range(8)]

    def bank(i):
        return PS[:, i * 512:(i + 1) * 512]

    with nc.sbuf_tensor("hT", [128, NCH, S_LEN], BF16) as hT:
        NB0 = S_LEN // 256
        NLOOP0 = 0 if SKIP else NB0
        t_hT = [T() for _ in range(NB0)]
        t_hTp = [T() for _ in range(NB0)]

        def h_reads(t0, t1):
            rng = range(t0 // 256, (t1 + 255) // 256)
            return [t_hT[i] for i in rng] + [t_hTp[i] for i in rng]

        with ExitStack() as _es0:
            xb0 = _es0.enter_context(nc.sbuf_tensor("xb0", [128, NCH, 256], F32))
            xb1 = _es0.enter_context(nc.sbuf_tensor("xb1", [128, NCH, 256], F32))
            sq0 = _es0.enter_context(nc.sbuf_tensor("sq0", [128, NCH, 256], BF16))
            r0a = _es0.enter_context(nc.sbuf_tensor("r0", [128, 256], F32))
            r0b = _es0.enter_context(nc.sbuf_tensor("r1", [128, 256], F32))
            p0tmp = (_es0.enter_context(nc.sbuf_tensor("p0ta", [128, 256], F32)), _es0.enter_context(nc.sbuf_tensor("p0tb", [128, 256], F32)))
            t_p0tmp = (T(), T())
            xbs = (xb0, xb1)
            t_xb = (T(), T())
            t_sq = T()
            rs = (r0a, r0b)
            t_r = (T(), T())
            for nb in range(NLOOP0):
                xb = xbs[nb % 2]
                txb = t_xb[nb % 2]
                r = rs[nb % 2]
                tr = t_r[nb % 2]
                tsl = slice(nb * 256, (nb + 1) * 256)
                S.dma("sp", xb[:], xT[nb].rearrange("p (c t) -> p c t", c=NCH), writes=[txb])
                S.op("act", lambda xb=xb: act.activation(out=sq0[:], in_=xb[:], func=AF.Square),
                     reads=[txb], writes=[t_sq])
                bk = nb % 2
                for c in range(NCH):
                    S.op("pe", lambda c=c, bk=bk: mm(bank(bk)[:, 0:256], lhsT=ones_b, rhs=sq0[:, c, :],
                                                           start=(c == 0), stop=(c == NCH - 1)),
                         reads=[t_sq, t_cst], writes=[t_ps[bk]], inc=(c == NCH - 1))
                S.op("act", lambda r=r, bk=bk: act.activation(out=r[:], in_=bank(bk)[:, 0:256], func=AF.Ln,
                                                               scale=1.0 / D, bias=EPS),
                     reads=[t_ps[bk]], writes=[tr])
                S.op("act", lambda r=r: act.activation(out=r[:], in_=r[:], func=AF.Exp, scale=-0.5),
                     reads=[tr], writes=[tr])
                for c in range(NCH):
                    if c % 3 == 2:
                        k = (c // 3) % 2
                        S.op("act", lambda c=c, xb=xb, k=k: act.activation(
                            out=p0tmp[k][:], in_=xb[:, c, :], func=AF.Identity, scale=cols_f[:, c:c + 1]),
                            reads=[txb, t_cst], writes=[t_p0tmp[k]])
                        S.op("pool", lambda c=c, r=r, k=k: pool.tensor_tensor(
                            out=hT[:, c, tsl], in0=p0tmp[k][:], in1=r[:], op=ALU.mult),
                            reads=[t_p0tmp[k], tr], writes=[t_hTp[nb]])
                        continue
                    S.op("dve", lambda c=c, xb=xb, r=r: dve.scalar_tensor_tensor(
                        out=hT[:, c, tsl], in0=xb[:, c, :], scalar=cols_f[:, c:c + 1], in1=r[:],
                        op0=ALU.mult, op1=ALU.mult),
                        reads=[txb, tr, t_cst], writes=[t_hT[nb]])
        S.barrier()
        if STOP <= 0:
            return nc

        def silu_evac(src_ap, dst_ap, tmp_a, tmp_b, t_tmp, reads, writes):
            S.op("act", lambda: act.activation(out=tmp_a, in_=src_ap, func=AF.Exp, scale=-1.0),
                 reads=reads, writes=[t_tmp])
            S.op("dve", lambda: dve.tensor_scalar_add(out=tmp_a, in0=tmp_a, scalar1=1.0),
                 reads=[t_tmp], writes=[t_tmp])
            S.op("dve", lambda: dve.reciprocal(out=tmp_b, in_=tmp_a), reads=[t_tmp], writes=[t_tmp])
            S.op("dve", lambda: dve.tensor_tensor(out=dst_ap, in0=src_ap, in1=tmp_b, op=ALU.mult),
                 reads=list(reads) + [t_tmp], writes=writes)

        with ExitStack() as _es1:
            def A1(name, shape, dt):
                return _es1.enter_context(nc.sbuf_tensor(name, shape, dt))
            wg = A1("wg", [128, NCH, 768], BF16)
            wlr_b = A1("wlr_b", [128, NCH, 16], BF16)
            g_qTs = (A1("g_qTa", [128, 512], BF16), A1("g_qTb", [128, 512], BF16))
            g_kTs = (A1("g_kTa", [128, 512], BF16), A1("g_kTb", [128, 512], BF16))
            g_silus = (A1("g_silua", [128, 2, 512], BF16), A1("g_silub", [128, 2, 512], BF16))
            g_vs = (A1("g_va", [128, 4, 256], BF16), A1("g_vb", [128, 4, 256], BF16))
            g_lr1 = A1("g_lra", [16, 512], F32)
            g_lrs = (g_lr1, g_lr1)
            g_tmpa = A1("g_tmpa", [128, 512], F32)
            g_tmpb = A1("g_tmpb", [128, 512], F32)
            g_mask4 = A1("g_mask4", [128, 4, 128], F32)
            g_e = A1("g_e", [128, 512], F32)
            g_tmpc = g_e
            g_sp = A1("g_sp", [128, 4, 128], F32)
            g_Eq = A1("g_Eq", [128, 512], F32)
            g_Ek = A1("g_Ek", [128, 512], F32)
            g_qe = A1("g_qe", [128, 512], BF16)
            g_ke = A1("g_ke", [128, 512], BF16)
            g_klT = A1("g_klT", [128, 512], BF16)
            g_kl = A1("g_kl", [128, 4, 128], BF16)
            g_scm = A1("g_scm", [128, 512], BF16)
            g_Sf = A1("g_Sf", [128, 256], F32)
            g_Sb = A1("g_Sb", [128, 4, 256], BF16)
            g_sq = A1("g_sq", [128, 2, 512], BF16)
            g_r = A1("g_r", [128, 512], F32)
            g_y = A1("g_y", [128, 2, 512], BF16)
            t_wg, t_wlr, t_mask4 = T(), T(), T()
            t_qT, t_kT, t_silu, t_v = ((T(), T()) for _ in range(4))
            t_lr = (T(),) * 2
            t_tmp = T()
            t_e, t_sp, t_Eq, t_Ek, t_qe, t_ke, t_klT, t_kl, t_scm = (T() for _ in range(9))
            t_Sf, t_sq2, t_r2, t_y = T(), T(), T(), T()
            t_tmpc = t_e
            t_Sb = [T() for _ in range(4)]
            S.dma("pool", wlr_b[:], wlr.rearrange("(c p) n -> p c n", p=128), writes=[t_wlr])
            for cc in range(4):
                S.op("pool", lambda cc=cc: pool.tensor_copy(g_mask4[:, cc, :], Uincl_f), reads=[t_cst], writes=[t_mask4])
            gblocks = [] if SKIP else [(g, nb) for g in range(2) for nb in range(S_LEN // 512)]
            rot = [0]

            def nextbank():
                rot[0] ^= 1
                return rot[0]

            def gla_inproj_gen(bi):
                g, nb = gblocks[bi]
                par = bi % 2
                t0 = nb * 512
                tok = slice(t0, t0 + 512)
                hr = h_reads(t0, t0 + 512)
                if nb == 0:
                    S.dma("pool", wg[:], wgla[g].rearrange("(c p) n -> p c n", p=128), writes=[t_wg])
                pending = [None]

                def flush():
                    if pending[0] is not None:
                        pending[0]()
                        pending[0] = None

                def group(lhs_fn, rhs_fn, out_fn, evac, extra_reads):
                    bk = nextbank()
                    for c in range(NCH):
                        S.op("pe", lambda c=c, bk=bk: mm(out_fn(bk), lhsT=lhs_fn(c), rhs=rhs_fn(c),
                                                               start=(c == 0), stop=(c == NCH - 1)),
                             reads=hr + extra_reads, writes=[t_ps[bk]], inc=(c == NCH - 1))
                        if c == 3:
                            flush()
                    pending[0] = lambda bk=bk: evac(bk)

                group(lambda c: wg[:, c, 0:128], lambda c: hT[:, c, tok], lambda bk: bank(bk),
                      lambda bk: S.op("act", lambda: act.activation(out=g_qTs[par][:], in_=bank(bk), func=AF.Identity,
                                                                    scale=128 ** -0.5),
                                      reads=[t_ps[bk]], writes=[t_qT[par]]), [t_wg])
                yield
                group(lambda c: wg[:, c, 128:256], lambda c: hT[:, c, tok], lambda bk: bank(bk),
                      lambda bk: S.op("dve", lambda: dve.tensor_copy(g_kTs[par][:], bank(bk)),
                                      reads=[t_ps[bk]], writes=[t_kT[par]]), [t_wg])
                yield
                for ec in range(2):
                    group(lambda c, ec=ec: wg[:, c, 512 + ec * 128:512 + ec * 128 + 128], lambda c: hT[:, c, tok],
                          lambda bk: bank(bk),
                          lambda bk, ec=ec: silu_evac(bank(bk), g_silus[par][:, ec, :], g_tmpa[:], g_tmpb[:], t_tmp,
                                                      [t_ps[bk]], [t_silu[par]]), [t_wg])
                    yield
                for sb in range(4):
                    group(lambda c, sb=sb: hT[:, c, t0 + sb * 128:t0 + sb * 128 + 128], lambda c: wg[:, c, 256:512],
                          lambda bk: bank(bk)[:, 0:256],
                          lambda bk, sb=sb: S.op("dve", lambda: dve.tensor_copy(g_vs[par][:, sb, :], bank(bk)[:, 0:256]),
                                                 reads=[t_ps[bk]], writes=[t_v[par]]), [t_wg])
                    yield
                group(lambda c: wlr_b[:, c, :], lambda c: hT[:, c, tok], lambda bk: bank(bk)[0:16, :],
                      lambda bk: S.op("dve", lambda: dve.tensor_copy(g_lrs[par][:], bank(bk)[0:16, :]),
                                      reads=[t_ps[bk]], writes=[t_lr[par]]), [t_wlr])
                flush()
                yield

            def gpump(gen, n):
                if gen is None:
                    return
                for _ in range(n):
                    try:
                        next(gen)
                    except StopIteration:
                        return

            if gblocks:
                gpump(gla_inproj_gen(0), 100)
            for bi, (g, nb) in enumerate(gblocks):
                par = bi % 2
                g_qT, g_kT, g_silu, g_v, g_lr = g_qTs[par], g_kTs[par], g_silus[par], g_vs[par], g_lrs[par]
                tqT, tkT, tsilu, tv, tlr = t_qT[par], t_kT[par], t_silu[par], t_v[par], t_lr[par]
                nxt = gla_inproj_gen(bi + 1) if bi + 1 < len(gblocks) else None
                t0 = nb * 512
                gsl = slice(g * 128, g * 128 + 128)
                if nb == 0:
                    S.op("dve", lambda: dve.memset(g_Sf[:], 0.0), writes=[t_Sf])
                    S.op("dve", lambda: dve.memset(g_Sb[:, 0, :], 0.0), writes=[t_Sb[0]])
                for cc in range(4):
                    cs = slice(cc * 128, cc * 128 + 128)
                    S.op("pe", lambda cs=cs, cc=cc: mm(
                        bank(2)[:, cs], lhsT=g_lr[0:16, cs], rhs=wa2_f[0:16, gsl], start=(cc == 0), stop=False),
                        reads=[tlr, t_cst], writes=[t_ps[2]], inc=False)
                    S.op("pe", lambda cs=cs, cc=cc: mm(
                        bank(2)[:, cs], lhsT=ones_f[0:1, :], rhs=ba_f[0:1, gsl], start=False, stop=True),
                        reads=[t_cst], writes=[t_ps[2]], inc=(cc == 3))
                gpump(nxt, 1)
                S.op("act", lambda: act.activation(out=g_e[:], in_=bank(2), func=AF.Exp, scale=-1.0),
                     reads=[t_ps[2]], writes=[t_e])
                S.op("act", lambda: act.activation(out=g_sp[:].rearrange("p a b -> p (a b)"), in_=g_e[:], func=AF.Ln, bias=1.0),
                     reads=[t_e], writes=[t_sp])
                for cc in range(4):
                    cs = slice(cc * 128, cc * 128 + 128)
                    S.op("pe", lambda cs=cs, cc=cc: mm(bank(3)[:, cs], lhsT=g_sp[:, cc, :], rhs=Uincl_f,
                                                              start=(cc == 0), stop=True),
                         reads=[t_sp, t_cst], writes=[t_ps[3]], inc=(cc == 3))
                gpump(nxt, 1)
                S.op("act", lambda: act.activation(out=g_Eq[:], in_=bank(3), func=AF.Exp, scale=-1.0 / 16),
                     reads=[t_ps[3]], writes=[t_Eq])
                S.op("act", lambda: act.activation(out=g_Ek[:], in_=bank(3), func=AF.Exp, scale=1.0 / 16),
                     reads=[t_ps[3]], writes=[t_Ek])
                S.op("dve", lambda: dve.tensor_tensor(out=g_ke[:], in0=g_kT[:], in1=g_Ek[:], op=ALU.mult),
                     reads=[tkT, t_Ek], writes=[t_ke])
                for cc in range(4):
                    cs = slice(cc * 128, cc * 128 + 128)
                    S.op("dve", lambda cs=cs, cc=cc: dve.scalar_tensor_tensor(
                        out=g_klT[:, cs], in0=g_kT[:, cs], scalar=g_Eq[:, cc * 128 + 127:cc * 128 + 128], in1=g_Ek[:, cs],
                        op0=ALU.mult, op1=ALU.mult),
                        reads=[tkT, t_Eq, t_Ek], writes=[t_klT])
                S.op("dve", lambda: dve.tensor_tensor(out=g_qe[:], in0=g_qT[:], in1=g_Eq[:], op=ALU.mult),
                     reads=[tqT, t_Eq], writes=[t_qe])
                for cc in range(4):
                    cs = slice(cc * 128, cc * 128 + 128)
                    S.op("pe", lambda cs=cs, cc=cc: mm(bank(4)[:, cs], lhsT=g_klT[:, cs], rhs=ident_b,
                                                              start=(cc == 0), stop=True),
                         reads=[t_klT, t_cst], writes=[t_ps[4]], inc=(cc == 3))
                for cc in range(4):
                    cs = slice(cc * 128, cc * 128 + 128)
                    S.op("pe", lambda cs=cs, cc=cc: mm(bank(5)[:, cs], lhsT=g_ke[:, cs], rhs=g_qe[:, cs],
                                                              start=(cc == 0), stop=True),
                         reads=[t_ke, t_qe], writes=[t_ps[5]], inc=(cc == 3))
                gpump(nxt, 1)
                S.op("act", lambda: act.activation(out=g_kl[:].rearrange("p a b -> p (a b)"), in_=bank(4), func=AF.Copy),
                     reads=[t_ps[4]], writes=[t_kl])
                S.op("dve", lambda: dve.tensor_tensor(out=g_scm[:], in0=bank(5), in1=g_mask4[:].rearrange("p a b -> p (a b)"),
                                                      op=ALU.mult),
                     reads=[t_ps[5], t_mask4], writes=[t_scm])
                for cc in range(4):
                    S.op("pe", lambda cc=cc: mm(
                        bank(2 + cc // 2)[:, (cc % 2) * 256:(cc % 2) * 256 + 256], lhsT=g_kl[:, cc, :], rhs=g_v[:, cc, :],
                        start=(cc % 2 == 0), stop=True),
                        reads=[t_kl, tv], writes=[t_ps[2 + cc // 2]])
                gpump(nxt, 1)
                for cc in range(4):
                    cs = slice(cc * 128, cc * 128 + 128)
                    for ec in range(2):
                        es = slice(ec * 128, ec * 128 + 128)
                        S.op("pe", lambda ec=ec, es=es, cs=cs, cc=cc: mm(
                            bank(6 + ec)[:, cs], lhsT=g_Sb[:, cc, es], rhs=g_qe[:, cs], start=(cc == 0), stop=False),
                            reads=[t_Sb[cc], t_qe], writes=[t_ps[6 + ec]], inc=False)
                        S.op("pe", lambda ec=ec, es=es, cs=cs, cc=cc: mm(
                            bank(6 + ec)[:, cs], lhsT=g_v[:, cc, es], rhs=g_scm[:, cs], start=False, stop=True),
                            reads=[tv, t_scm], writes=[t_ps[6 + ec]])
                    S.op("dve", lambda cc=cc: dve.scalar_tensor_tensor(
                        out=g_Sf[:], in0=g_Sf[:], scalar=g_Eq[:, cc * 128 + 127:cc * 128 + 128],
                        in1=bank(2 + cc // 2)[:, (cc % 2) * 256:(cc % 2) * 256 + 256], op0=ALU.mult, op1=ALU.add),
                        reads=[t_Sf, t_Eq, t_ps[2 + cc // 2]], writes=[t_Sf])
                    S.op("pool", lambda cc=cc: pool.tensor_copy(g_Sb[:, (cc + 1) % 4, :], g_Sf[:]),
                         reads=[t_Sf], writes=[t_Sb[(cc + 1) % 4]])
                    gpump(nxt, 1)
                gpump(nxt, 100)
                for ec in range(2):
                    S.op("act", lambda ec=ec: act.activation(out=g_sq[:, ec, :], in_=bank(6 + ec), func=AF.Square),
                         reads=[t_ps[6 + ec]], writes=[t_sq2])
                for ec in range(2):
                    S.op("pe", lambda ec=ec: mm(bank(5), lhsT=ones_b, rhs=g_sq[:, ec, :], start=(ec == 0), stop=(ec == 1)),
                         reads=[t_sq2, t_cst], writes=[t_ps[5]], inc=(ec == 1))
                S.op("act", lambda: act.activation(out=g_r[:], in_=bank(5), func=AF.Ln, scale=1.0 / 256, bias=EPS),
                     reads=[t_ps[5]], writes=[t_r2])
                S.op("act", lambda: act.activation(out=g_r[:], in_=g_r[:], func=AF.Exp, scale=-0.5),
                     reads=[t_r2], writes=[t_r2])
                for ec in range(2):
                    S.op("dve", lambda ec=ec: dve.scalar_tensor_tensor(
                        out=g_tmpc[:], in0=bank(6 + ec), scalar=cols_f[:, 48 + ec:49 + ec], in1=g_r[:],
                        op0=ALU.mult, op1=ALU.mult),
                        reads=[t_ps[6 + ec], t_r2, t_cst], writes=[t_tmpc])
                    S.op("dve", lambda ec=ec: dve.tensor_tensor(out=g_y[:, ec, :], in0=g_tmpc[:], in1=g_silu[:, ec, :], op=ALU.mult),
                         reads=[t_tmpc, tsilu], writes=[t_y])
                S.dma("sp", mg_own[t0 // EB, g * 256:(g + 1) * 256, t0 % EB:t0 % EB + 512].rearrange("(e p) t -> p e t", p=128), g_y[:],
                      reads=[t_y])
        S.barrier()

        NKB = S_LEN // 128
        with ExitStack() as _es2:
            def A2(name, shape, dt):
                return _es2.enter_context(nc.sbuf_tensor(name, shape, dt))
            ws = A2("ws", [128, NCH, 512], BF16)
            s_kT = A2("s_kT", [128, S_LEN], BF16)
            s_v = A2("s_v", [128, NKB, 128], BF16)
            s_qTs = (A2("s_qTa", [128, TQ], BF16), A2("s_qTb", [128, TQ], BF16))
            s_gss = (A2("s_gsa", [128, TQ], BF16), A2("s_gsb", [128, TQ], BF16))
            s_ta = A2("s_ta", [128, 512], F32)
            s_tb = A2("s_tb", [128, 512], F32)
            e1s = (A2("s_e1a", [128, TQ], F32), A2("s_e1b", [128, TQ], F32), A2("s_e1c", [128, TQ], F32))
            sps = (A2("s_spa", [128, TQ], BF16), A2("s_spb", [128, TQ], BF16))
            gs_ = (A2("s_ga", [128, TQ], BF16), A2("s_gb", [128, TQ], BF16))
            ws_ = (A2("s_wa", [128, TQ], BF16), A2("s_wb", [128, TQ], BF16))
            s_y = A2("s_y", [128, TQ], BF16)
            t_ws, t_sy, t_stmp, t_msown = T(), T(), T(), T()
            t_skT = [T() for _ in range(S_LEN // 512)]
            t_sv = [T() for _ in range(S_LEN // 512)]
            t_sqT, t_sgs = (T(), T()), (T(), T())
            t_e1, t_sps, t_gs_, t_ws_ = (T(), T(), T()), (T(), T()), (T(), T()), (T(), T())
            ZB, BB, OB = 0, 2, 4
            blocks = [] if SKIP else [(hd, tb) for hd in range(4) for tb in range(S_LEN // TQ)]

            def inproj_gen(bi):
                hd, tb = blocks[bi]
                s_qT, s_gs = s_qTs[bi % 2], s_gss[bi % 2]
                tq_, tg_ = t_sqT[bi % 2], t_sgs[bi % 2]
                q0 = tb * TQ
                if tb == 0:
                    S.dma("pool", ws[:], wsb[hd].rearrange("(c p) n -> p c n", p=128), writes=[t_ws])
                pending = [None]

                def flush():
                    if pending[0] is not None:
                        pending[0]()
                        pending[0] = None

                for half in range(NBK):
                    t0 = q0 + half * 512
                    tok = slice(t0, t0 + 512)
                    loc = slice(half * 512, half * 512 + 512)
                    hr = h_reads(t0, t0 + 512)
                    for (col0, bk_, evac) in (
                        (0, 6, lambda loc=loc: S.op("dve", lambda: dve.tensor_scalar_mul(
                            out=s_qT[:, loc], in0=bank(6), scalar1=128 ** -0.5), reads=[t_ps[6]], writes=[tq_])),
                        (128, 7, lambda tok=tok, t0=t0: S.op("dve", lambda: dve.tensor_copy(s_kT[:, tok], bank(7)),
                                                             reads=[t_ps[7]], writes=[t_skT[t0 // 512]])),
                        (384, 6, lambda loc=loc: silu_evac(bank(6), s_gs[:, loc], s_ta[:], s_tb[:], t_stmp,
                                                           [t_ps[6]], [tg_])),
                    ):
                        for c in range(NCH):
                            S.op("pe", lambda c=c, col0=col0, bk_=bk_: mm(
                                bank(bk_), lhsT=ws[:, c, col0:col0 + 128], rhs=hT[:, c, tok],
                                start=(c == 0), stop=(c == NCH - 1)),
                                reads=hr + [t_ws], writes=[t_ps[bk_]], inc=(c == NCH - 1))
                            if c % 4 == 3:
                                if c == 3:
                                    flush()
                                yield
                        pending[0] = evac
                    for sb in range(4):
                        kb = (t0 // 128) + sb
                        for c in range(NCH):
                            S.op("pe", lambda c=c, kb=kb, sb=sb: mm(
                                bank(7)[:, sb * 128:sb * 128 + 128], lhsT=hT[:, c, kb * 128:kb * 128 + 128],
                                rhs=ws[:, c, 256:384], start=(c == 0 and sb == 0), stop=(c == NCH - 1)),
                                reads=hr + [t_ws], writes=[t_ps[7]], inc=(c == NCH - 1 and sb == 3))
                            if c % 4 == 3:
                                if c == 3 and sb == 0:
                                    flush()
                                yield
                    pending[0] = (lambda t0=t0: S.op("dve", lambda: dve.tensor_copy(
                        s_v[:, t0 // 128:t0 // 128 + 4, :], bank(7).rearrange("p (a b) -> p a b", a=4)),
                        reads=[t_ps[7]], writes=[t_sv[t0 // 512]]))
                flush()
                yield

            NYIELD = NBK * 28 + 1

            def pump(gen, n):
                if gen is None:
                    return
                for _ in range(n):
                    try:
                        next(gen)
                    except StopIteration:
                        return

            if blocks:
                pump(inproj_gen(0), 10 ** 6)
                for k in range(4):
                    issue_cc(mg_own[k], mg_all[k])
                for nm, src, dst in (("wo", wo, wo16), ("wgt", wgt, wgt16)):
                    for j in range(4):
                        S.dma("pool", dst[j * 512:(j + 1) * 512, :], src[j * 512:(j + 1) * 512, :],
                              writes=[t_w16[nm][j]])
            for bi, (hd, tb) in enumerate(blocks):
                    q0 = tb * TQ
                    s_qT, s_gs = s_qTs[bi % 2], s_gss[bi % 2]
                    tq_, tg_ = t_sqT[bi % 2], t_sgs[bi % 2]
                    nxt = inproj_gen(bi + 1) if bi + 1 < len(blocks) else None
                    kbs = list(range((tb + 1) * (TQ // 128) - 1, -1, -1))
                    P = len(kbs)
                    npump = (NYIELD + P - 1) // P
                    startedB = [False] * NBK
                    startedO = [False] * NBK

                    def geom(p):
                        kb = kbs[p]
                        off = kb * 128 - q0
                        lo = max(0, off)
                        segs = []
                        for bki in range(NBK):
                            c0 = max(lo, bki * 512)
                            c1 = (bki + 1) * 512
                            if c0 < c1:
                                segs.append((bki, c0, c1))
                        return kb, off, lo, segs

                    def emit_Z(p):
                        kb, off, lo, segs = geom(p)
                        for (bki, c0, c1) in segs:
                            S.op("pe", lambda bki=bki, c0=c0, c1=c1, kb=kb: mm(
                                bank(ZB + bki)[:, c0 - bki * 512:c1 - bki * 512],
                                lhsT=s_kT[:, kb * 128:kb * 128 + 128], rhs=s_qT[:, c0:c1], start=True, stop=True),
                                reads=[t_skT[kb // 4], tq_], writes=[t_ps[ZB + bki]])

                    def zb_reads(segs, base):
                        return [t_ps[base + bki] for (bki, _, _) in segs]

                    def emit_E1(p):
                        kb, off, lo, segs = geom(p)
                        e1, te1 = e1s[p % 3], t_e1[p % 3]
                        S.op("act", lambda: act.activation(
                            out=e1[:, lo:TQ], in_=PS[:, ZB * 512 + lo:ZB * 512 + TQ], func=AF.Exp),
                            reads=zb_reads(segs, ZB), writes=[te1])
                        if off >= 0:
                            S.op("dve", lambda: dve.tensor_tensor(
                                out=e1[:, off:off + 128], in0=e1[:, off:off + 128], in1=Ustr_f, op=ALU.mult),
                                reads=[te1, t_cst], writes=[te1])

                    emit_Z(0)
                    emit_E1(0)
                    if P > 1:
                        emit_Z(1)
                    for p in range(P + 1):
                        if p >= 1:
                            kbp, offp, lop, segsp = geom(p - 1)
                            e1p, te1p = e1s[(p - 1) % 3], t_e1[(p - 1) % 3]
                            spp, tspp = sps[(p - 1) % 2], t_sps[(p - 1) % 2]
                            gp, tgp = gs_[(p - 1) % 2], t_gs_[(p - 1) % 2]
                            wp_, twp = ws_[(p - 1) % 2], t_ws_[(p - 1) % 2]
                            S.op("act", lambda gp=gp, lop=lop: act.activation(
                                out=gp[:, lop:TQ], in_=PS[:, BB * 512 + lop:BB * 512 + TQ], func=AF.Exp, scale=-1.0),
                                reads=zb_reads(segsp, BB), writes=[tgp])
                            if p - 1 < P - 1:
                                for (bki, c0, c1) in segsp:
                                    S.op("pe", lambda bki=bki, c0=c0, c1=c1, spp=spp: mm(
                                        bank(BB + bki)[:, c0 - bki * 512:c1 - bki * 512], lhsT=Ustr_b, rhs=spp[:, c0:c1],
                                        start=False, stop=True),
                                        reads=[tspp, t_cst], writes=[t_ps[BB + bki]])
                        if p < P:
                            kb, off, lo, segs = geom(p)
                            e1, te1 = e1s[p % 3], t_e1[p % 3]
                            sp_, tsp = sps[p % 2], t_sps[p % 2]
                            S.op("act", lambda e1=e1, sp_=sp_, lo=lo: act.activation(
                                out=sp_[:, lo:TQ], in_=e1[:, lo:TQ], func=AF.Ln, bias=1.0),
                                reads=[te1], writes=[tsp])
                            for (bki, c0, c1) in segs:
                                S.op("pe", lambda bki=bki, c0=c0, c1=c1, sp_=sp_, st=(not startedB[bki]): mm(
                                    bank(BB + bki)[:, c0 - bki * 512:c1 - bki * 512], lhsT=Lincl_b, rhs=sp_[:, c0:c1],
                                    start=st, stop=True),
                                    reads=[tsp, t_cst], writes=[t_ps[BB + bki]])
                                startedB[bki] = True
                        if p + 1 < P:
                            emit_E1(p + 1)
                        pump(nxt, npump // 2)
                        if p + 2 < P:
                            emit_Z(p + 2)
                        if p >= 1:
                            S.op("dve", lambda wp_=wp_, e1p=e1p, gp=gp, lop=lop: dve.tensor_tensor(
                                out=wp_[:, lop:TQ], in0=e1p[:, lop:TQ], in1=gp[:, lop:TQ], op=ALU.mult),
                                reads=[te1p, tgp], writes=[twp])
                            for (bki, c0, c1) in segsp:
                                S.op("pe", lambda bki=bki, c0=c0, c1=c1, wp_=wp_, kbp=kbp, st=(not startedO[bki]): mm(
                                    bank(OB + bki)[:, c0 - bki * 512:c1 - bki * 512], lhsT=s_v[:, kbp, :], rhs=wp_[:, c0:c1],
                                    start=st, stop=True),
                                    reads=[t_sv[kbp // 4], twp], writes=[t_ps[OB + bki]])
                                startedO[bki] = True
                        pump(nxt, npump - npump // 2)
                    pump(nxt, 10 ** 6)
                    S.op("dve", lambda: dve.tensor_tensor(out=s_y[:], in0=PS[:, OB * 512:OB * 512 + TQ], in1=s_gs[:], op=ALU.mult),
                         reads=[t_ps[OB + i] for i in range(NBK)] + [tg_], writes=[t_sy])
                    S.dma("sp", ms_own[hd, q0 // HS, :, q0 % HS:q0 % HS + TQ], s_y[:], reads=[t_sy], writes=[t_msown])
                    if tb == S_LEN // TQ - 1 and hd < 3:
                        S._wait("pool", S._deps("pool", [t_msown], []))
                        issue_cc(ms_own[hd].rearrange("hh p t -> (hh p) t"), ms_all[hd])
        S.barrier(skip=("cc",))

    BT = 256
    if mode == "AB":
        par = nc.sync.partition_id() % 2
        mg_v = mg_all.rearrange("k (c p) t -> k p c t", p=128)
        ms_v = ms_all.rearrange("h q t -> (h q) t").rearrange("(hr hh p) t -> hh p hr t", hr=8, hh=2)
    else:
        mixin_v = mixin.rearrange("(c p) t -> p c t", p=128)
    with ExitStack() as _es3:
        wo_b = _es3.enter_context(nc.sbuf_tensor("wo_b", [128, NCH, D], BF16))
        wgt_b = _es3.enter_context(nc.sbuf_tensor("wgt_b", [128, NCH, D], BF16))
        wp_b = _es3.enter_context(nc.sbuf_tensor("wp_b", [128, 2, D], BF16))
        mxa = _es3.enter_context(nc.sbuf_tensor("mxa", [128, NCH, BT], BF16))
        xra = _es3.enter_context(nc.sbuf_tensor("xra", [128, NCH, BT], F32))
        pbts = (_es3.enter_context(nc.sbuf_tensor("pbt", [128, 2, BT], BF16)), _es3.enter_context(nc.sbuf_tensor("pbt2", [128, 2, BT], BF16)))
        m1 = _es3.enter_context(nc.sbuf_tensor("m1", [128, NCH, BT], F32))
        sq3 = _es3.enter_context(nc.sbuf_tensor("sq3", [128, NCH, BT], BF16))
        r3 = _es3.enter_context(nc.sbuf_tensor("r3", [128, BT], F32))
        gte = _es3.enter_context(nc.sbuf_tensor("gte", [128, 2, BT], F32))
        oo = _es3.enter_context(nc.sbuf_tensor("oo", [128, 4, BT], F32))
        t_wo = [T() for _ in range(4)]
        t_wgt = [T() for _ in range(4)]
        t_wp = T()
        for j in range(4):
            S.dma("sp", wo_b[:, 4 * j:4 * j + 4, :],
                  wo16[j * 512:(j + 1) * 512, :].rearrange("(c p) n -> p c n", p=128),
                  reads=[t_w16["wo"][j]], writes=[t_wo[j]])
        for j in range(4):
            S.dma("sp", wgt_b[:, 4 * j:4 * j + 4, :],
                  wgt16[j * 512:(j + 1) * 512, :].rearrange("(c p) n -> p c n", p=128),
                  reads=[t_w16["wgt"][j]], writes=[t_wgt[j]])
        S.dma("pool", wp_b[:], wp.rearrange("(c p) n -> p c n", p=128), writes=[t_wp])
        if not SKIP:
            issue_cc(ms_own[3].rearrange("hh p t -> (hh p) t"), ms_all[3])
        mxs, t_mx = (mxa, mxa), (T(),) * 2
        xrs, t_xr = (xra, xra), (T(),) * 2
        h1b = sq3
        t_pb, t_m1, t_sq3, t_r3 = T(), T(), T(), T()
        t_h1b = t_sq3
        t_gte = (T(), T())
        t_oo = [T() for _ in range(4)]
        pTo_v = pTo.rearrange("(c p) t -> p c t", p=128)
        outT_v = outT.rearrange("(c p) t -> p c t", p=128)
        rot3 = [0]

        def nb3():
            rot3[0] = (rot3[0] + 1) % 8
            return rot3[0]

        t_m1c = [T() for _ in range(NCH)]
        t_mxp = [T() for _ in range(5)]
        t_sqc = [T() for _ in range(NCH)]
        t_pbs = (T(), T())
        NB3 = HS // BT

        def emit_loads(nb):
            tsl = slice(nb * BT, (nb + 1) * BT)
            if mode == "AB":
                jj = (nb * BT) // EB
                cc0 = (nb * BT) % EB
                S.dma("sp", mxa[:, 0:8, :], mg_v[bass.ds(par * 2 + jj, 1), :, :, cc0:cc0 + BT].rearrange("o p c t -> p (o c) t"),
                      reads=[t_mix], writes=[t_mxp[0]])
                S.dma("sp", mxa[:, 8:16, :],
                      ms_v[bass.ds(par, 1), :, :, nb * BT:(nb + 1) * BT].rearrange("o p r t -> p (o r) t"),
                      reads=[t_mix], writes=[t_mxp[1]])
            else:
                S.dma("sp", mxa[:], mixin_v[:, :, tsl], writes=[t_mx[0]])
            S.dma("sp", xra[:], xTo[nb].rearrange("p (c t) -> p c t", c=NCH), writes=[t_xr[0]])
            S.dma("pool", pbts[nb % 2][:], pTo_v[:, :, tsl], writes=[t_pbs[nb % 2]])

        for nb in range(NB3):
            emit_loads(nb)
            tsl = slice(nb * BT, (nb + 1) * BT)
            mx, tmx = mxa, t_mx[0]
            xr, txr = xra, t_xr[0]
            pbt, t_pb = pbts[nb % 2], t_pbs[nb % 2]
            for oc in range(NCH):
                bk = nb3()
                for c in range(NCH):
                    S.op("pe", lambda c=c, oc=oc, bk=bk: mm(
                        bank(bk)[:, 0:BT], lhsT=wo_b[:, c, oc * 128:(oc + 1) * 128], rhs=mx[:, c, :],
                        start=(c == 0), stop=(c == NCH - 1)),
                        reads=[t_wo[c // 4], tmx] + t_mxp, writes=[t_ps[bk]], inc=(c == NCH - 1))
                S.op("act", lambda oc=oc, bk=bk: act.activation(out=sq3[:, oc, :], in_=bank(bk)[:, 0:BT], func=AF.Square),
                     reads=[t_ps[bk]], writes=[t_sqc[oc]])
                S.op("dve", lambda oc=oc, bk=bk: dve.tensor_scalar_mul(
                    out=m1[:, oc, :], in0=bank(bk)[:, 0:BT], scalar1=cols_f[:, 16 + oc:17 + oc]),
                    reads=[t_ps[bk], t_cst], writes=[t_m1c[oc]])
            bk = nb3()
            for c in range(NCH):
                S.op("pe", lambda c=c, bk=bk: mm(bank(bk)[:, 0:BT], lhsT=ones_b, rhs=sq3[:, c, :],
                                                       start=(c == 0), stop=(c == NCH - 1)),
                     reads=[t_sqc[c], t_cst], writes=[t_ps[bk]], inc=(c == NCH - 1))
            S.op("act", lambda bk=bk: act.activation(out=r3[:], in_=bank(bk)[:, 0:BT], func=AF.Ln, scale=1.0 / D, bias=EPS),
                 reads=[t_ps[bk]], writes=[t_r3])
            S.op("act", lambda: act.activation(out=r3[:], in_=r3[:], func=AF.Exp, scale=-0.5),
                 reads=[t_r3], writes=[t_r3])
            for oc in range(NCH):
                S.op("dve", lambda oc=oc: dve.tensor_tensor(out=m1[:, oc, :], in0=m1[:, oc, :], in1=r3[:], op=ALU.mult),
                     reads=[t_m1c[oc], t_r3], writes=[t_m1c[oc]])
                S.op("pool", lambda oc=oc: pool.tensor_tensor(out=m1[:, oc, :], in0=m1[:, oc, :], in1=xr[:, oc, :], op=ALU.add),
                     reads=[t_m1c[oc], txr], writes=[t_m1c[oc]])
                S.op("act", lambda oc=oc: act.activation(out=h1b[:, oc, :], in_=m1[:, oc, :], func=AF.Copy),
                     reads=[t_m1c[oc]], writes=[t_sqc[oc]])
            for oc in range(NCH):
                bk = nb3()
                for c in range(NCH):
                    S.op("pe", lambda c=c, oc=oc, bk=bk: mm(
                        bank(bk)[:, 0:BT], lhsT=wgt_b[:, c, oc * 128:(oc + 1) * 128], rhs=h1b[:, c, :],
                        start=(c == 0), stop=(c == NCH - 1)),
                        reads=[t_wgt[c // 4], t_sqc[c]], writes=[t_ps[bk]], inc=(c == NCH - 1))
                gt, tgt = gte[:, oc % 2, :], t_gte[oc % 2]
                S.op("act", lambda oc=oc, bk=bk, gt=gt: act.activation(
                    out=gt, in_=bank(bk)[:, 0:BT], func=AF.Sigmoid, bias=cols_f[:, 32 + oc:33 + oc]),
                    reads=[t_ps[bk], t_cst], writes=[tgt])
                bk2 = nb3()
                for c in range(2):
                    S.op("pe", lambda c=c, oc=oc, bk2=bk2: mm(
                        bank(bk2)[:, 0:BT], lhsT=wp_b[:, c, oc * 128:(oc + 1) * 128], rhs=pbt[:, c, :],
                        start=(c == 0), stop=(c == 1)),
                        reads=[t_wp, t_pb], writes=[t_ps[bk2]], inc=(c == 1))
                S.op("dve", lambda oc=oc, bk2=bk2, gt=gt: dve.tensor_tensor(
                    out=gt, in0=gt, in1=bank(bk2)[:, 0:BT], op=ALU.mult),
                    reads=[tgt, t_ps[bk2]], writes=[tgt])
                S.op("dve", lambda oc=oc, gt=gt: dve.tensor_tensor(
                    out=oo[:, oc % 4, :], in0=gt, in1=m1[:, oc, :], op=ALU.add),
                    reads=[tgt, t_m1c[oc]], writes=[t_oo[oc % 4]])
                if oc % 4 == 3:
                    S.dma("pool", outT_v[:, oc - 3:oc + 1, tsl], oo[:], reads=t_oo)
        S.barrier()
    return nc


_PROG = {}


def kernel(x, p, g_pre, w_in, w_a2, b_a, g_gla_head, w_out, g_post, w_ple_gate, b_ple_gate, w_ple_proj):
    x = np.asarray(x, np.float32)
    B, S_LEN, _ = x.shape
    HS = S_LEN // 2
    f = lambda a: np.ascontiguousarray(np.asarray(a, np.float32))
    p, g_pre, w_in, w_a2, b_a = f(p), f(g_pre), f(w_in), f(w_a2), f(b_a)
    g_gla_head, w_out, g_post = f(g_gla_head), f(w_out), f(g_post)
    w_ple_gate, b_ple_gate, w_ple_proj = f(w_ple_gate), f(b_ple_gate), f(w_ple_proj)
    W = w_in[0]
    GQ, GK, GV, GG, LR, SQ, SK, SV, SG = 0, 512, 1024, 2048, 3072, 3088, 4112, 5136, 6160

    def colv(v):
        return v.reshape(-1, 128).T

    ii = np.arange(128)
    cst = np.zeros((128, 5, 128), np.float32)
    cst[:, 0, :] = (ii[:, None] == ii[None, :])
    cst[:, 1, :] = (ii[:, None] <= ii[None, :])
    cst[:, 2, :] = (ii[:, None] < ii[None, :])
    cst[:, 3, :] = (ii[:, None] >= ii[None, :])
    cst[:, 4, :] = 1.0
    cols = np.zeros((128, 64), np.float32)
    cols[:, 0:16] = colv(g_pre[0])
    cols[:, 16:32] = colv(g_post[0])
    cols[:, 32:48] = colv(b_ple_gate[0])
    cols[:, 48:50] = colv(g_gla_head[0])
    in_maps = []
    for core in range(8):
        b, hh = core // 2, core % 2
        wsb = np.stack([np.concatenate([W[:, SQ + 128 * H:SQ + 128 * H + 128], W[:, SK + 128 * H:SK + 128 * H + 128],
                                        W[:, SV + 128 * H:SV + 128 * H + 128], W[:, SG + 128 * H:SG + 128 * H + 128]], axis=1)
                        for H in range(4 * hh, 4 * hh + 4)])
        wgla = np.stack([np.concatenate([W[:, GQ + 128 * G:GQ + 128 * G + 128], W[:, GK + 128 * G:GK + 128 * G + 128],
                                         W[:, GV + 256 * G:GV + 256 * G + 256], W[:, GG + 256 * G:GG + 256 * G + 256]], axis=1)
                         for G in range(2 * hh, 2 * hh + 2)])
        xtile = np.ascontiguousarray(x[b].reshape(S_LEN // 256, 256, NCH, 128).transpose(0, 3, 2, 1)).reshape(
            S_LEN // 256, 128, NCH * 256)
        wo_perm = np.concatenate([w_out[0][0:1024]] + [w_out[0][1024 + (4 * r + h) * 128:1024 + (4 * r + h) * 128 + 128]
                                                       for h in range(4) for r in range(2)], axis=0)
        in_maps.append({
            "xT": xtile,
            "xTo": np.ascontiguousarray(xtile[hh * (HS // 256):(hh + 1) * (HS // 256)]),
            "pTo": np.ascontiguousarray(p[0, b, hh * HS:(hh + 1) * HS].T),
            "wsb": np.ascontiguousarray(wsb),
            "wgla": np.ascontiguousarray(wgla),
            "wlr": np.ascontiguousarray(W[:, LR:LR + 16]),
            "wa2": np.ascontiguousarray(w_a2[0][:, 256 * hh:256 * hh + 256]),
            "ba": np.ascontiguousarray(b_a[0][None, 256 * hh:256 * hh + 256]),
            "cols": cols,
            "wo": np.ascontiguousarray(wo_perm),
            "wgt": w_ple_gate[0],
            "wp": w_ple_proj[0],
            "cst": cst,
        })
    if S_LEN not in _PROG:
        _PROG[S_LEN] = build_program(S_LEN, "AB")
    res = run_bass_kernel_spmd(_PROG[S_LEN], in_maps, core_ids=list(range(8)))
    out = np.empty((B, S_LEN, D), np.float32)
    for core in range(8):
        b, hh = core // 2, core % 2
        out[b, hh * HS:(hh + 1) * HS, :] = res.results[core]["outT"].T
    return out
```

```python
from contextlib import ExitStack
import numpy as np
import concourse.bass as bass
import concourse.mybir as mybir
from concourse.bass_utils import run_bass_kernel_spmd

F32 = mybir.dt.float32
BF16 = mybir.dt.bfloat16
AF = mybir.ActivationFunctionType
ALU = mybir.AluOpType

D = 2048
NCH = 16
EPS = 1e-6
TQ = 1024
NBK = TQ // 512


class T:
    __slots__ = ("w", "r", "x")

    def __init__(self, x=False):
        self.w = None
        self.r = []
        self.x = x


class Sched:
    ENG = ("pe", "act", "dve", "pool", "sp")

    def __init__(self, nc, n_dma_sems=28):
        self.nc = nc
        self.e = dict(pe=nc.tensor, act=nc.scalar, dve=nc.vector, pool=nc.gpsimd, sp=nc.sync)
        self.sems = {}
        self.cnt = {}
        for k in self.ENG:
            self.sems[k] = nc.alloc_semaphore("s_" + k)
            self.cnt[k] = 0
        self.dma_keys = []
        for i in range(n_dma_sems):
            k = "d%d" % i
            self.sems[k] = nc.alloc_semaphore("s_" + k)
            self.cnt[k] = 0
            self.dma_keys.append(k)
        self.sems["cc"] = nc.alloc_semaphore("s_cc")
        self.cnt["cc"] = 0
        self.dma_rr = 0
        self.seen = {k: {} for k in self.ENG}

    def _deps(self, eng, reads, writes):
        need = {}

        def add(d, same_ok):
            if d is None:
                return
            k, v = d
            if k == eng and same_ok:
                return
            if need.get(k, 0) < v:
                need[k] = v
        for t in reads:
            add(t.w, False)
            if t.x:
                for d in t.r:
                    add(d, True)
        for t in writes:
            add(t.w, True)
            for d in t.r:
                add(d, False)
        return need

    def _wait(self, eng, need):
        seen = self.seen[eng]
        for k, v in need.items():
            if seen.get(k, 0) < v:
                self.e[eng].wait_ge(self.sems[k], v)
                seen[k] = v

    def _record(self, d, reads, writes):
        for t in reads:
            t.r.append(d)
        for t in writes:
            t.w = d
            t.r = []

    def op(self, eng, fn, reads=(), writes=(), inc=True):
        self._wait(eng, self._deps(eng, reads, writes))
        ins = fn()
        if inc:
            self.cnt[eng] += 1
            ins.then_inc(self.sems[eng], 1)
            seq = self.cnt[eng]
        else:
            seq = self.cnt[eng] + 1
        self._record((eng, seq), reads, writes)
        return ins

    def dma(self, eng, out, in_, reads=(), writes=()):
        self._wait(eng, self._deps(eng, reads, writes))
        k = self.dma_keys[self.dma_rr]
        self.dma_rr = (self.dma_rr + 1) % len(self.dma_keys)
        ins = self.e[eng].dma_start(out=out, in_=in_)
        self.cnt[k] += 16
        ins.then_inc(self.sems[k], 16)
        self._record((k, self.cnt[k]), reads, writes)
        return ins

    def barrier(self, skip=()):
        for eng in self.ENG:
            need = {k: v for k, v in self.cnt.items() if v > 0 and k not in skip}
            self._wait(eng, need)


def build_program(S_LEN, mode="AB"):
    STOP = 9
    P3 = 9
    HS = S_LEN // 2
    nc = bass.Bass("TRN2", target_bir_lowering=False)
    S = Sched(nc)
    pe, act, dve, pool = nc.tensor, nc.scalar, nc.vector, nc.gpsimd

    def mm(out, **kw):
        return pe.matmul(out, skip_group_check=True, **kw)

    def din(name, shape, dt=F32):
        return nc.dram_tensor(name, shape, dt, kind="ExternalInput").ap()

    xT = din("xT", [S_LEN // 256, 128, NCH * 256])
    xTo = din("xTo", [HS // 256, 128, NCH * 256])
    pTo = din("pTo", [256, HS])
    wsb = din("wsb", [4, D, 512])
    wgla = din("wgla", [2, D, 768])
    wlr = din("wlr", [D, 16])
    wa2 = din("wa2", [16, 256])
    ba = din("ba", [1, 256])
    cols = din("cols", [128, 64])
    wo = din("wo", [D, D])
    wgt = din("wgt", [D, D])
    wp = din("wp", [256, D])
    cst = din("cst", [128, 5, 128])
    outT = nc.dram_tensor("outT", [D, HS], F32, kind="ExternalOutput").ap()
    if mode == "A":
        mix_own = nc.dram_tensor("mix_own", [4, 1024, S_LEN // 4], BF16, kind="ExternalOutput").ap()
    else:
        mix_own = nc.dram_tensor("mix_own", [4, 1024, S_LEN // 4], BF16).ap()
    mix_all = nc.dram_tensor("mix_all", [4, 2048, S_LEN // 4], BF16).ap()
    mg_own = nc.dram_tensor("mg_own", [4, 512, S_LEN // 4], BF16).ap()
    mg_all = nc.dram_tensor("mg_all", [4, 1024, S_LEN // 4], BF16).ap()
    ms_own = nc.dram_tensor("ms_own", [4, 2, 128, HS], BF16).ap()
    ms_all = nc.dram_tensor("ms_all", [4, 512, HS], BF16).ap()
    GROUPS = [[0, 1], [2, 3], [4, 5], [6, 7]]
    wo16 = nc.dram_tensor("wo16", [D, D], BF16).ap()
    wgt16 = nc.dram_tensor("wgt16", [D, D], BF16).ap()
    t_w16 = {"wo": [T() for _ in range(4)], "wgt": [T() for _ in range(4)]}
    t_mix = T()

    def issue_cc(src, dst):
        pool.collective_compute("AllGather", ALU.bypass, replica_groups=GROUPS, ins=[src], outs=[dst]
                                ).then_inc(S.sems["cc"], 1)
        S.cnt["cc"] += 1
        t_mix.w = ("cc", S.cnt["cc"])
    EB = S_LEN // 4
    if mode == "B":
        mixin = din("mixin", [2048, HS], BF16)
    SKIP = (mode == "B")

    cst_f = nc.alloc_sbuf_tensor("cst_f", [128, 5, 128], F32)
    cst_b = nc.alloc_sbuf_tensor("cst_b", [128, 5, 128], BF16)
    cols_f = nc.alloc_sbuf_tensor("cols_f", [128, 64], F32)
    wa2_f = nc.alloc_sbuf_tensor("wa2_f", [16, 256], F32)
    ba_f = nc.alloc_sbuf_tensor("ba_f", [1, 256], F32)
    t_cst = T()
    S.dma("sp", cst_f[:], cst[:, :, :], writes=[t_cst])
    S.dma("sp", cols_f[:], cols[:, :], writes=[t_cst])
    S.dma("sp", wa2_f[:], wa2[:, :], writes=[t_cst])
    S.dma("sp", ba_f[:], ba[:, :], writes=[t_cst])
    S.op("dve", lambda: dve.tensor_copy(cst_b[:], cst_f[:]), reads=[t_cst], writes=[t_cst])
    ident_b = cst_b[:, 0, :]
    Uincl_f = cst_f[:, 1, :]
    Ustr_f = cst_f[:, 2, :]
    Ustr_b = cst_b[:, 2, :]
    Lincl_b = cst_b[:, 3, :]
    ones_b = cst_b[:, 4, :]
    ones_f = cst_f[:, 4, :]

    PS = nc.alloc_psum_tensor("ps", [128, 4096], F32)
    t_ps = [T(True) for _ in range(8)]

    def bank(i):
        return PS[:, i * 512:(i + 1) * 512]

    with nc.sbuf_tensor("hT", [128, NCH, S_LEN], BF16) as hT:
        NB0 = S_LEN // 256
        NLOOP0 = 0 if SKIP else NB0
        t_hT = [T() for _ in range(NB0)]
        t_hTp = [T() for _ in range(NB0)]

        def h_reads(t0, t1):
            rng = range(t0 // 256, (t1 + 255) // 256)
            return [t_hT[i] for i in rng] + [t_hTp[i] for i in rng]

        with ExitStack() as _es0:
            xb0 = _es0.enter_context(nc.sbuf_tensor("xb0", [128, NCH, 256], F32))
            xb1 = _es0.enter_context(nc.sbuf_tensor("xb1", [128, NCH, 256], F32))
            sq0 = _es0.enter_context(nc.sbuf_tensor("sq0", [128, NCH, 256], BF16))
            r0a = _es0.enter_context(nc.sbuf_tensor("r0", [128, 256], F32))
            r0b = _es0.enter_context(nc.sbuf_tensor("r1", [128, 256], F32))
            p0tmp = (_es0.enter_context(nc.sbuf_tensor("p0ta", [128, 256], F32)), _es0.enter_context(nc.sbuf_tensor("p0tb", [128, 256], F32)))
            t_p0tmp = (T(), T())
            xbs = (xb0, xb1)
            t_xb = (T(), T())
            t_sq = T()
            rs = (r0a, r0b)
            t_r = (T(), T())
            for nb in range(NLOOP0):
                xb = xbs[nb % 2]
                txb = t_xb[nb % 2]
                r = rs[nb % 2]
                tr = t_r[nb % 2]
                tsl = slice(nb * 256, (nb + 1) * 256)
                S.dma("sp", xb[:], xT[nb].rearrange("p (c t) -> p c t", c=NCH), writes=[txb])
                S.op("act", lambda xb=xb: act.activation(out=sq0[:], in_=xb[:], func=AF.Square),
                     reads=[txb], writes=[t_sq])
                bk = nb % 2
                for c in range(NCH):
                    S.op("pe", lambda c=c, bk=bk: mm(bank(bk)[:, 0:256], lhsT=ones_b, rhs=sq0[:, c, :],
                                                           start=(c == 0), stop=(c == NCH - 1)),
                         reads=[t_sq, t_cst], writes=[t_ps[bk]], inc=(c == NCH - 1))
                S.op("act", lambda r=r, bk=bk: act.activation(out=r[:], in_=bank(bk)[:, 0:256], func=AF.Ln,
                                                               scale=1.0 / D, bias=EPS),
                     reads=[t_ps[bk]], writes=[tr])
                S.op("act", lambda r=r: act.activation(out=r[:], in_=r[:], func=AF.Exp, scale=-0.5),
                     reads=[tr], writes=[tr])
                for c in range(NCH):
                    if c % 3 == 2:
                        k = (c // 3) % 2
                        S.op("act", lambda c=c, xb=xb, k=k: act.activation(
                            out=p0tmp[k][:], in_=xb[:, c, :], func=AF.Identity, scale=cols_f[:, c:c + 1]),
                            reads=[txb, t_cst], writes=[t_p0tmp[k]])
                        S.op("pool", lambda c=c, r=r, k=k: pool.tensor_tensor(
                            out=hT[:, c, tsl], in0=p0tmp[k][:], in1=r[:], op=ALU.mult),
                            reads=[t_p0tmp[k], tr], writes=[t_hTp[nb]])
                        continue
                    S.op("dve", lambda c=c, xb=xb, r=r: dve.scalar_tensor_tensor(
                        out=hT[:, c, tsl], in0=xb[:, c, :], scalar=cols_f[:, c:c + 1], in1=r[:],
                        op0=ALU.mult, op1=ALU.mult),
                        reads=[txb, tr, t_cst], writes=[t_hT[nb]])
        S.barrier()
        if STOP <= 0:
            return nc

        def silu_evac(src_ap, dst_ap, tmp_a, tmp_b, t_tmp, reads, writes):
            S.op("act", lambda: act.activation(out=tmp_a, in_=src_ap, func=AF.Exp, scale=-1.0),
                 reads=reads, writes=[t_tmp])
            S.op("dve", lambda: dve.tensor_scalar_add(out=tmp_a, in0=tmp_a, scalar1=1.0),
                 reads=[t_tmp], writes=[t_tmp])
            S.op("dve", lambda: dve.reciprocal(out=tmp_b, in_=tmp_a), reads=[t_tmp], writes=[t_tmp])
            S.op("dve", lambda: dve.tensor_tensor(out=dst_ap, in0=src_ap, in1=tmp_b, op=ALU.mult),
                 reads=list(reads) + [t_tmp], writes=writes)

        with ExitStack() as _es1:
            def A1(name, shape, dt):
                return _es1.enter_context(nc.sbuf_tensor(name, shape, dt))
            wg = A1("wg", [128, NCH, 768], BF16)
            wlr_b = A1("wlr_b", [128, NCH, 16], BF16)
            g_qTs = (A1("g_qTa", [128, 512], BF16), A1("g_qTb", [128, 512], BF16))
            g_kTs = (A1("g_kTa", [128, 512], BF16), A1("g_kTb", [128, 512], BF16))
            g_silus = (A1("g_silua", [128, 2, 512], BF16), A1("g_silub", [128, 2, 512], BF16))
            g_vs = (A1("g_va", [128, 4, 256], BF16), A1("g_vb", [128, 4, 256], BF16))
            g_lr1 = A1("g_lra", [16, 512], F32)
            g_lrs = (g_lr1, g_lr1)
            g_tmpa = A1("g_tmpa", [128, 512], F32)
            g_tmpb = A1("g_tmpb", [128, 512], F32)
            g_mask4 = A1("g_mask4", [128, 4, 128], F32)
            g_e = A1("g_e", [128, 512], F32)
            g_tmpc = g_e
            g_sp = A1("g_sp", [128, 4, 128], F32)
            g_Eq = A1("g_Eq", [128, 512], F32)
            g_Ek = A1("g_Ek", [128, 512], F32)
            g_qe = A1("g_qe", [128, 512], BF16)
            g_ke = A1("g_ke", [128, 512], BF16)
            g_klT = A1("g_klT", [128, 512], BF16)
            g_kl = A1("g_kl", [128, 4, 128], BF16)
            g_scm = A1("g_scm", [128, 512], BF16)
            g_Sf = A1("g_Sf", [128, 256], F32)
            g_Sb = A1("g_Sb", [128, 4, 256], BF16)
            g_sq = A1("g_sq", [128, 2, 512], BF16)
            g_r = A1("g_r", [128, 512], F32)
            g_y = A1("g_y", [128, 2, 512], BF16)
            t_wg, t_wlr, t_mask4 = T(), T(), T()
            t_qT, t_kT, t_silu, t_v = ((T(), T()) for _ in range(4))
            t_lr = (T(),) * 2
            t_tmp = T()
            t_e, t_sp, t_Eq, t_Ek, t_qe, t_ke, t_klT, t_kl, t_scm = (T() for _ in range(9))
            t_Sf, t_sq2, t_r2, t_y = T(), T(), T(), T()
            t_tmpc = t_e
            t_Sb = [T() for _ in range(4)]
            S.dma("pool", wlr_b[:], wlr.rearrange("(c p) n -> p c n", p=128), writes=[t_wlr])
            for cc in range(4):
                S.op("pool", lambda cc=cc: pool.tensor_copy(g_mask4[:, cc, :], Uincl_f), reads=[t_cst], writes=[t_mask4])
            gblocks = [] if SKIP else [(g, nb) for g in range(2) for nb in range(S_LEN // 512)]
            lr_d = nc.dram_tensor("lr_d", [S_LEN // 512, 16, 512], F32).ap()
            t_lrd = [T() for _ in range(S_LEN // 512)]
            rot = [0]

            def nextbank():
                rot[0] ^= 1
                return rot[0]

            def gla_inproj_gen(bi):
                g, nb = gblocks[bi]
                par = bi % 2
                t0 = nb * 512
                tok = slice(t0, t0 + 512)
                hr = h_reads(t0, t0 + 512)
                if nb == 0:
                    S.dma("pool", wg[:], wgla[g].rearrange("(c p) n -> p c n", p=128), writes=[t_wg])
                pending = [None]

                def flush():
                    if pending[0] is not None:
                        pending[0]()
                        pending[0] = None

                def group(lhs_fn, rhs_fn, out_fn, evac, extra_reads):
                    bk = nextbank()
                    for c in range(NCH):
                        S.op("pe", lambda c=c, bk=bk: mm(out_fn(bk), lhsT=lhs_fn(c), rhs=rhs_fn(c),
                                                               start=(c == 0), stop=(c == NCH - 1)),
                             reads=hr + extra_reads, writes=[t_ps[bk]], inc=(c == NCH - 1))
                        if c == 3:
                            flush()
                    pending[0] = lambda bk=bk: evac(bk)

                group(lambda c: wg[:, c, 0:128], lambda c: hT[:, c, tok], lambda bk: bank(bk),
                      lambda bk: S.op("act", lambda: act.activation(out=g_qTs[par][:], in_=bank(bk), func=AF.Identity,
                                                                    scale=128 ** -0.5),
                                      reads=[t_ps[bk]], writes=[t_qT[par]]), [t_wg])
                yield
                group(lambda c: wg[:, c, 128:256], lambda c: hT[:, c, tok], lambda bk: bank(bk),
                      lambda bk: S.op("dve", lambda: dve.tensor_copy(g_kTs[par][:], bank(bk)),
                                      reads=[t_ps[bk]], writes=[t_kT[par]]), [t_wg])
                yield
                for ec in range(2):
                    group(lambda c, ec=ec: wg[:, c, 512 + ec * 128:512 + ec * 128 + 128], lambda c: hT[:, c, tok],
                          lambda bk: bank(bk),
                          lambda bk, ec=ec: silu_evac(bank(bk), g_silus[par][:, ec, :], g_tmpa[:], g_tmpb[:], t_tmp,
                                                      [t_ps[bk]], [t_silu[par]]), [t_wg])
                    yield
                for sb in range(4):
                    group(lambda c, sb=sb: hT[:, c, t0 + sb * 128:t0 + sb * 128 + 128], lambda c: wg[:, c, 256:512],
                          lambda bk: bank(bk)[:, 0:256],
                          lambda bk, sb=sb: S.op("dve", lambda: dve.tensor_copy(g_vs[par][:, sb, :], bank(bk)[:, 0:256]),
                                                 reads=[t_ps[bk]], writes=[t_v[par]]), [t_wg])
                    yield
                if g == 0:
                    group(lambda c: wlr_b[:, c, :], lambda c: hT[:, c, tok], lambda bk: bank(bk)[0:16, :],
                          lambda bk: S.op("dve", lambda: dve.tensor_copy(g_lrs[par][:], bank(bk)[0:16, :]),
                                          reads=[t_ps[bk]], writes=[t_lr[par]]), [t_wlr])
                    flush()
                    S.dma("sp", lr_d[nb], g_lrs[par][:], reads=[t_lr[par]], writes=[t_lrd[nb]])
                else:
                    flush()
                    S.dma("sp", g_lrs[par][:], lr_d[nb], reads=[t_lrd[nb]], writes=[t_lr[par]])
                yield

            def gpump(gen, n):
                if gen is None:
                    return
                for _ in range(n):
                    try:
                        next(gen)
                    except StopIteration:
                        return

            if gblocks:
                gpump(gla_inproj_gen(0), 100)
            for bi, (g, nb) in enumerate(gblocks):
                par = bi % 2
                g_qT, g_kT, g_silu, g_v, g_lr = g_qTs[par], g_kTs[par], g_silus[par], g_vs[par], g_lrs[par]
                tqT, tkT, tsilu, tv, tlr = t_qT[par], t_kT[par], t_silu[par], t_v[par], t_lr[par]
                nxt = gla_inproj_gen(bi + 1) if bi + 1 < len(gblocks) else None
                t0 = nb * 512
                gsl = slice(g * 128, g * 128 + 128)
                if nb == 0:
                    S.op("dve", lambda: dve.memset(g_Sf[:], 0.0), writes=[t_Sf])
                    S.op("dve", lambda: dve.memset(g_Sb[:, 0, :], 0.0), writes=[t_Sb[0]])
                for cc in range(4):
                    cs = slice(cc * 128, cc * 128 + 128)
                    S.op("pe", lambda cs=cs, cc=cc: mm(
                        bank(2)[:, cs], lhsT=g_lr[0:16, cs], rhs=wa2_f[0:16, gsl], start=(cc == 0), stop=False),
                        reads=[tlr, t_cst], writes=[t_ps[2]], inc=False)
                    S.op("pe", lambda cs=cs, cc=cc: mm(
                        bank(2)[:, cs], lhsT=ones_f[0:1, :], rhs=ba_f[0:1, gsl], start=False, stop=True),
                        reads=[t_cst], writes=[t_ps[2]], inc=(cc == 3))
                gpump(nxt, 1)
                S.op("act", lambda: act.activation(out=g_e[:], in_=bank(2), func=AF.Exp, scale=-1.0),
                     reads=[t_ps[2]], writes=[t_e])
                S.op("act", lambda: act.activation(out=g_sp[:].rearrange("p a b -> p (a b)"), in_=g_e[:], func=AF.Ln, bias=1.0),
                     reads=[t_e], writes=[t_sp])
                for cc in range(4):
                    cs = slice(cc * 128, cc * 128 + 128)
                    S.op("pe", lambda cs=cs, cc=cc: mm(bank(3)[:, cs], lhsT=g_sp[:, cc, :], rhs=Uincl_f,
                                                              start=(cc == 0), stop=True),
                         reads=[t_sp, t_cst], writes=[t_ps[3]], inc=(cc == 3))
                gpump(nxt, 1)
                S.op("act", lambda: act.activation(out=g_Eq[:], in_=bank(3), func=AF.Exp, scale=-1.0 / 16),
                     reads=[t_ps[3]], writes=[t_Eq])
                S.op("act", lambda: act.activation(out=g_Ek[:], in_=bank(3), func=AF.Exp, scale=1.0 / 16),
                     reads=[t_ps[3]], writes=[t_Ek])
                S.op("dve", lambda: dve.tensor_tensor(out=g_ke[:], in0=g_kT[:], in1=g_Ek[:], op=ALU.mult),
                     reads=[tkT, t_Ek], writes=[t_ke])
                for cc in range(4):
                    cs = slice(cc * 128, cc * 128 + 128)
                    S.op("dve", lambda cs=cs, cc=cc: dve.scalar_tensor_tensor(
                        out=g_klT[:, cs], in0=g_kT[:, cs], scalar=g_Eq[:, cc * 128 + 127:cc * 128 + 128], in1=g_Ek[:, cs],
                        op0=ALU.mult, op1=ALU.mult),
                        reads=[tkT, t_Eq, t_Ek], writes=[t_klT])
                S.op("dve", lambda: dve.tensor_tensor(out=g_qe[:], in0=g_qT[:], in1=g_Eq[:], op=ALU.mult),
                     reads=[tqT, t_Eq], writes=[t_qe])
                for cc in range(4):
                    cs = slice(cc * 128, cc * 128 + 128)
                    S.op("pe", lambda cs=cs, cc=cc: mm(bank(4)[:, cs], lhsT=g_klT[:, cs], rhs=ident_b,
                                                              start=(cc == 0), stop=True),
                         reads=[t_klT, t_cst], writes=[t_ps[4]], inc=(cc == 3))
                for cc in range(4):
                    cs = slice(cc * 128, cc * 128 + 128)
                    S.op("pe", lambda cs=cs, cc=cc: mm(bank(5)[:, cs], lhsT=g_ke[:, cs], rhs=g_qe[:, cs],
                                                              start=(cc == 0), stop=True),
                         reads=[t_ke, t_qe], writes=[t_ps[5]], inc=(cc == 3))
                gpump(nxt, 1)
                S.op("act", lambda: act.activation(out=g_kl[:].rearrange("p a b -> p (a b)"), in_=bank(4), func=AF.Copy),
                     reads=[t_ps[4]], writes=[t_kl])
                S.op("dve", lambda: dve.tensor_tensor(out=g_scm[:], in0=bank(5), in1=g_mask4[:].rearrange("p a b -> p (a b)"),
                                                      op=ALU.mult),
                     reads=[t_ps[5], t_mask4], writes=[t_scm])
                for cc in range(4):
                    S.op("pe", lambda cc=cc: mm(
                        bank(2 + cc // 2)[:, (cc % 2) * 256:(cc % 2) * 256 + 256], lhsT=g_kl[:, cc, :], rhs=g_v[:, cc, :],
                        start=(cc % 2 == 0), stop=True),
                        reads=[t_kl, tv], writes=[t_ps[2 + cc // 2]])
                gpump(nxt, 1)
                for cc in range(4):
                    cs = slice(cc * 128, cc * 128 + 128)
                    for ec in range(2):
                        es = slice(ec * 128, ec * 128 + 128)
                        S.op("pe", lambda ec=ec, es=es, cs=cs, cc=cc: mm(
                            bank(6 + ec)[:, cs], lhsT=g_Sb[:, cc, es], rhs=g_qe[:, cs], start=(cc == 0), stop=False),
                            reads=[t_Sb[cc], t_qe], writes=[t_ps[6 + ec]], inc=False)
                        S.op("pe", lambda ec=ec, es=es, cs=cs, cc=cc: mm(
                            bank(6 + ec)[:, cs], lhsT=g_v[:, cc, es], rhs=g_scm[:, cs], start=False, stop=True),
                            reads=[tv, t_scm], writes=[t_ps[6 + ec]])
                    S.op("dve", lambda cc=cc: dve.scalar_tensor_tensor(
                        out=g_Sf[:], in0=g_Sf[:], scalar=g_Eq[:, cc * 128 + 127:cc * 128 + 128],
                        in1=bank(2 + cc // 2)[:, (cc % 2) * 256:(cc % 2) * 256 + 256], op0=ALU.mult, op1=ALU.add),
                        reads=[t_Sf, t_Eq, t_ps[2 + cc // 2]], writes=[t_Sf])
                    S.op("pool", lambda cc=cc: pool.tensor_copy(g_Sb[:, (cc + 1) % 4, :], g_Sf[:]),
                         reads=[t_Sf], writes=[t_Sb[(cc + 1) % 4]])
                    gpump(nxt, 1)
                gpump(nxt, 100)
                for ec in range(2):
                    S.op("act", lambda ec=ec: act.activation(out=g_sq[:, ec, :], in_=bank(6 + ec), func=AF.Square),
                         reads=[t_ps[6 + ec]], writes=[t_sq2])
                for ec in range(2):
                    S.op("pe", lambda ec=ec: mm(bank(5), lhsT=ones_b, rhs=g_sq[:, ec, :], start=(ec == 0), stop=(ec == 1)),
                         reads=[t_sq2, t_cst], writes=[t_ps[5]], inc=(ec == 1))
                S.op("act", lambda: act.activation(out=g_r[:], in_=bank(5), func=AF.Ln, scale=1.0 / 256, bias=EPS),
                     reads=[t_ps[5]], writes=[t_r2])
                S.op("act", lambda: act.activation(out=g_r[:], in_=g_r[:], func=AF.Exp, scale=-0.5),
                     reads=[t_r2], writes=[t_r2])
                for ec in range(2):
                    S.op("dve", lambda ec=ec: dve.scalar_tensor_tensor(
                        out=g_tmpc[:], in0=bank(6 + ec), scalar=cols_f[:, 48 + ec:49 + ec], in1=g_r[:],
                        op0=ALU.mult, op1=ALU.mult),
                        reads=[t_ps[6 + ec], t_r2, t_cst], writes=[t_tmpc])
                    S.op("dve", lambda ec=ec: dve.tensor_tensor(out=g_y[:, ec, :], in0=g_tmpc[:], in1=g_silu[:, ec, :], op=ALU.mult),
                         reads=[t_tmpc, tsilu], writes=[t_y])
                S.dma("sp", mg_own[t0 // EB, g * 256:(g + 1) * 256, t0 % EB:t0 % EB + 512].rearrange("(e p) t -> p e t", p=128), g_y[:],
                      reads=[t_y])
        S.barrier()

        NKB = S_LEN // 128
        with ExitStack() as _es2:
            def A2(name, shape, dt):
                return _es2.enter_context(nc.sbuf_tensor(name, shape, dt))
            ws = A2("ws", [128, NCH, 512], BF16)
            s_kT = A2("s_kT", [128, S_LEN], BF16)
            s_v = A2("s_v", [128, NKB, 128], BF16)
            s_qTs = (A2("s_qTa", [128, TQ], BF16), A2("s_qTb", [128, TQ], BF16))
            s_gss = (A2("s_gsa", [128, TQ], BF16), A2("s_gsb", [128, TQ], BF16))
            s_ta = A2("s_ta", [128, 512], F32)
            s_tb = A2("s_tb", [128, 512], F32)
            e1s = (A2("s_e1a", [128, TQ], F32), A2("s_e1b", [128, TQ], F32), A2("s_e1c", [128, TQ], F32))
            sps = (A2("s_spa", [128, TQ], BF16), A2("s_spb", [128, TQ], BF16))
            gs_ = (A2("s_ga", [128, TQ], BF16), A2("s_gb", [128, TQ], BF16))
            ws_ = (A2("s_wa", [128, TQ], BF16), A2("s_wb", [128, TQ], BF16))
            s_y = A2("s_y", [128, TQ], BF16)
            t_ws, t_sy, t_stmp, t_msown = T(), T(), T(), T()
            t_skT = [T() for _ in range(S_LEN // 512)]
            t_sv = [T() for _ in range(S_LEN // 512)]
            t_sqT, t_sgs = (T(), T()), (T(), T())
            t_e1, t_sps, t_gs_, t_ws_ = (T(), T(), T()), (T(), T()), (T(), T()), (T(), T())
            ZB, BB, OB = 0, 2, 4
            blocks = [] if SKIP else [(hd, tb) for hd in range(4) for tb in range(S_LEN // TQ)]

            def inproj_gen(bi):
                hd, tb = blocks[bi]
                s_qT, s_gs = s_qTs[bi % 2], s_gss[bi % 2]
                tq_, tg_ = t_sqT[bi % 2], t_sgs[bi % 2]
                q0 = tb * TQ
                if tb == 0:
                    S.dma("pool", ws[:], wsb[hd].rearrange("(c p) n -> p c n", p=128), writes=[t_ws])
                pending = [None]

                def flush():
                    if pending[0] is not None:
                        pending[0]()
                        pending[0] = None

                for half in range(NBK):
                    t0 = q0 + half * 512
                    tok = slice(t0, t0 + 512)
                    loc = slice(half * 512, half * 512 + 512)
                    hr = h_reads(t0, t0 + 512)
                    for (col0, bk_, evac) in (
                        (0, 6, lambda loc=loc: S.op("dve", lambda: dve.tensor_scalar_mul(
                            out=s_qT[:, loc], in0=bank(6), scalar1=128 ** -0.5), reads=[t_ps[6]], writes=[tq_])),
                        (128, 7, lambda tok=tok, t0=t0: S.op("dve", lambda: dve.tensor_copy(s_kT[:, tok], bank(7)),
                                                             reads=[t_ps[7]], writes=[t_skT[t0 // 512]])),
                        (384, 6, lambda loc=loc: silu_evac(bank(6), s_gs[:, loc], s_ta[:], s_tb[:], t_stmp,
                                                           [t_ps[6]], [tg_])),
                    ):
                        for c in range(NCH):
                            S.op("pe", lambda c=c, col0=col0, bk_=bk_: mm(
                                bank(bk_), lhsT=ws[:, c, col0:col0 + 128], rhs=hT[:, c, tok],
                                start=(c == 0), stop=(c == NCH - 1)),
                                reads=hr + [t_ws], writes=[t_ps[bk_]], inc=(c == NCH - 1))
                            if c % 4 == 3:
                                if c == 3:
                                    flush()
                                yield
                        pending[0] = evac
                    for sb in range(4):
                        kb = (t0 // 128) + sb
                        for c in range(NCH):
                            S.op("pe", lambda c=c, kb=kb, sb=sb: mm(
                                bank(7)[:, sb * 128:sb * 128 + 128], lhsT=hT[:, c, kb * 128:kb * 128 + 128],
                                rhs=ws[:, c, 256:384], start=(c == 0 and sb == 0), stop=(c == NCH - 1)),
                                reads=hr + [t_ws], writes=[t_ps[7]], inc=(c == NCH - 1 and sb == 3))
                            if c % 4 == 3:
                                if c == 3 and sb == 0:
                                    flush()
                                yield
                    pending[0] = (lambda t0=t0: S.op("dve", lambda: dve.tensor_copy(
                        s_v[:, t0 // 128:t0 // 128 + 4, :], bank(7).rearrange("p (a b) -> p a b", a=4)),
                        reads=[t_ps[7]], writes=[t_sv[t0 // 512]]))
                flush()
                yield

            NYIELD = NBK * 28 + 1

            def pump(gen, n):
                if gen is None:
                    return
                for _ in range(n):
                    try:
                        next(gen)
                    except StopIteration:
                        return

            if blocks:
                pump(inproj_gen(0), 10 ** 6)
                for k in range(4):
                    issue_cc(mg_own[k], mg_all[k])
                for nm, src, dst in (("wo", wo, wo16), ("wgt", wgt, wgt16)):
                    for j in range(4):
                        S.dma("pool", dst[j * 512:(j + 1) * 512, :], src[j * 512:(j + 1) * 512, :],
                              writes=[t_w16[nm][j]])
            for bi, (hd, tb) in enumerate(blocks):
                    q0 = tb * TQ
                    s_qT, s_gs = s_qTs[bi % 2], s_gss[bi % 2]
                    tq_, tg_ = t_sqT[bi % 2], t_sgs[bi % 2]
                    nxt = inproj_gen(bi + 1) if bi + 1 < len(blocks) else None
                    kbs = list(range((tb + 1) * (TQ // 128) - 1, -1, -1))
                    P = len(kbs)
                    npump = (NYIELD + P - 1) // P
                    startedB = [False] * NBK
                    startedO = [False] * NBK

                    def geom(p):
                        kb = kbs[p]
                        off = kb * 128 - q0
                        lo = max(0, off)
                        segs = []
                        for bki in range(NBK):
                            c0 = max(lo, bki * 512)
                            c1 = (bki + 1) * 512
                            if c0 < c1:
                                segs.append((bki, c0, c1))
                        return kb, off, lo, segs

                    def emit_Z(p):
                        kb, off, lo, segs = geom(p)
                        for (bki, c0, c1) in segs:
                            S.op("pe", lambda bki=bki, c0=c0, c1=c1, kb=kb: mm(
                                bank(ZB + bki)[:, c0 - bki * 512:c1 - bki * 512],
                                lhsT=s_kT[:, kb * 128:kb * 128 + 128], rhs=s_qT[:, c0:c1], start=True, stop=True),
                                reads=[t_skT[kb // 4], tq_], writes=[t_ps[ZB + bki]])

                    def zb_reads(segs, base):
                        return [t_ps[base + bki] for (bki, _, _) in segs]

                    def emit_E1(p):
                        kb, off, lo, segs = geom(p)
                        e1, te1 = e1s[p % 3], t_e1[p % 3]
                        S.op("act", lambda: act.activation(
                            out=e1[:, lo:TQ], in_=PS[:, ZB * 512 + lo:ZB * 512 + TQ], func=AF.Exp),
                            reads=zb_reads(segs, ZB), writes=[te1])
                        if off >= 0:
                            S.op("dve", lambda: dve.tensor_tensor(
                                out=e1[:, off:off + 128], in0=e1[:, off:off + 128], in1=Ustr_f, op=ALU.mult),
                                reads=[te1, t_cst], writes=[te1])

                    emit_Z(0)
                    emit_E1(0)
                    if P > 1:
                        emit_Z(1)
                    for p in range(P + 1):
                        if p >= 1:
                            kbp, offp, lop, segsp = geom(p - 1)
                            e1p, te1p = e1s[(p - 1) % 3], t_e1[(p - 1) % 3]
                            spp, tspp = sps[(p - 1) % 2], t_sps[(p - 1) % 2]
                            gp, tgp = gs_[(p - 1) % 2], t_gs_[(p - 1) % 2]
                            wp_, twp = ws_[(p - 1) % 2], t_ws_[(p - 1) % 2]
                            S.op("act", lambda gp=gp, lop=lop: act.activation(
                                out=gp[:, lop:TQ], in_=PS[:, BB * 512 + lop:BB * 512 + TQ], func=AF.Exp, scale=-1.0),
                                reads=zb_reads(segsp, BB), writes=[tgp])
                            if p - 1 < P - 1:
                                for (bki, c0, c1) in segsp:
                                    S.op("pe", lambda bki=bki, c0=c0, c1=c1, spp=spp: mm(
                                        bank(BB + bki)[:, c0 - bki * 512:c1 - bki * 512], lhsT=Ustr_b, rhs=spp[:, c0:c1],
                                        start=False, stop=True),
                                        reads=[tspp, t_cst], writes=[t_ps[BB + bki]])
                        if p < P:
                            kb, off, lo, segs = geom(p)
                            e1, te1 = e1s[p % 3], t_e1[p % 3]
                            sp_, tsp = sps[p % 2], t_sps[p % 2]
                            S.op("act", lambda e1=e1, sp_=sp_, lo=lo: act.activation(
                                out=sp_[:, lo:TQ], in_=e1[:, lo:TQ], func=AF.Ln, bias=1.0),
                                reads=[te1], writes=[tsp])
                            for (bki, c0, c1) in segs:
                                S.op("pe", lambda bki=bki, c0=c0, c1=c1, sp_=sp_, st=(not startedB[bki]): mm(
                                    bank(BB + bki)[:, c0 - bki * 512:c1 - bki * 512], lhsT=Lincl_b, rhs=sp_[:, c0:c1],
                                    start=st, stop=True),
                                    reads=[tsp, t_cst], writes=[t_ps[BB + bki]])
                                startedB[bki] = True
                        if p + 1 < P:
                            emit_E1(p + 1)
                        pump(nxt, npump // 2)
                        if p + 2 < P:
                            emit_Z(p + 2)
                        if p >= 1:
                            S.op("dve", lambda wp_=wp_, e1p=e1p, gp=gp, lop=lop: dve.tensor_tensor(
                                out=wp_[:, lop:TQ], in0=e1p[:, lop:TQ], in1=gp[:, lop:TQ], op=ALU.mult),
                                reads=[te1p, tgp], writes=[twp])
                            for (bki, c0, c1) in segsp:
                                S.op("pe", lambda bki=bki, c0=c0, c1=c1, wp_=wp_, kbp=kbp, st=(not startedO[bki]): mm(
                                    bank(OB + bki)[:, c0 - bki * 512:c1 - bki * 512], lhsT=s_v[:, kbp, :], rhs=wp_[:, c0:c1],
                                    start=st, stop=True),
                                    reads=[t_sv[kbp // 4], twp], writes=[t_ps[OB + bki]])
                                startedO[bki] = True
                        pump(nxt, npump - npump // 2)
                    pump(nxt, 10 ** 6)
                    S.op("dve", lambda: dve.tensor_tensor(out=s_y[:], in0=PS[:, OB * 512:OB * 512 + TQ], in1=s_gs[:], op=ALU.mult),
                         reads=[t_ps[OB + i] for i in range(NBK)] + [tg_], writes=[t_sy])
                    S.dma("sp", ms_own[hd, q0 // HS, :, q0 % HS:q0 % HS + TQ], s_y[:], reads=[t_sy], writes=[t_msown])
                    if tb == S_LEN // TQ - 1 and hd < 3:
                        S._wait("pool", S._deps("pool", [t_msown], []))
                        issue_cc(ms_own[hd].rearrange("hh p t -> (hh p) t"), ms_all[hd])
        S.barrier(skip=("cc",))

    BT = 256
    if mode == "AB":
        par = nc.sync.partition_id() % 2
        mg_v = mg_all.rearrange("k (c p) t -> k p c t", p=128)
        ms_v = ms_all.rearrange("h q t -> (h q) t").rearrange("(hr hh p) t -> hh p hr t", hr=8, hh=2)
    else:
        mixin_v = mixin.rearrange("(c p) t -> p c t", p=128)
    with ExitStack() as _es3:
        wo_b = _es3.enter_context(nc.sbuf_tensor("wo_b", [128, NCH, D], BF16))
        wgt_b = _es3.enter_context(nc.sbuf_tensor("wgt_b", [128, NCH, D], BF16))
        wp_b = _es3.enter_context(nc.sbuf_tensor("wp_b", [128, 2, D], BF16))
        mxa = _es3.enter_context(nc.sbuf_tensor("mxa", [128, NCH, BT], BF16))
        xra = _es3.enter_context(nc.sbuf_tensor("xra", [128, NCH, BT], F32))
        pbts = (_es3.enter_context(nc.sbuf_tensor("pbt", [128, 2, BT], BF16)), _es3.enter_context(nc.sbuf_tensor("pbt2", [128, 2, BT], BF16)))
        m1 = _es3.enter_context(nc.sbuf_tensor("m1", [128, NCH, BT], F32))
        sq3 = _es3.enter_context(nc.sbuf_tensor("sq3", [128, NCH, BT], BF16))
        r3 = _es3.enter_context(nc.sbuf_tensor("r3", [128, BT], F32))
        gte = _es3.enter_context(nc.sbuf_tensor("gte", [128, 2, BT], F32))
        oo = _es3.enter_context(nc.sbuf_tensor("oo", [128, 4, BT], F32))
        t_wo = [T() for _ in range(4)]
        t_wgt = [T() for _ in range(4)]
        t_wp = T()
        for j in range(4):
            S.dma("sp", wo_b[:, 4 * j:4 * j + 4, :],
                  wo16[j * 512:(j + 1) * 512, :].rearrange("(c p) n -> p c n", p=128),
                  reads=[t_w16["wo"][j]], writes=[t_wo[j]])
        for j in range(4):
            S.dma("sp", wgt_b[:, 4 * j:4 * j + 4, :],
                  wgt16[j * 512:(j + 1) * 512, :].rearrange("(c p) n -> p c n", p=128),
                  reads=[t_w16["wgt"][j]], writes=[t_wgt[j]])
        S.dma("pool", wp_b[:], wp.rearrange("(c p) n -> p c n", p=128), writes=[t_wp])
        if not SKIP:
            issue_cc(ms_own[3].rearrange("hh p t -> (hh p) t"), ms_all[3])
        mxs, t_mx = (mxa, mxa), (T(),) * 2
        xrs, t_xr = (xra, xra), (T(),) * 2
        h1b = sq3
        t_pb, t_m1, t_sq3, t_r3 = T(), T(), T(), T()
        t_h1b = t_sq3
        t_gte = (T(), T())
        t_oo = [T() for _ in range(4)]
        pTo_v = pTo.rearrange("(c p) t -> p c t", p=128)
        outT_v = outT.rearrange("(c p) t -> p c t", p=128)
        rot3 = [0]

        def nb3():
            rot3[0] = (rot3[0] + 1) % 8
            return rot3[0]

        t_m1c = [T() for _ in range(NCH)]
        t_mxp = [T() for _ in range(5)]
        t_sqc = [T() for _ in range(NCH)]
        t_pbs = (T(), T())
        NB3 = HS // BT

        def emit_loads(nb):
            tsl = slice(nb * BT, (nb + 1) * BT)
            if mode == "AB":
                jj = (nb * BT) // EB
                cc0 = (nb * BT) % EB
                S.dma("sp", mxa[:, 0:8, :], mg_v[bass.ds(par * 2 + jj, 1), :, :, cc0:cc0 + BT].rearrange("o p c t -> p (o c) t"),
                      reads=[t_mix], writes=[t_mxp[0]])
                S.dma("sp", mxa[:, 8:16, :],
                      ms_v[bass.ds(par, 1), :, :, nb * BT:(nb + 1) * BT].rearrange("o p r t -> p (o r) t"),
                      reads=[t_mix], writes=[t_mxp[1]])
            else:
                S.dma("sp", mxa[:], mixin_v[:, :, tsl], writes=[t_mx[0]])
            S.dma("sp", xra[:], xTo[nb].rearrange("p (c t) -> p c t", c=NCH), writes=[t_xr[0]])
            S.dma("pool", pbts[nb % 2][:], pTo_v[:, :, tsl], writes=[t_pbs[nb % 2]])

        for nb in range(NB3):
            emit_loads(nb)
            tsl = slice(nb * BT, (nb + 1) * BT)
            mx, tmx = mxa, t_mx[0]
            xr, txr = xra, t_xr[0]
            pbt, t_pb = pbts[nb % 2], t_pbs[nb % 2]
            for oc in range(NCH):
                bk = nb3()
                for c in range(NCH):
                    S.op("pe", lambda c=c, oc=oc, bk=bk: mm(
                        bank(bk)[:, 0:BT], lhsT=wo_b[:, c, oc * 128:(oc + 1) * 128], rhs=mx[:, c, :],
                        start=(c == 0), stop=(c == NCH - 1)),
                        reads=[t_wo[c // 4], tmx] + t_mxp, writes=[t_ps[bk]], inc=(c == NCH - 1))
                S.op("act", lambda oc=oc, bk=bk: act.activation(out=sq3[:, oc, :], in_=bank(bk)[:, 0:BT], func=AF.Square),
                     reads=[t_ps[bk]], writes=[t_sqc[oc]])
                S.op("dve", lambda oc=oc, bk=bk: dve.tensor_scalar_mul(
                    out=m1[:, oc, :], in0=bank(bk)[:, 0:BT], scalar1=cols_f[:, 16 + oc:17 + oc]),
                    reads=[t_ps[bk], t_cst], writes=[t_m1c[oc]])
            bk = nb3()
            for c in range(NCH):
                S.op("pe", lambda c=c, bk=bk: mm(bank(bk)[:, 0:BT], lhsT=ones_b, rhs=sq3[:, c, :],
                                                       start=(c == 0), stop=(c == NCH - 1)),
                     reads=[t_sqc[c], t_cst], writes=[t_ps[bk]], inc=(c == NCH - 1))
            S.op("act", lambda bk=bk: act.activation(out=r3[:], in_=bank(bk)[:, 0:BT], func=AF.Ln, scale=1.0 / D, bias=EPS),
                 reads=[t_ps[bk]], writes=[t_r3])
            S.op("act", lambda: act.activation(out=r3[:], in_=r3[:], func=AF.Exp, scale=-0.5),
                 reads=[t_r3], writes=[t_r3])
            for oc in range(NCH):
                S.op("dve", lambda oc=oc: dve.tensor_tensor(out=m1[:, oc, :], in0=m1[:, oc, :], in1=r3[:], op=ALU.mult),
                     reads=[t_m1c[oc], t_r3], writes=[t_m1c[oc]])
                S.op("pool", lambda oc=oc: pool.tensor_tensor(out=m1[:, oc, :], in0=m1[:, oc, :], in1=xr[:, oc, :], op=ALU.add),
                     reads=[t_m1c[oc], txr], writes=[t_m1c[oc]])
                S.op("act", lambda oc=oc: act.activation(out=h1b[:, oc, :], in_=m1[:, oc, :], func=AF.Copy),
                     reads=[t_m1c[oc]], writes=[t_sqc[oc]])
            for oc in range(NCH):
                bk = nb3()
                for c in range(NCH):
                    S.op("pe", lambda c=c, oc=oc, bk=bk: mm(
                        bank(bk)[:, 0:BT], lhsT=wgt_b[:, c, oc * 128:(oc + 1) * 128], rhs=h1b[:, c, :],
                        start=(c == 0), stop=(c == NCH - 1)),
                        reads=[t_wgt[c // 4], t_sqc[c]], writes=[t_ps[bk]], inc=(c == NCH - 1))
                gt, tgt = gte[:, oc % 2, :], t_gte[oc % 2]
                S.op("act", lambda oc=oc, bk=bk, gt=gt: act.activation(
                    out=gt, in_=bank(bk)[:, 0:BT], func=AF.Sigmoid, bias=cols_f[:, 32 + oc:33 + oc]),
                    reads=[t_ps[bk], t_cst], writes=[tgt])
                bk2 = nb3()
                for c in range(2):
                    S.op("pe", lambda c=c, oc=oc, bk2=bk2: mm(
                        bank(bk2)[:, 0:BT], lhsT=wp_b[:, c, oc * 128:(oc + 1) * 128], rhs=pbt[:, c, :],
                        start=(c == 0), stop=(c == 1)),
                        reads=[t_wp, t_pb], writes=[t_ps[bk2]], inc=(c == 1))
                S.op("dve", lambda oc=oc, bk2=bk2, gt=gt: dve.tensor_tensor(
                    out=gt, in0=gt, in1=bank(bk2)[:, 0:BT], op=ALU.mult),
                    reads=[tgt, t_ps[bk2]], writes=[tgt])
                S.op("dve", lambda oc=oc, gt=gt: dve.tensor_tensor(
                    out=oo[:, oc % 4, :], in0=gt, in1=m1[:, oc, :], op=ALU.add),
                    reads=[tgt, t_m1c[oc]], writes=[t_oo[oc % 4]])
                if oc % 4 == 3:
                    S.dma("pool", outT_v[:, oc - 3:oc + 1, tsl], oo[:], reads=t_oo)
        S.barrier()
    return nc


_PROG = {}


def kernel(x, p, g_pre, w_in, w_a2, b_a, g_gla_head, w_out, g_post, w_ple_gate, b_ple_gate, w_ple_proj):
    x = np.asarray(x, np.float32)
    B, S_LEN, _ = x.shape
    HS = S_LEN // 2
    f = lambda a: np.ascontiguousarray(np.asarray(a, np.float32))
    p, g_pre, w_in, w_a2, b_a = f(p), f(g_pre), f(w_in), f(w_a2), f(b_a)
    g_gla_head, w_out, g_post = f(g_gla_head), f(w_out), f(g_post)
    w_ple_gate, b_ple_gate, w_ple_proj = f(w_ple_gate), f(b_ple_gate), f(w_ple_proj)
    W = w_in[0]
    GQ, GK, GV, GG, LR, SQ, SK, SV, SG = 0, 512, 1024, 2048, 3072, 3088, 4112, 5136, 6160

    def colv(v):
        return v.reshape(-1, 128).T

    ii = np.arange(128)
    cst = np.zeros((128, 5, 128), np.float32)
    cst[:, 0, :] = (ii[:, None] == ii[None, :])
    cst[:, 1, :] = (ii[:, None] <= ii[None, :])
    cst[:, 2, :] = (ii[:, None] < ii[None, :])
    cst[:, 3, :] = (ii[:, None] >= ii[None, :])
    cst[:, 4, :] = 1.0
    cols = np.zeros((128, 64), np.float32)
    cols[:, 0:16] = colv(g_pre[0])
    cols[:, 16:32] = colv(g_post[0])
    cols[:, 32:48] = colv(b_ple_gate[0])
    cols[:, 48:50] = colv(g_gla_head[0])
    in_maps = []
    for core in range(8):
        b, hh = core // 2, core % 2
        wsb = np.stack([np.concatenate([W[:, SQ + 128 * H:SQ + 128 * H + 128], W[:, SK + 128 * H:SK + 128 * H + 128],
                                        W[:, SV + 128 * H:SV + 128 * H + 128], W[:, SG + 128 * H:SG + 128 * H + 128]], axis=1)
                        for H in range(4 * hh, 4 * hh + 4)])
        wgla = np.stack([np.concatenate([W[:, GQ + 128 * G:GQ + 128 * G + 128], W[:, GK + 128 * G:GK + 128 * G + 128],
                                         W[:, GV + 256 * G:GV + 256 * G + 256], W[:, GG + 256 * G:GG + 256 * G + 256]], axis=1)
                         for G in range(2 * hh, 2 * hh + 2)])
        xtile = np.ascontiguousarray(x[b].reshape(S_LEN // 256, 256, NCH, 128).transpose(0, 3, 2, 1)).reshape(
            S_LEN // 256, 128, NCH * 256)
        wo_perm = np.concatenate([w_out[0][0:1024]] + [w_out[0][1024 + (4 * r + h) * 128:1024 + (4 * r + h) * 128 + 128]
                                                       for h in range(4) for r in range(2)], axis=0)
        in_maps.append({
            "xT": xtile,
            "xTo": np.ascontiguousarray(xtile[hh * (HS // 256):(hh + 1) * (HS // 256)]),
            "pTo": np.ascontiguousarray(p[0, b, hh * HS:(hh + 1) * HS].T),
            "wsb": np.ascontiguousarray(wsb),
            "wgla": np.ascontiguousarray(wgla),
            "wlr": np.ascontiguousarray(W[:, LR:LR + 16]),
            "wa2": np.ascontiguousarray(w_a2[0][:, 256 * hh:256 * hh + 256]),
            "ba": np.ascontiguousarray(b_a[0][None, 256 * hh:256 * hh + 256]),
            "cols": cols,
            "wo": np.ascontiguousarray(wo_perm),
            "wgt": w_ple_gate[0],
            "wp": w_ple_proj[0],
            "cst": cst,
        })
    if S_LEN not in _PROG:
        _PROG[S_LEN] = build_program(S_LEN, "AB")
    res = run_bass_kernel_spmd(_PROG[S_LEN], in_maps, core_ids=list(range(8)))
    out = np.empty((B, S_LEN, D), np.float32)
    for core in range(8):
        b, hh = core // 2, core % 2
        out[b, hh * HS:(hh + 1) * HS, :] = res.results[core]["outT"].T
    return out
```

```python
from contextlib import ExitStack
import numpy as np
import concourse.bass as bass
import concourse.mybir as mybir
from concourse.bass_utils import run_bass_kernel_spmd

F32 = mybir.dt.float32
BF16 = mybir.dt.bfloat16
AF = mybir.ActivationFunctionType
ALU = mybir.AluOpType

D = 2048
NCH = 16
EPS = 1e-6
TQ = 1024
NBK = TQ // 512


class T:
    __slots__ = ("w", "r", "x")

    def __init__(self, x=False):
        self.w = None
        self.r = []
        self.x = x


class Sched:
    ENG = ("pe", "act", "dve", "pool", "sp")

    def __init__(self, nc, n_dma_sems=28):
        self.nc = nc
        self.e = dict(pe=nc.tensor, act=nc.scalar, dve=nc.vector, pool=nc.gpsimd, sp=nc.sync)
        self.sems = {}
        self.cnt = {}
        for k in self.ENG:
            self.sems[k] = nc.alloc_semaphore("s_" + k)
            self.cnt[k] = 0
        self.dma_keys = []
        for i in range(n_dma_sems):
            k = "d%d" % i
            self.sems[k] = nc.alloc_semaphore("s_" + k)
            self.cnt[k] = 0
            self.dma_keys.append(k)
        self.sems["cc"] = nc.alloc_semaphore("s_cc")
        self.cnt["cc"] = 0
        self.dma_rr = 0
        self.seen = {k: {} for k in self.ENG}

    def _deps(self, eng, reads, writes):
        need = {}

        def add(d, same_ok):
            if d is None:
                return
            k, v = d
            if k == eng and same_ok:
                return
            if need.get(k, 0) < v:
                need[k] = v
        for t in reads:
            add(t.w, False)
            if t.x:
                for d in t.r:
                    add(d, True)
        for t in writes:
            add(t.w, True)
            for d in t.r:
                add(d, False)
        return need

    def _wait(self, eng, need):
        seen = self.seen[eng]
        for k, v in need.items():
            if seen.get(k, 0) < v:
                self.e[eng].wait_ge(self.sems[k], v)
                seen[k] = v

    def _record(self, d, reads, writes):
        for t in reads:
            t.r.append(d)
        for t in writes:
            t.w = d
            t.r = []

    def op(self, eng, fn, reads=(), writes=(), inc=True):
        self._wait(eng, self._deps(eng, reads, writes))
        ins = fn()
        if inc:
            self.cnt[eng] += 1
            ins.then_inc(self.sems[eng], 1)
            seq = self.cnt[eng]
        else:
            seq = self.cnt[eng] + 1
        self._record((eng, seq), reads, writes)
        return ins

    def dma(self, eng, out, in_, reads=(), writes=()):
        self._wait(eng, self._deps(eng, reads, writes))
        k = self.dma_keys[self.dma_rr]
        self.dma_rr = (self.dma_rr + 1) % len(self.dma_keys)
        ins = self.e[eng].dma_start(out=out, in_=in_)
        self.cnt[k] += 16
        ins.then_inc(self.sems[k], 16)
        self._record((k, self.cnt[k]), reads, writes)
        return ins

    def barrier(self, skip=()):
        for eng in self.ENG:
            need = {k: v for k, v in self.cnt.items() if v > 0 and k not in skip}
            self._wait(eng, need)


def build_program(S_LEN, mode="AB"):
    STOP = 9
    P3 = 9
    HS = S_LEN // 2
    nc = bass.Bass("TRN2", target_bir_lowering=False)
    S = Sched(nc)
    pe, act, dve, pool = nc.tensor, nc.scalar, nc.vector, nc.gpsimd

    def mm(out, **kw):
        return pe.matmul(out, skip_group_check=True, **kw)

    def din(name, shape, dt=F32):
        return nc.dram_tensor(name, shape, dt, kind="ExternalInput").ap()

    xT = din("xT", [S_LEN // 256, 128, NCH * 256])
    xTo = din("xTo", [HS // 256, 128, NCH * 256])
    pTo = din("pTo", [256, HS])
    wsb = din("wsb", [4, D, 512])
    wgla = din("wgla", [2, D, 768])
    wlr = din("wlr", [D, 16])
    wa2 = din("wa2", [16, 256])
    ba = din("ba", [1, 256])
    cols = din("cols", [128, 64])
    wo = din("wo", [D, D])
    wgt = din("wgt", [D, D])
    wp = din("wp", [256, D])
    cst = din("cst", [128, 5, 128])
    outT = nc.dram_tensor("outT", [D, HS], F32, kind="ExternalOutput").ap()
    if mode == "A":
        mix_own = nc.dram_tensor("mix_own", [4, 1024, S_LEN // 4], BF16, kind="ExternalOutput").ap()
    else:
        mix_own = nc.dram_tensor("mix_own", [4, 1024, S_LEN // 4], BF16).ap()
    mix_all = nc.dram_tensor("mix_all", [4, 2048, S_LEN // 4], BF16).ap()
    mg_own = nc.dram_tensor("mg_own", [4, 512, S_LEN // 4], BF16).ap()
    mg_all = nc.dram_tensor("mg_all", [4, 1024, S_LEN // 4], BF16).ap()
    ms_own = nc.dram_tensor("ms_own", [4, 2, 128, HS], BF16).ap()
    ms_all = nc.dram_tensor("ms_all", [4, 512, HS], BF16).ap()
    GROUPS = [[0, 1], [2, 3], [4, 5], [6, 7]]
    wo16 = nc.dram_tensor("wo16", [D, D], BF16).ap()
    wgt16 = nc.dram_tensor("wgt16", [D, D], BF16).ap()
    t_w16 = {"wo": [T() for _ in range(4)], "wgt": [T() for _ in range(4)]}
    t_mix = T()

    def issue_cc(src, dst):
        pool.collective_compute("AllGather", ALU.bypass, replica_groups=GROUPS, ins=[src], outs=[dst]
                                ).then_inc(S.sems["cc"], 1)
        S.cnt["cc"] += 1
        t_mix.w = ("cc", S.cnt["cc"])
    EB = S_LEN // 4
    if mode == "B":
        mixin = din("mixin", [2048, HS], BF16)
    SKIP = (mode == "B")

    cst_f = nc.alloc_sbuf_tensor("cst_f", [128, 5, 128], F32)
    cst_b = nc.alloc_sbuf_tensor("cst_b", [128, 5, 128], BF16)
    cols_f = nc.alloc_sbuf_tensor("cols_f", [128, 64], F32)
    wa2_f = nc.alloc_sbuf_tensor("wa2_f", [16, 256], F32)
    ba_f = nc.alloc_sbuf_tensor("ba_f", [1, 256], F32)
    t_cst = T()
    S.dma("sp", cst_f[:], cst[:, :, :], writes=[t_cst])
    S.dma("sp", cols_f[:], cols[:, :], writes=[t_cst])
    S.dma("sp", wa2_f[:], wa2[:, :], writes=[t_cst])
    S.dma("sp", ba_f[:], ba[:, :], writes=[t_cst])
    S.op("dve", lambda: dve.tensor_copy(cst_b[:], cst_f[:]), reads=[t_cst], writes=[t_cst])
    ident_b = cst_b[:, 0, :]
    Uincl_f = cst_f[:, 1, :]
    Ustr_f = cst_f[:, 2, :]
    Ustr_b = cst_b[:, 2, :]
    Lincl_b = cst_b[:, 3, :]
    ones_b = cst_b[:, 4, :]
    ones_f = cst_f[:, 4, :]

    PS = nc.alloc_psum_tensor("ps", [128, 4096], F32)
    t_ps = [T(True) for _ in range(8)]

    def bank(i):
        return PS[:, i * 512:(i + 1) * 512]

    with nc.sbuf_tensor("hT", [128, NCH, S_LEN], BF16) as hT:
        NB0 = S_LEN // 256
        NLOOP0 = 0 if SKIP else NB0
        t_hT = [T() for _ in range(NB0)]
        t_hTp = [T() for _ in range(NB0)]

        def h_reads(t0, t1):
            rng = range(t0 // 256, (t1 + 255) // 256)
            return [t_hT[i] for i in rng] + [t_hTp[i] for i in rng]

        with ExitStack() as _es0:
            xb0 = _es0.enter_context(nc.sbuf_tensor("xb0", [128, NCH, 256], F32))
            xb1 = _es0.enter_context(nc.sbuf_tensor("xb1", [128, NCH, 256], F32))
            sq0 = _es0.enter_context(nc.sbuf_tensor("sq0", [128, NCH, 256], BF16))
            r0a = _es0.enter_context(nc.sbuf_tensor("r0", [128, 256], F32))
            r0b = _es0.enter_context(nc.sbuf_tensor("r1", [128, 256], F32))
            p0tmp = (_es0.enter_context(nc.sbuf_tensor("p0ta", [128, 256], F32)), _es0.enter_context(nc.sbuf_tensor("p0tb", [128, 256], F32)))
            t_p0tmp = (T(), T())
            xbs = (xb0, xb1)
            t_xb = (T(), T())
            t_sq = T()
            rs = (r0a, r0b)
            t_r = (T(), T())
            for nb in range(NLOOP0):
                xb = xbs[nb % 2]
                txb = t_xb[nb % 2]
                r = rs[nb % 2]
                tr = t_r[nb % 2]
                tsl = slice(nb * 256, (nb + 1) * 256)
                S.dma("sp", xb[:], xT[nb].rearrange("p (c t) -> p c t", c=NCH), writes=[txb])
                S.op("act", lambda xb=xb: act.activation(out=sq0[:], in_=xb[:], func=AF.Square),
                     reads=[txb], writes=[t_sq])
                bk = nb % 2
                for c in range(NCH):
                    S.op("pe", lambda c=c, bk=bk: mm(bank(bk)[:, 0:256], lhsT=ones_b, rhs=sq0[:, c, :],
                                                           start=(c == 0), stop=(c == NCH - 1)),
                         reads=[t_sq, t_cst], writes=[t_ps[bk]], inc=(c == NCH - 1))
                S.op("act", lambda r=r, bk=bk: act.activation(out=r[:], in_=bank(bk)[:, 0:256], func=AF.Ln,
                                                               scale=1.0 / D, bias=EPS),
                     reads=[t_ps[bk]], writes=[tr])
                S.op("act", lambda r=r: act.activation(out=r[:], in_=r[:], func=AF.Exp, scale=-0.5),
                     reads=[tr], writes=[tr])
                for c in range(NCH):
                    if c % 3 == 2:
                        k = (c // 3) % 2
                        S.op("act", lambda c=c, xb=xb, k=k: act.activation(
                            out=p0tmp[k][:], in_=xb[:, c, :], func=AF.Identity, scale=cols_f[:, c:c + 1]),
                            reads=[txb, t_cst], writes=[t_p0tmp[k]])
                        S.op("pool", lambda c=c, r=r, k=k: pool.tensor_tensor(
                            out=hT[:, c, tsl], in0=p0tmp[k][:], in1=r[:], op=ALU.mult),
                            reads=[t_p0tmp[k], tr], writes=[t_hTp[nb]])
                        continue
                    S.op("dve", lambda c=c, xb=xb, r=r: dve.scalar_tensor_tensor(
                        out=hT[:, c, tsl], in0=xb[:, c, :], scalar=cols_f[:, c:c + 1], in1=r[:],
                        op0=ALU.mult, op1=ALU.mult),
                        reads=[txb, tr, t_cst], writes=[t_hT[nb]])
        S.barrier()
        if STOP <= 0:
            return nc

        def silu_evac(src_ap, dst_ap, tmp_a, tmp_b, t_tmp, reads, writes):
            S.op("act", lambda: act.activation(out=tmp_a, in_=src_ap, func=AF.Exp, scale=-1.0),
                 reads=reads, writes=[t_tmp])
            S.op("dve", lambda: dve.tensor_scalar_add(out=tmp_a, in0=tmp_a, scalar1=1.0),
                 reads=[t_tmp], writes=[t_tmp])
            S.op("dve", lambda: dve.reciprocal(out=tmp_b, in_=tmp_a), reads=[t_tmp], writes=[t_tmp])
            S.op("dve", lambda: dve.tensor_tensor(out=dst_ap, in0=src_ap, in1=tmp_b, op=ALU.mult),
                 reads=list(reads) + [t_tmp], writes=writes)

        with ExitStack() as _es1:
            def A1(name, shape, dt):
                return _es1.enter_context(nc.sbuf_tensor(name, shape, dt))
            wg = A1("wg", [128, NCH, 768], BF16)
            wlr_b = A1("wlr_b", [128, NCH, 16], BF16)
            g_qTs = (A1("g_qTa", [128, 512], BF16), A1("g_qTb", [128, 512], BF16))
            g_kTs = (A1("g_kTa", [128, 512], BF16), A1("g_kTb", [128, 512], BF16))
            g_silus = (A1("g_silua", [128, 2, 512], BF16), A1("g_silub", [128, 2, 512], BF16))
            g_vs = (A1("g_va", [128, 4, 256], BF16), A1("g_vb", [128, 4, 256], BF16))
            g_lr1 = A1("g_lra", [16, 512], F32)
            g_lrs = (g_lr1, g_lr1)
            g_tmpa = A1("g_tmpa", [128, 512], F32)
            g_tmpb = A1("g_tmpb", [128, 512], F32)
            g_mask4 = A1("g_mask4", [128, 4, 128], F32)
            g_e = A1("g_e", [128, 512], F32)
            g_tmpc = g_e
            g_sp = A1("g_sp", [128, 4, 128], F32)
            g_Eq = A1("g_Eq", [128, 512], F32)
            g_Ek = A1("g_Ek", [128, 512], F32)
            g_qe = A1("g_qe", [128, 512], BF16)
            g_ke = A1("g_ke", [128, 512], BF16)
            g_klT = A1("g_klT", [128, 512], BF16)
            g_kl = A1("g_kl", [128, 4, 128], BF16)
            g_scm = A1("g_scm", [128, 512], BF16)
            g_Sf = A1("g_Sf", [128, 256], F32)
            g_Sb = A1("g_Sb", [128, 4, 256], BF16)
            g_sq = A1("g_sq", [128, 2, 512], BF16)
            g_r = A1("g_r", [128, 512], F32)
            g_y = A1("g_y", [128, 2, 512], BF16)
            t_wg, t_wlr, t_mask4 = T(), T(), T()
            t_qT, t_kT, t_silu, t_v = ((T(), T()) for _ in range(4))
            t_lr = (T(),) * 2
            t_tmp = T()
            t_e, t_sp, t_Eq, t_Ek, t_qe, t_ke, t_klT, t_kl, t_scm = (T() for _ in range(9))
            t_Sf, t_sq2, t_r2, t_y = T(), T(), T(), T()
            t_tmpc = t_e
            t_Sb = [T() for _ in range(4)]
            S.dma("pool", wlr_b[:], wlr.rearrange("(c p) n -> p c n", p=128), writes=[t_wlr])
            for cc in range(4):
                S.op("pool", lambda cc=cc: pool.tensor_copy(g_mask4[:, cc, :], Uincl_f), reads=[t_cst], writes=[t_mask4])
            gblocks = [] if SKIP else [(g, nb) for g in range(2) for nb in range(S_LEN // 512)]
            lr_d = nc.dram_tensor("lr_d", [S_LEN // 512, 16, 512], F32).ap()
            t_lrd = [T() for _ in range(S_LEN // 512)]
            rot = [0]

            def nextbank():
                rot[0] ^= 1
                return rot[0]

            def gla_inproj_gen(bi):
                g, nb = gblocks[bi]
                par = bi % 2
                t0 = nb * 512
                tok = slice(t0, t0 + 512)
                hr = h_reads(t0, t0 + 512)
                if nb == 0:
                    S.dma("pool", wg[:], wgla[g].rearrange("(c p) n -> p c n", p=128), writes=[t_wg])
                pending = [None]

                def flush():
                    if pending[0] is not None:
                        pending[0]()
                        pending[0] = None

                def group(lhs_fn, rhs_fn, out_fn, evac, extra_reads):
                    bk = nextbank()
                    for c in range(NCH):
                        S.op("pe", lambda c=c, bk=bk: mm(out_fn(bk), lhsT=lhs_fn(c), rhs=rhs_fn(c),
                                                               start=(c == 0), stop=(c == NCH - 1)),
                             reads=hr + extra_reads, writes=[t_ps[bk]], inc=(c == NCH - 1))
                        if c == 3:
                            flush()
                    pending[0] = lambda bk=bk: evac(bk)

                group(lambda c: wg[:, c, 0:128], lambda c: hT[:, c, tok], lambda bk: bank(bk),
                      lambda bk: S.op("act", lambda: act.activation(out=g_qTs[par][:], in_=bank(bk), func=AF.Identity,
                                                                    scale=128 ** -0.5),
                                      reads=[t_ps[bk]], writes=[t_qT[par]]), [t_wg])
                yield
                group(lambda c: wg[:, c, 128:256], lambda c: hT[:, c, tok], lambda bk: bank(bk),
                      lambda bk: S.op("dve", lambda: dve.tensor_copy(g_kTs[par][:], bank(bk)),
                                      reads=[t_ps[bk]], writes=[t_kT[par]]), [t_wg])
                yield
                for ec in range(2):
                    group(lambda c, ec=ec: wg[:, c, 512 + ec * 128:512 + ec * 128 + 128], lambda c: hT[:, c, tok],
                          lambda bk: bank(bk),
                          lambda bk, ec=ec: silu_evac(bank(bk), g_silus[par][:, ec, :], g_tmpa[:], g_tmpb[:], t_tmp,
                                                      [t_ps[bk]], [t_silu[par]]), [t_wg])
                    yield
                for sb in range(4):
                    group(lambda c, sb=sb: hT[:, c, t0 + sb * 128:t0 + sb * 128 + 128], lambda c: wg[:, c, 256:512],
                          lambda bk: bank(bk)[:, 0:256],
                          lambda bk, sb=sb: S.op("dve", lambda: dve.tensor_copy(g_vs[par][:, sb, :], bank(bk)[:, 0:256]),
                                                 reads=[t_ps[bk]], writes=[t_v[par]]), [t_wg])
                    yield
                if g == 0:
                    group(lambda c: wlr_b[:, c, :], lambda c: hT[:, c, tok], lambda bk: bank(bk)[0:16, :],
                          lambda bk: S.op("dve", lambda: dve.tensor_copy(g_lrs[par][:], bank(bk)[0:16, :]),
                                          reads=[t_ps[bk]], writes=[t_lr[par]]), [t_wlr])
                    flush()
                    S.dma("sp", lr_d[nb], g_lrs[par][:], reads=[t_lr[par]], writes=[t_lrd[nb]])
                else:
                    flush()
                    S.dma("sp", g_lrs[par][:], lr_d[nb], reads=[t_lrd[nb]], writes=[t_lr[par]])
                yield

            def gpump(gen, n):
                if gen is None:
                    return
                for _ in range(n):
                    try:
                        next(gen)
                    except StopIteration:
                        return

            if gblocks:
                gpump(gla_inproj_gen(0), 100)
            for bi, (g, nb) in enumerate(gblocks):
                par = bi % 2
                g_qT, g_kT, g_silu, g_v, g_lr = g_qTs[par], g_kTs[par], g_silus[par], g_vs[par], g_lrs[par]
                tqT, tkT, tsilu, tv, tlr = t_qT[par], t_kT[par], t_silu[par], t_v[par], t_lr[par]
                nxt = gla_inproj_gen(bi + 1) if bi + 1 < len(gblocks) else None
                t0 = nb * 512
                gsl = slice(g * 128, g * 128 + 128)
                if nb == 0:
                    S.op("dve", lambda: dve.memset(g_Sf[:], 0.0), writes=[t_Sf])
                    S.op("dve", lambda: dve.memset(g_Sb[:, 0, :], 0.0), writes=[t_Sb[0]])
                for cc in range(4):
                    cs = slice(cc * 128, cc * 128 + 128)
                    S.op("pe", lambda cs=cs, cc=cc: mm(
                        bank(2)[:, cs], lhsT=g_lr[0:16, cs], rhs=wa2_f[0:16, gsl], start=(cc == 0), stop=False),
                        reads=[tlr, t_cst], writes=[t_ps[2]], inc=False)
                    S.op("pe", lambda cs=cs, cc=cc: mm(
                        bank(2)[:, cs], lhsT=ones_f[0:1, :], rhs=ba_f[0:1, gsl], start=False, stop=True),
                        reads=[t_cst], writes=[t_ps[2]], inc=(cc == 3))
                gpump(nxt, 1)
                S.op("act", lambda: act.activation(out=g_e[:], in_=bank(2), func=AF.Exp, scale=-1.0),
                     reads=[t_ps[2]], writes=[t_e])
                S.op("act", lambda: act.activation(out=g_sp[:].rearrange("p a b -> p (a b)"), in_=g_e[:], func=AF.Ln, bias=1.0),
                     reads=[t_e], writes=[t_sp])
                for cc in range(4):
                    cs = slice(cc * 128, cc * 128 + 128)
                    S.op("pe", lambda cs=cs, cc=cc: mm(bank(3)[:, cs], lhsT=g_sp[:, cc, :], rhs=Uincl_f,
                                                              start=(cc == 0), stop=True),
                         reads=[t_sp, t_cst], writes=[t_ps[3]], inc=(cc == 3))
                gpump(nxt, 1)
                S.op("act", lambda: act.activation(out=g_Eq[:], in_=bank(3), func=AF.Exp, scale=-1.0 / 16),
                     reads=[t_ps[3]], writes=[t_Eq])
                S.op("act", lambda: act.activation(out=g_Ek[:], in_=bank(3), func=AF.Exp, scale=1.0 / 16),
                     reads=[t_ps[3]], writes=[t_Ek])
                S.op("dve", lambda: dve.tensor_tensor(out=g_ke[:], in0=g_kT[:], in1=g_Ek[:], op=ALU.mult),
                     reads=[tkT, t_Ek], writes=[t_ke])
                for cc in range(4):
                    cs = slice(cc * 128, cc * 128 + 128)
                    S.op("dve", lambda cs=cs, cc=cc: dve.scalar_tensor_tensor(
                        out=g_klT[:, cs], in0=g_kT[:, cs], scalar=g_Eq[:, cc * 128 + 127:cc * 128 + 128], in1=g_Ek[:, cs],
                        op0=ALU.mult, op1=ALU.mult),
                        reads=[tkT, t_Eq, t_Ek], writes=[t_klT])
                S.op("dve", lambda: dve.tensor_tensor(out=g_qe[:], in0=g_qT[:], in1=g_Eq[:], op=ALU.mult),
                     reads=[tqT, t_Eq], writes=[t_qe])
                for cc in range(4):
                    cs = slice(cc * 128, cc * 128 + 128)
                    S.op("pe", lambda cs=cs, cc=cc: mm(bank(4)[:, cs], lhsT=g_klT[:, cs], rhs=ident_b,
                                                              start=(cc == 0), stop=True),
                         reads=[t_klT, t_cst], writes=[t_ps[4]], inc=(cc == 3))
                for cc in range(4):
                    cs = slice(cc * 128, cc * 128 + 128)
                    S.op("pe", lambda cs=cs, cc=cc: mm(bank(5)[:, cs], lhsT=g_ke[:, cs], rhs=g_qe[:, cs],
                                                              start=(cc == 0), stop=True),
                         reads=[t_ke, t_qe], writes=[t_ps[5]], inc=(cc == 3))
                gpump(nxt, 1)
                S.op("act", lambda: act.activation(out=g_kl[:].rearrange("p a b -> p (a b)"), in_=bank(4), func=AF.Copy),
                     reads=[t_ps[4]], writes=[t_kl])
                S.op("dve", lambda: dve.tensor_tensor(out=g_scm[:], in0=bank(5), in1=g_mask4[:].rearrange("p a b -> p (a b)"),
                                                      op=ALU.mult),
                     reads=[t_ps[5], t_mask4], writes=[t_scm])
                for cc in range(4):
                    S.op("pe", lambda cc=cc: mm(
                        bank(2 + cc // 2)[:, (cc % 2) * 256:(cc % 2) * 256 + 256], lhsT=g_kl[:, cc, :], rhs=g_v[:, cc, :],
                        start=(cc % 2 == 0), stop=True),
                        reads=[t_kl, tv], writes=[t_ps[2 + cc // 2]])
                gpump(nxt, 1)
                for cc in range(4):
                    cs = slice(cc * 128, cc * 128 + 128)
                    for ec in range(2):
                        es = slice(ec * 128, ec * 128 + 128)
                        S.op("pe", lambda ec=ec, es=es, cs=cs, cc=cc: mm(
                            bank(6 + ec)[:, cs], lhsT=g_Sb[:, cc, es], rhs=g_qe[:, cs], start=(cc == 0), stop=False),
                            reads=[t_Sb[cc], t_qe], writes=[t_ps[6 + ec]], inc=False)
                        S.op("pe", lambda ec=ec, es=es, cs=cs, cc=cc: mm(
                            bank(6 + ec)[:, cs], lhsT=g_v[:, cc, es], rhs=g_scm[:, cs], start=False, stop=True),
                            reads=[tv, t_scm], writes=[t_ps[6 + ec]])
                    S.op("dve", lambda cc=cc: dve.scalar_tensor_tensor(
                        out=g_Sf[:], in0=g_Sf[:], scalar=g_Eq[:, cc * 128 + 127:cc * 128 + 128],
                        in1=bank(2 + cc // 2)[:, (cc % 2) * 256:(cc % 2) * 256 + 256], op0=ALU.mult, op1=ALU.add),
                        reads=[t_Sf, t_Eq, t_ps[2 + cc // 2]], writes=[t_Sf])
                    S.op("pool", lambda cc=cc: pool.tensor_copy(g_Sb[:, (cc + 1) % 4, :], g_Sf[:]),
                         reads=[t_Sf], writes=[t_Sb[(cc + 1) % 4]])
                    gpump(nxt, 1)
                gpump(nxt, 100)
                for ec in range(2):
                    S.op("act", lambda ec=ec: act.activation(out=g_sq[:, ec, :], in_=bank(6 + ec), func=AF.Square),
                         reads=[t_ps[6 + ec]], writes=[t_sq2])
                for ec in range(2):
                    S.op("pe", lambda ec=ec: mm(bank(5), lhsT=ones_b, rhs=g_sq[:, ec, :], start=(ec == 0), stop=(ec == 1)),
                         reads=[t_sq2, t_cst], writes=[t_ps[5]], inc=(ec == 1))
                S.op("act", lambda: act.activation(out=g_r[:], in_=bank(5), func=AF.Ln, scale=1.0 / 256, bias=EPS),
                     reads=[t_ps[5]], writes=[t_r2])
                S.op("act", lambda: act.activation(out=g_r[:], in_=g_r[:], func=AF.Exp, scale=-0.5),
                     reads=[t_r2], writes=[t_r2])
                for ec in range(2):
                    S.op("dve", lambda ec=ec: dve.scalar_tensor_tensor(
                        out=g_tmpc[:], in0=bank(6 + ec), scalar=cols_f[:, 48 + ec:49 + ec], in1=g_r[:],
                        op0=ALU.mult, op1=ALU.mult),
                        reads=[t_ps[6 + ec], t_r2, t_cst], writes=[t_tmpc])
                    S.op("dve", lambda ec=ec: dve.tensor_tensor(out=g_y[:, ec, :], in0=g_tmpc[:], in1=g_silu[:, ec, :], op=ALU.mult),
                         reads=[t_tmpc, tsilu], writes=[t_y])
                S.dma("sp", mg_own[t0 // EB, g * 256:(g + 1) * 256, t0 % EB:t0 % EB + 512].rearrange("(e p) t -> p e t", p=128), g_y[:],
                      reads=[t_y])
        S.barrier()

        NKB = S_LEN // 128
        with ExitStack() as _es2:
            def A2(name, shape, dt):
                return _es2.enter_context(nc.sbuf_tensor(name, shape, dt))
            ws = A2("ws", [128, NCH, 512], BF16)
            s_kT = A2("s_kT", [128, S_LEN], BF16)
            s_v = A2("s_v", [128, NKB, 128], BF16)
            s_qTs = (A2("s_qTa", [128, TQ], BF16), A2("s_qTb", [128, TQ], BF16))
            s_gss = (A2("s_gsa", [128, TQ], BF16), A2("s_gsb", [128, TQ], BF16))
            s_ta = A2("s_ta", [128, 512], F32)
            s_tb = A2("s_tb", [128, 512], F32)
            e1s = (A2("s_e1a", [128, TQ], F32), A2("s_e1b", [128, TQ], F32), A2("s_e1c", [128, TQ], F32))
            sps = (A2("s_spa", [128, TQ], BF16), A2("s_spb", [128, TQ], BF16))
            gs_ = (A2("s_ga", [128, TQ], BF16), A2("s_gb", [128, TQ], BF16))
            ws_ = (A2("s_wa", [128, TQ], BF16), A2("s_wb", [128, TQ], BF16))
            s_y = A2("s_y", [128, TQ], BF16)
            t_ws, t_sy, t_stmp, t_msown = T(), T(), T(), T()
            t_skT = [T() for _ in range(S_LEN // 512)]
            t_sv = [T() for _ in range(S_LEN // 512)]
            t_sqT, t_sgs = (T(), T()), (T(), T())
            t_e1, t_sps, t_gs_, t_ws_ = (T(), T(), T()), (T(), T()), (T(), T()), (T(), T())
            ZB, BB, OB = 0, 2, 4
            blocks = [] if SKIP else [(hd, tb) for hd in range(4) for tb in range(S_LEN // TQ)]

            def inproj_gen(bi):
                hd, tb = blocks[bi]
                s_qT, s_gs = s_qTs[bi % 2], s_gss[bi % 2]
                tq_, tg_ = t_sqT[bi % 2], t_sgs[bi % 2]
                q0 = tb * TQ
                if tb == 0:
                    S.dma("pool", ws[:], wsb[hd].rearrange("(c p) n -> p c n", p=128), writes=[t_ws])
                pending = [None]

                def flush():
                    if pending[0] is not None:
                        pending[0]()
                        pending[0] = None

                for half in range(NBK):
                    t0 = q0 + half * 512
                    tok = slice(t0, t0 + 512)
                    loc = slice(half * 512, half * 512 + 512)
                    hr = h_reads(t0, t0 + 512)
                    for (col0, bk_, evac) in (
                        (0, 6, lambda loc=loc: S.op("dve", lambda: dve.tensor_scalar_mul(
                            out=s_qT[:, loc], in0=bank(6), scalar1=128 ** -0.5), reads=[t_ps[6]], writes=[tq_])),
                        (128, 7, lambda tok=tok, t0=t0: S.op("dve", lambda: dve.tensor_copy(s_kT[:, tok], bank(7)),
                                                             reads=[t_ps[7]], writes=[t_skT[t0 // 512]])),
                        (384, 6, lambda loc=loc: silu_evac(bank(6), s_gs[:, loc], s_ta[:], s_tb[:], t_stmp,
                                                           [t_ps[6]], [tg_])),
                    ):
                        for c in range(NCH):
                            S.op("pe", lambda c=c, col0=col0, bk_=bk_: mm(
                                bank(bk_), lhsT=ws[:, c, col0:col0 + 128], rhs=hT[:, c, tok],
                                start=(c == 0), stop=(c == NCH - 1)),
                                reads=hr + [t_ws], writes=[t_ps[bk_]], inc=(c == NCH - 1))
                            if c % 4 == 3:
                                if c == 3:
                                    flush()
                                yield
                        pending[0] = evac
                    for sb in range(4):
                        kb = (t0 // 128) + sb
                        for c in range(NCH):
                            S.op("pe", lambda c=c, kb=kb, sb=sb: mm(
                                bank(7)[:, sb * 128:sb * 128 + 128], lhsT=hT[:, c, kb * 128:kb * 128 + 128],
                                rhs=ws[:, c, 256:384], start=(c == 0 and sb == 0), stop=(c == NCH - 1)),
                                reads=hr + [t_ws], writes=[t_ps[7]], inc=(c == NCH - 1 and sb == 3))
                            if c % 4 == 3:
                                if c == 3 and sb == 0:
                                    flush()
                                yield
                    pending[0] = (lambda t0=t0: S.op("dve", lambda: dve.tensor_copy(
                        s_v[:, t0 // 128:t0 // 128 + 4, :], bank(7).rearrange("p (a b) -> p a b", a=4)),
                        reads=[t_ps[7]], writes=[t_sv[t0 // 512]]))
                flush()
                yield

            NYIELD = NBK * 28 + 1

            def pump(gen, n):
                if gen is None:
                    return
                for _ in range(n):
                    try:
                        next(gen)
                    except StopIteration:
                        return

            if blocks:
                pump(inproj_gen(0), 10 ** 6)
                for k in range(4):
                    issue_cc(mg_own[k], mg_all[k])
                for nm, src, dst in (("wo", wo, wo16), ("wgt", wgt, wgt16)):
                    for j in range(4):
                        S.dma("pool", dst[j * 512:(j + 1) * 512, :], src[j * 512:(j + 1) * 512, :],
                              writes=[t_w16[nm][j]])
            for bi, (hd, tb) in enumerate(blocks):
                    q0 = tb * TQ
                    s_qT, s_gs = s_qTs[bi % 2], s_gss[bi % 2]
                    tq_, tg_ = t_sqT[bi % 2], t_sgs[bi % 2]
                    nxt = inproj_gen(bi + 1) if bi + 1 < len(blocks) else None
                    kbs = list(range((tb + 1) * (TQ // 128) - 1, -1, -1))
                    P = len(kbs)
                    npump = (NYIELD + P - 1) // P
                    startedB = [False] * NBK
                    startedO = [False] * NBK

                    def geom(p):
                        kb = kbs[p]
                        off = kb * 128 - q0
                        lo = max(0, off)
                        segs = []
                        for bki in range(NBK):
                            c0 = max(lo, bki * 512)
                            c1 = (bki + 1) * 512
                            if c0 < c1:
                                segs.append((bki, c0, c1))
                        return kb, off, lo, segs

                    def emit_Z(p):
                        kb, off, lo, segs = geom(p)
                        for (bki, c0, c1) in segs:
                            S.op("pe", lambda bki=bki, c0=c0, c1=c1, kb=kb: mm(
                                bank(ZB + bki)[:, c0 - bki * 512:c1 - bki * 512],
                                lhsT=s_kT[:, kb * 128:kb * 128 + 128], rhs=s_qT[:, c0:c1], start=True, stop=True),
                                reads=[t_skT[kb // 4], tq_], writes=[t_ps[ZB + bki]])

                    def zb_reads(segs, base):
                        return [t_ps[base + bki] for (bki, _, _) in segs]

                    def emit_E1(p):
                        kb, off, lo, segs = geom(p)
                        e1, te1 = e1s[p % 3], t_e1[p % 3]
                        S.op("act", lambda: act.activation(
                            out=e1[:, lo:TQ], in_=PS[:, ZB * 512 + lo:ZB * 512 + TQ], func=AF.Exp),
                            reads=zb_reads(segs, ZB), writes=[te1])
                        if off >= 0:
                            S.op("dve", lambda: dve.tensor_tensor(
                                out=e1[:, off:off + 128], in0=e1[:, off:off + 128], in1=Ustr_f, op=ALU.mult),
                                reads=[te1, t_cst], writes=[te1])

                    emit_Z(0)
                    emit_E1(0)
                    if P > 1:
                        emit_Z(1)
                    pump(nxt, 4)
                    for p in range(P + 1):
                        if p >= 1:
                            kbp, offp, lop, segsp = geom(p - 1)
                            e1p, te1p = e1s[(p - 1) % 3], t_e1[(p - 1) % 3]
                            spp, tspp = sps[(p - 1) % 2], t_sps[(p - 1) % 2]
                            gp, tgp = gs_[(p - 1) % 2], t_gs_[(p - 1) % 2]
                            wp_, twp = ws_[(p - 1) % 2], t_ws_[(p - 1) % 2]
                            S.op("act", lambda gp=gp, lop=lop: act.activation(
                                out=gp[:, lop:TQ], in_=PS[:, BB * 512 + lop:BB * 512 + TQ], func=AF.Exp, scale=-1.0),
                                reads=zb_reads(segsp, BB), writes=[tgp])
                            if p - 1 < P - 1:
                                for (bki, c0, c1) in segsp:
                                    S.op("pe", lambda bki=bki, c0=c0, c1=c1, spp=spp: mm(
                                        bank(BB + bki)[:, c0 - bki * 512:c1 - bki * 512], lhsT=Ustr_b, rhs=spp[:, c0:c1],
                                        start=False, stop=True),
                                        reads=[tspp, t_cst], writes=[t_ps[BB + bki]])
                        if p < P:
                            kb, off, lo, segs = geom(p)
                            e1, te1 = e1s[p % 3], t_e1[p % 3]
                            sp_, tsp = sps[p % 2], t_sps[p % 2]
                            S.op("act", lambda e1=e1, sp_=sp_, lo=lo: act.activation(
                                out=sp_[:, lo:TQ], in_=e1[:, lo:TQ], func=AF.Ln, bias=1.0),
                                reads=[te1], writes=[tsp])
                            for (bki, c0, c1) in segs:
                                S.op("pe", lambda bki=bki, c0=c0, c1=c1, sp_=sp_, st=(not startedB[bki]): mm(
                                    bank(BB + bki)[:, c0 - bki * 512:c1 - bki * 512], lhsT=Lincl_b, rhs=sp_[:, c0:c1],
                                    start=st, stop=True),
                                    reads=[tsp, t_cst], writes=[t_ps[BB + bki]])
                                startedB[bki] = True
                        if p + 1 < P:
                            emit_E1(p + 1)
                        pump(nxt, npump // 2)
                        if p + 2 < P:
                            emit_Z(p + 2)
                        if p >= 1:
                            S.op("dve", lambda wp_=wp_, e1p=e1p, gp=gp, lop=lop: dve.tensor_tensor(
                                out=wp_[:, lop:TQ], in0=e1p[:, lop:TQ], in1=gp[:, lop:TQ], op=ALU.mult),
                                reads=[te1p, tgp], writes=[twp])
                            for (bki, c0, c1) in segsp:
                                S.op("pe", lambda bki=bki, c0=c0, c1=c1, wp_=wp_, kbp=kbp, st=(not startedO[bki]): mm(
                                    bank(OB + bki)[:, c0 - bki * 512:c1 - bki * 512], lhsT=s_v[:, kbp, :], rhs=wp_[:, c0:c1],
                                    start=st, stop=True),
                                    reads=[t_sv[kbp // 4], twp], writes=[t_ps[OB + bki]])
                                startedO[bki] = True
                        pump(nxt, npump - npump // 2)
                    pump(nxt, 10 ** 6)
                    S.op("dve", lambda: dve.tensor_tensor(out=s_y[:], in0=PS[:, OB * 512:OB * 512 + TQ], in1=s_gs[:], op=ALU.mult),
                         reads=[t_ps[OB + i] for i in range(NBK)] + [tg_], writes=[t_sy])
                    S.dma("sp", ms_own[hd, q0 // HS, :, q0 % HS:q0 % HS + TQ], s_y[:], reads=[t_sy], writes=[t_msown])
                    if tb == S_LEN // TQ - 1 and hd < 3:
                        S._wait("pool", S._deps("pool", [t_msown], []))
                        issue_cc(ms_own[hd].rearrange("hh p t -> (hh p) t"), ms_all[hd])
        S.barrier(skip=("cc",))

    BT = 256
    if mode == "AB":
        par = nc.sync.partition_id() % 2
        mg_v = mg_all.rearrange("k (c p) t -> k p c t", p=128)
        ms_v = ms_all.rearrange("h q t -> (h q) t").rearrange("(hr hh p) t -> hh p hr t", hr=8, hh=2)
    else:
        mixin_v = mixin.rearrange("(c p) t -> p c t", p=128)
    with ExitStack() as _es3:
        wo_b = _es3.enter_context(nc.sbuf_tensor("wo_b", [128, NCH, D], BF16))
        wgt_b = _es3.enter_context(nc.sbuf_tensor("wgt_b", [128, NCH, D], BF16))
        wp_b = _es3.enter_context(nc.sbuf_tensor("wp_b", [128, 2, D], BF16))
        mxa = _es3.enter_context(nc.sbuf_tensor("mxa", [128, NCH, BT], BF16))
        xra = _es3.enter_context(nc.sbuf_tensor("xra", [128, NCH, BT], F32))
        pbts = (_es3.enter_context(nc.sbuf_tensor("pbt", [128, 2, BT], BF16)), _es3.enter_context(nc.sbuf_tensor("pbt2", [128, 2, BT], BF16)))
        m1 = _es3.enter_context(nc.sbuf_tensor("m1", [128, NCH, BT], F32))
        sq3 = _es3.enter_context(nc.sbuf_tensor("sq3", [128, NCH, BT], BF16))
        r3 = _es3.enter_context(nc.sbuf_tensor("r3", [128, BT], F32))
        gte = _es3.enter_context(nc.sbuf_tensor("gte", [128, 2, BT], F32))
        oo = _es3.enter_context(nc.sbuf_tensor("oo", [128, 4, BT], F32))
        t_wo = [T() for _ in range(4)]
        t_wgt = [T() for _ in range(4)]
        t_wp = T()
        for j in range(4):
            S.dma("sp", wo_b[:, 4 * j:4 * j + 4, :],
                  wo16[j * 512:(j + 1) * 512, :].rearrange("(c p) n -> p c n", p=128),
                  reads=[t_w16["wo"][j]], writes=[t_wo[j]])
        for j in range(4):
            S.dma("sp", wgt_b[:, 4 * j:4 * j + 4, :],
                  wgt16[j * 512:(j + 1) * 512, :].rearrange("(c p) n -> p c n", p=128),
                  reads=[t_w16["wgt"][j]], writes=[t_wgt[j]])
        S.dma("pool", wp_b[:], wp.rearrange("(c p) n -> p c n", p=128), writes=[t_wp])
        if not SKIP:
            issue_cc(ms_own[3].rearrange("hh p t -> (hh p) t"), ms_all[3])
        mxs, t_mx = (mxa, mxa), (T(),) * 2
        xrs, t_xr = (xra, xra), (T(),) * 2
        h1b = sq3
        t_pb, t_m1, t_sq3, t_r3 = T(), T(), T(), T()
        t_h1b = t_sq3
        t_gte = (T(), T())
        t_oo = [T() for _ in range(4)]
        pTo_v = pTo.rearrange("(c p) t -> p c t", p=128)
        outT_v = outT.rearrange("(c p) t -> p c t", p=128)
        rot3 = [0]

        def nb3():
            rot3[0] = (rot3[0] + 1) % 8
            return rot3[0]

        t_m1c = [T() for _ in range(NCH)]
        t_mxp = [T() for _ in range(5)]
        t_sqc = [T() for _ in range(NCH)]
        t_pbs = (T(), T())
        NB3 = HS // BT

        def emit_loads(nb):
            tsl = slice(nb * BT, (nb + 1) * BT)
            if mode == "AB":
                jj = (nb * BT) // EB
                cc0 = (nb * BT) % EB
                S.dma("sp", mxa[:, 0:8, :], mg_v[bass.ds(par * 2 + jj, 1), :, :, cc0:cc0 + BT].rearrange("o p c t -> p (o c) t"),
                      reads=[t_mix], writes=[t_mxp[0]])
                S.dma("sp", mxa[:, 8:16, :],
                      ms_v[bass.ds(par, 1), :, :, nb * BT:(nb + 1) * BT].rearrange("o p r t -> p (o r) t"),
                      reads=[t_mix], writes=[t_mxp[1]])
            else:
                S.dma("sp", mxa[:], mixin_v[:, :, tsl], writes=[t_mx[0]])
            S.dma("sp", xra[:], xTo[nb].rearrange("p (c t) -> p c t", c=NCH), writes=[t_xr[0]])
            S.dma("pool", pbts[nb % 2][:], pTo_v[:, :, tsl], writes=[t_pbs[nb % 2]])

        for nb in range(NB3):
            emit_loads(nb)
            tsl = slice(nb * BT, (nb + 1) * BT)
            mx, tmx = mxa, t_mx[0]
            xr, txr = xra, t_xr[0]
            pbt, t_pb = pbts[nb % 2], t_pbs[nb % 2]
            for oc in range(NCH):
                bk = nb3()
                for c in range(NCH):
                    S.op("pe", lambda c=c, oc=oc, bk=bk: mm(
                        bank(bk)[:, 0:BT], lhsT=wo_b[:, c, oc * 128:(oc + 1) * 128], rhs=mx[:, c, :],
                        start=(c == 0), stop=(c == NCH - 1)),
                        reads=[t_wo[c // 4], tmx] + t_mxp, writes=[t_ps[bk]], inc=(c == NCH - 1))
                S.op("act", lambda oc=oc, bk=bk: act.activation(out=sq3[:, oc, :], in_=bank(bk)[:, 0:BT], func=AF.Square),
                     reads=[t_ps[bk]], writes=[t_sqc[oc]])
                S.op("dve", lambda oc=oc, bk=bk: dve.tensor_scalar_mul(
                    out=m1[:, oc, :], in0=bank(bk)[:, 0:BT], scalar1=cols_f[:, 16 + oc:17 + oc]),
                    reads=[t_ps[bk], t_cst], writes=[t_m1c[oc]])
            bk = nb3()
            for c in range(NCH):
                S.op("pe", lambda c=c, bk=bk: mm(bank(bk)[:, 0:BT], lhsT=ones_b, rhs=sq3[:, c, :],
                                                       start=(c == 0), stop=(c == NCH - 1)),
                     reads=[t_sqc[c], t_cst], writes=[t_ps[bk]], inc=(c == NCH - 1))
            S.op("act", lambda bk=bk: act.activation(out=r3[:], in_=bank(bk)[:, 0:BT], func=AF.Ln, scale=1.0 / D, bias=EPS),
                 reads=[t_ps[bk]], writes=[t_r3])
            S.op("act", lambda: act.activation(out=r3[:], in_=r3[:], func=AF.Exp, scale=-0.5),
                 reads=[t_r3], writes=[t_r3])
            for oc in range(NCH):
                S.op("dve", lambda oc=oc: dve.tensor_tensor(out=m1[:, oc, :], in0=m1[:, oc, :], in1=r3[:], op=ALU.mult),
                     reads=[t_m1c[oc], t_r3], writes=[t_m1c[oc]])
                S.op("pool", lambda oc=oc: pool.tensor_tensor(out=m1[:, oc, :], in0=m1[:, oc, :], in1=xr[:, oc, :], op=ALU.add),
                     reads=[t_m1c[oc], txr], writes=[t_m1c[oc]])
                S.op("act", lambda oc=oc: act.activation(out=h1b[:, oc, :], in_=m1[:, oc, :], func=AF.Copy),
                     reads=[t_m1c[oc]], writes=[t_sqc[oc]])
            for oc in range(NCH):
                bk = nb3()
                for c in range(NCH):
                    S.op("pe", lambda c=c, oc=oc, bk=bk: mm(
                        bank(bk)[:, 0:BT], lhsT=wgt_b[:, c, oc * 128:(oc + 1) * 128], rhs=h1b[:, c, :],
                        start=(c == 0), stop=(c == NCH - 1)),
                        reads=[t_wgt[c // 4], t_sqc[c]], writes=[t_ps[bk]], inc=(c == NCH - 1))
                gt, tgt = gte[:, oc % 2, :], t_gte[oc % 2]
                S.op("act", lambda oc=oc, bk=bk, gt=gt: act.activation(
                    out=gt, in_=bank(bk)[:, 0:BT], func=AF.Sigmoid, bias=cols_f[:, 32 + oc:33 + oc]),
                    reads=[t_ps[bk], t_cst], writes=[tgt])
                bk2 = nb3()
                for c in range(2):
                    S.op("pe", lambda c=c, oc=oc, bk2=bk2: mm(
                        bank(bk2)[:, 0:BT], lhsT=wp_b[:, c, oc * 128:(oc + 1) * 128], rhs=pbt[:, c, :],
                        start=(c == 0), stop=(c == 1)),
                        reads=[t_wp, t_pb], writes=[t_ps[bk2]], inc=(c == 1))
                S.op("dve", lambda oc=oc, bk2=bk2, gt=gt: dve.tensor_tensor(
                    out=gt, in0=gt, in1=bank(bk2)[:, 0:BT], op=ALU.mult),
                    reads=[tgt, t_ps[bk2]], writes=[tgt])
                S.op("dve", lambda oc=oc, gt=gt: dve.tensor_tensor(
                    out=oo[:, oc % 4, :], in0=gt, in1=m1[:, oc, :], op=ALU.add),
                    reads=[tgt, t_m1c[oc]], writes=[t_oo[oc % 4]])
                if oc % 4 == 3:
                    S.dma("pool", outT_v[:, oc - 3:oc + 1, tsl], oo[:], reads=t_oo)
        S.barrier()
    return nc


_PROG = {}


def kernel(x, p, g_pre, w_in, w_a2, b_a, g_gla_head, w_out, g_post, w_ple_gate, b_ple_gate, w_ple_proj):
    x = np.asarray(x, np.float32)
    B, S_LEN, _ = x.shape
    HS = S_LEN // 2
    f = lambda a: np.ascontiguousarray(np.asarray(a, np.float32))
    p, g_pre, w_in, w_a2, b_a = f(p), f(g_pre), f(w_in), f(w_a2), f(b_a)
    g_gla_head, w_out, g_post = f(g_gla_head), f(w_out), f(g_post)
    w_ple_gate, b_ple_gate, w_ple_proj = f(w_ple_gate), f(b_ple_gate), f(w_ple_proj)
    W = w_in[0]
    GQ, GK, GV, GG, LR, SQ, SK, SV, SG = 0, 512, 1024, 2048, 3072, 3088, 4112, 5136, 6160

    def colv(v):
        return v.reshape(-1, 128).T

    ii = np.arange(128)
    cst = np.zeros((128, 5, 128), np.float32)
    cst[:, 0, :] = (ii[:, None] == ii[None, :])
    cst[:, 1, :] = (ii[:, None] <= ii[None, :])
    cst[:, 2, :] = (ii[:, None] < ii[None, :])
    cst[:, 3, :] = (ii[:, None] >= ii[None, :])
    cst[:, 4, :] = 1.0
    cols = np.zeros((128, 64), np.float32)
    cols[:, 0:16] = colv(g_pre[0])
    cols[:, 16:32] = colv(g_post[0])
    cols[:, 32:48] = colv(b_ple_gate[0])
    cols[:, 48:50] = colv(g_gla_head[0])
    in_maps = []
    for core in range(8):
        b, hh = core // 2, core % 2
        wsb = np.stack([np.concatenate([W[:, SQ + 128 * H:SQ + 128 * H + 128], W[:, SK + 128 * H:SK + 128 * H + 128],
                                        W[:, SV + 128 * H:SV + 128 * H + 128], W[:, SG + 128 * H:SG + 128 * H + 128]], axis=1)
                        for H in range(4 * hh, 4 * hh + 4)])
        wgla = np.stack([np.concatenate([W[:, GQ + 128 * G:GQ + 128 * G + 128], W[:, GK + 128 * G:GK + 128 * G + 128],
                                         W[:, GV + 256 * G:GV + 256 * G + 256], W[:, GG + 256 * G:GG + 256 * G + 256]], axis=1)
                         for G in range(2 * hh, 2 * hh + 2)])
        xtile = np.ascontiguousarray(x[b].reshape(S_LEN // 256, 256, NCH, 128).transpose(0, 3, 2, 1)).reshape(
            S_LEN // 256, 128, NCH * 256)
        wo_perm = np.concatenate([w_out[0][0:1024]] + [w_out[0][1024 + (4 * r + h) * 128:1024 + (4 * r + h) * 128 + 128]
                                                       for h in range(4) for r in range(2)], axis=0)
        in_maps.append({
            "xT": xtile,
            "xTo": np.ascontiguousarray(xtile[hh * (HS // 256):(hh + 1) * (HS // 256)]),
            "pTo": np.ascontiguousarray(p[0, b, hh * HS:(hh + 1) * HS].T),
            "wsb": np.ascontiguousarray(wsb),
            "wgla": np.ascontiguousarray(wgla),
            "wlr": np.ascontiguousarray(W[:, LR:LR + 16]),
            "wa2": np.ascontiguousarray(w_a2[0][:, 256 * hh:256 * hh + 256]),
            "ba": np.ascontiguousarray(b_a[0][None, 256 * hh:256 * hh + 256]),
            "cols": cols,
            "wo": np.ascontiguousarray(wo_perm),
            "wgt": w_ple_gate[0],
            "wp": w_ple_proj[0],
            "cst": cst,
        })
    if S_LEN not in _PROG:
        _PROG[S_LEN] = build_program(S_LEN, "AB")
    res = run_bass_kernel_spmd(_PROG[S_LEN], in_maps, core_ids=list(range(8)))
    out = np.empty((B, S_LEN, D), np.float32)
    for core in range(8):
        b, hh = core // 2, core % 2
        out[b, hh * HS:(hh + 1) * HS, :] = res.results[core]["outT"].T
    return out
```
